# Optimizing a Trainium2 kernel written in Bass

```python
import math
import jax, jax.numpy as jnp
from jax import lax
import numpy as np

D_MODEL = 2048
BATCH = 2
SEQ = 4096
DEPTH = 1

ATT_HEADS = 16
ATT_KV_HEADS = 4
HEAD_DIM = 128
REP = ATT_HEADS // ATT_KV_HEADS
IDX_HEADS = 16
IDX_DIM = 64
TOPK_MAX = 256
Q_BLOCK = 128
N_BUCKETS = 32
MAX_DISTANCE = 128
SSM_EXPAND = 2
SSM_INNER = SSM_EXPAND * D_MODEL
SSM_HEAD_DIM = 64
SSM_HEADS = SSM_INNER // SSM_HEAD_DIM
SSM_GROUPS = 8
SSM_STATE = 128
CONV_WIDTH = 4
CHUNK = 128
MLP_HIDDEN = 4 * D_MODEL
EPS = 1e-6

ATT_Q = ATT_HEADS * HEAD_DIM
ATT_KV = ATT_KV_HEADS * HEAD_DIM
IDX_Q = IDX_HEADS * IDX_DIM
SSM_BC = SSM_GROUPS * SSM_STATE
CONV_DIM = SSM_INNER + 2 * SSM_BC
SPLITS = (D_MODEL, D_MODEL, ATT_Q, ATT_KV, ATT_KV, IDX_Q, IDX_DIM, IDX_HEADS, SSM_INNER, CONV_DIM, SSM_HEADS)
IN_DIM = sum(SPLITS)
SPLIT_POINTS = tuple(sum(SPLITS[:i + 1]) for i in range(len(SPLITS) - 1))

kernel_name = "hybrid_dsa_ssd_gated_block"


def rms_norm(x, g, eps=EPS):
    xf = x.astype(jnp.float32)
    y = xf * lax.rsqrt(jnp.mean(xf * xf, axis=-1, keepdims=True) + eps)
    return (y * g.astype(jnp.float32)).astype(x.dtype)


def t5_bucket(dist):
    n = jnp.maximum(dist, 0)
    max_exact = N_BUCKETS // 2
    nf = jnp.maximum(n, 1).astype(jnp.float32)
    large = max_exact + (jnp.log(nf / max_exact) / math.log(MAX_DISTANCE / max_exact)
                         * (N_BUCKETS - max_exact)).astype(jnp.int32)
    large = jnp.minimum(large, N_BUCKETS - 1)
    return jnp.where(n < max_exact, n, large)


def dsa_attention(q, k, v, q_idx, k_idx, w_idx, rel_bias):
    b, s = q.shape[0], q.shape[1]
    topk = min(TOPK_MAX, s // 4)
    nb = s // Q_BLOCK
    key_pos = jnp.arange(s)
    gather = jax.vmap(lambda t, idx: t[idx])

    def to_blocks(t):
        return t.reshape((b, nb, Q_BLOCK) + t.shape[2:]).swapaxes(0, 1)

    def one_block(args):
        qb, qib, wb, tpos = args
        rel = jax.nn.relu(jnp.einsum('bthd,bsd->bths', qib, k_idx))
        score = jnp.einsum('bths,bth->bts', rel, wb).astype(jnp.float32)
        admissible = key_pos[None, None, :] <= tpos[None, :, None]
        score = jnp.where(admissible, score, -jnp.inf)
        _, sel = lax.top_k(score, topk)
        valid = sel <= tpos[None, :, None]
        kg = gather(k, sel)
        vg = gather(v, sel)
        qg = qb.reshape(b, Q_BLOCK, ATT_KV_HEADS, REP, HEAD_DIM)
        logits = jnp.einsum('btgrd,btkgd->btgrk', qg, kg).astype(jnp.float32)
        bias = rel_bias[t5_bucket(tpos[None, :, None] - sel)]
        bias = bias.reshape(b, Q_BLOCK, topk, ATT_KV_HEADS, REP).transpose(0, 1, 3, 4, 2)
        logits = jnp.where(valid[:, :, None, None, :], logits + bias.astype(jnp.float32), -jnp.inf)
        p = jax.nn.softmax(logits, axis=-1).astype(v.dtype)
        o = jnp.einsum('btgrk,btkgd->btgrd', p, vg)
        return o.reshape(b, Q_BLOCK, ATT_Q)

    tpos_blocks = jnp.arange(s).reshape(nb, Q_BLOCK)
    out = lax.map(one_block, (to_blocks(q), to_blocks(q_idx), to_blocks(w_idx), tpos_blocks))
    return out.swapaxes(0, 1).reshape(b, s, ATT_Q)


def causal_depthwise_conv(x, w, bias):
    y = lax.conv_general_dilated(x, w[:, None, :].astype(x.dtype), window_strides=(1,),
                                 padding=[(CONV_WIDTH - 1, 0)],
                                 dimension_numbers=('NWC', 'WIO', 'NWC'),
                                 feature_group_count=x.shape[-1])
    return y + bias.astype(x.dtype)


def ssd_chunked(x, dt, A, Bm, Cm):
    in_dtype = x.dtype
    f32 = jnp.float32
    b, s, h, p = x.shape
    g, n = Bm.shape[2], Bm.shape[3]
    e = h // g
    nc = s // CHUNK
    xd = (x.astype(f32) * dt[..., None]).reshape(b, nc, CHUNK, g, e, p)
    a = (dt * A).reshape(b, nc, CHUNK, g, e)
    Bc = Bm.astype(f32).reshape(b, nc, CHUNK, g, n)
    Cc = Cm.astype(f32).reshape(b, nc, CHUNK, g, n)
    a_cum = jnp.cumsum(a, axis=2)
    causal = jnp.tril(jnp.ones((CHUNK, CHUNK), dtype=bool))
    seg = a_cum[:, :, :, None] - a_cum[:, :, None, :]
    decay = jnp.exp(jnp.where(causal[None, None, :, :, None, None], seg, -jnp.inf))
    cb = jnp.einsum('bclgn,bcsgn->bclsg', Cc, Bc)
    y_diag = jnp.einsum('bclsg,bclsge,bcsgep->bclgep', cb, decay, xd)
    decay_to_end = jnp.exp(a_cum[:, :, -1:] - a_cum)
    chunk_states = jnp.einsum('bclgn,bclge,bclgep->bcgepn', Bc, decay_to_end, xd)
    chunk_decay = jnp.exp(a_cum[:, :, -1])

    def step(state, inp):
        st, dec = inp
        return state * dec[..., None, None] + st, state

    init = jnp.zeros((b, g, e, p, n), f32)
    _, entering = lax.scan(step, init, (chunk_states.swapaxes(0, 1), chunk_decay.swapaxes(0, 1)))
    entering = entering.swapaxes(0, 1)
    y_off = jnp.einsum('bclgn,bcgepn,bclge->bclgep', Cc, entering, jnp.exp(a_cum))
    return (y_diag + y_off).reshape(b, s, h, p).astype(in_dtype)


def setup_inputs(seed: int = 0) -> dict:
    key = jax.random.key(seed)
    ks = jax.random.split(key, 24)
    f32 = jnp.float32
    L = DEPTH

    def nrm(k, shape, scale):
        return jax.random.normal(k, shape, f32) * scale

    dt0 = jnp.exp(jax.random.uniform(ks[5], (L, SSM_HEADS), f32) * (math.log(0.1) - math.log(0.001)) + math.log(0.001))
    return {
        "x": nrm(ks[0], (BATCH, SEQ, D_MODEL), 1.0),
        "norm1_g": 1.0 + nrm(ks[1], (L, D_MODEL), 0.02),
        "w_in": nrm(ks[2], (L, D_MODEL, IN_DIM), D_MODEL ** -0.5),
        "conv_w": nrm(ks[3], (L, CONV_WIDTH, CONV_DIM), CONV_WIDTH ** -0.5),
        "conv_b": nrm(ks[4], (L, CONV_DIM), 0.02),
        "dt_bias": dt0 + jnp.log(-jnp.expm1(-dt0)),
        "a_log": jnp.log(jax.random.uniform(ks[6], (L, SSM_HEADS), f32, 1.0, 16.0)),
        "d_skip": 1.0 + nrm(ks[7], (L, SSM_HEADS), 0.02),
        "ssm_norm_g": 1.0 + nrm(ks[8], (L, SSM_INNER), 0.02),
        "q_norm_g": 1.0 + nrm(ks[9], (L, HEAD_DIM), 0.02),
        "k_norm_g": 1.0 + nrm(ks[10], (L, HEAD_DIM), 0.02),
        "rel_bias": nrm(ks[11], (N_BUCKETS, ATT_HEADS), 0.5),
        "w_att_branch": nrm(ks[12], (L, ATT_Q, D_MODEL), ATT_Q ** -0.5),
        "w_ssm_branch": nrm(ks[13], (L, SSM_INNER, D_MODEL), SSM_INNER ** -0.5),
        "w_out": nrm(ks[14], (L, D_MODEL, D_MODEL), D_MODEL ** -0.5),
        "norm2_g": 1.0 + nrm(ks[15], (L, D_MODEL), 0.02),
        "w_up": nrm(ks[16], (L, D_MODEL, MLP_HIDDEN), D_MODEL ** -0.5),
        "w_down": nrm(ks[17], (L, MLP_HIDDEN, D_MODEL), MLP_HIDDEN ** -0.5),
    }


def reference(x, norm1_g, w_in, conv_w, conv_b, dt_bias, a_log, d_skip, ssm_norm_g, q_norm_g,
              k_norm_g, rel_bias, w_att_branch, w_ssm_branch, w_out, norm2_g, w_up, w_down):
    b, s, _ = x.shape
    f32 = jnp.float32
    for l in range(DEPTH):
        h = rms_norm(x, norm1_g[l])
        proj = h @ w_in[l]
        (gate_att, gate_ssm, q, k, v, q_idx, k_idx, w_idx, z, xbc, dt) = jnp.split(proj, SPLIT_POINTS, axis=-1)

        q = rms_norm(q.reshape(b, s, ATT_HEADS, HEAD_DIM), q_norm_g[l]) * (HEAD_DIM ** -0.5)
        k = rms_norm(k.reshape(b, s, ATT_KV_HEADS, HEAD_DIM), k_norm_g[l])
        v = v.reshape(b, s, ATT_KV_HEADS, HEAD_DIM)
        q_idx = q_idx.reshape(b, s, IDX_HEADS, IDX_DIM) * (IDX_DIM ** -0.5)
        w_idx = w_idx * (IDX_HEADS ** -0.5)
        att = dsa_attention(q, k, v, q_idx, k_idx, w_idx, rel_bias)

        xbc = jax.nn.silu(causal_depthwise_conv(xbc, conv_w[l], conv_b[l]))
        xs, Bm, Cm = jnp.split(xbc, (SSM_INNER, SSM_INNER + SSM_BC), axis=-1)
        xs = xs.reshape(b, s, SSM_HEADS, SSM_HEAD_DIM)
        dt_act = jax.nn.softplus(dt.astype(f32) + dt_bias[l].astype(f32))
        A = -jnp.exp(a_log[l].astype(f32))
        y = ssd_chunked(xs, dt_act, A, Bm.reshape(b, s, SSM_GROUPS, SSM_STATE),
                        Cm.reshape(b, s, SSM_GROUPS, SSM_STATE))
        y = y + d_skip[l][:, None].astype(y.dtype) * xs
        y = y.reshape(b, s, SSM_INNER) * jax.nn.silu(z)
        y = rms_norm(y.reshape(b, s, SSM_GROUPS, SSM_INNER // SSM_GROUPS),
                     ssm_norm_g[l].reshape(SSM_GROUPS, SSM_INNER // SSM_GROUPS)).reshape(b, s, SSM_INNER)

        merged = (jax.nn.sigmoid(gate_att) * (att @ w_att_branch[l])
                  + jax.nn.sigmoid(gate_ssm) * (y @ w_ssm_branch[l]))
        x = x + merged @ w_out[l]

        h2 = rms_norm(x, norm2_g[l])
        x = x + jnp.square(jax.nn.relu(h2 @ w_up[l])) @ w_down[l]
    return x
```

```python
from contextlib import ExitStack
import numpy as np
import concourse.bass as bass
import concourse.mybir as mybir
from concourse.bass_utils import run_bass_kernel_spmd

F32 = mybir.dt.float32
BF16 = mybir.dt.bfloat16
AF = mybir.ActivationFunctionType
ALU = mybir.AluOpType
AX = mybir.AxisListType

D = 2048
SEQ = 4096
NB = 32
OWN = 8
HALO = 3
BW = 128 + HALO
IN_DIM = 18576
O_GA, O_GS, O_Q, O_K, O_V, O_QI, O_KI, O_WI, O_Z, O_X, O_B, O_C, O_DT = (
    0, 2048, 4096, 6144, 6656, 7168, 8192, 8256, 8272, 12368, 16464, 17488, 18512)
EPS = 1e-6
NEG = -30000.0
DEBUG = False
STAGE = 99

SUB = 99


class Prod:
    def __init__(self, nc, name):
        self.sem = nc.alloc_semaphore(name)
        self.cnt = 0
        self.name = name


class Eng:
    def __init__(self, nc, e, name, selfsync=True):
        self.e = e
        self.p = Prod(nc, "s_" + name)
        self.seen = {}
        self.selfsync = selfsync
        self.name = name

    def wait(self, tok):
        if tok is None:
            return
        prod, val = tok
        if prod is self.p and not self.selfsync:
            return
        if self.seen.get(prod, 0) >= val:
            return
        self.seen[prod] = val
        self.e.wait_ge(prod.sem, val)


class Buf:
    def __init__(self, name="b"):
        self.name = name
        self.w = None
        self.rs = {}


def build_program():
    nc = bass.Bass("TRN2", target_bir_lowering=False)
    PE = Eng(nc, nc.tensor, "pe", selfsync=False)
    ACT = Eng(nc, nc.scalar, "act")
    DVE = Eng(nc, nc.vector, "dve")
    POOL = Eng(nc, nc.gpsimd, "pool")
    SP = Eng(nc, nc.sync, "sp")

    def deps(E, reads, writes):
        for b in reads:
            E.wait(b.w)
        for b in writes:
            E.wait(b.w)
            for t in list(b.rs.items()):
                E.wait(t)

    def commit(tok, reads, writes):
        prod, val = tok
        for b in reads:
            b.rs[prod] = val
        for b in writes:
            b.w = tok
            b.rs = {}

    B_psum_lock = Buf("psum_lock")

    def op(E, fn, reads=(), writes=()):
        extra = [b for b in reads if getattr(b, "psum", False) and b not in writes]
        if extra:
            writes = list(writes) + extra + [B_psum_lock]
        deps(E, reads, writes)
        inst = fn(E.e)
        E.p.cnt += 1
        inst.then_inc(E.p.sem, 1)
        commit((E.p, E.p.cnt), reads, writes)
        return inst

    def pe_group(fns, reads=(), writes=()):
        deps(PE, reads, writes)
        inst = None
        for fn in fns:
            inst = fn(nc.tensor)
        PE.p.cnt += 1
        inst.then_inc(PE.p.sem, 1)
        commit((PE.p, PE.p.cnt), reads, writes)

    dma_slots = {}

    def dma(E, fns, reads=(), writes=(), nslots=6):
        key = E.name
        if key not in dma_slots:
            dma_slots[key] = ([Prod(nc, f"d_{key}{i}") for i in range(nslots)], [0])
        slots, ctr = dma_slots[key]
        slot = slots[ctr[0] % len(slots)]
        ctr[0] += 1
        deps(E, reads, writes)
        if slot.cnt > 0:
            E.wait((slot, slot.cnt))
        for fn in fns:
            fn(E.e).then_inc(slot.sem, 16)
            slot.cnt += 16
        commit((slot, slot.cnt), reads, writes)

    def din(name, shape, dt=F32):
        return nc.dram_tensor(name, list(shape), dt, kind="ExternalInput").ap()

    x_all = din("x_all", [SEQ, D])
    x_own = din("x_own", [OWN, 128, D])
    x_halo = din("x_halo", [OWN * HALO, D])
    w_in = din("w_in", [D, IN_DIM])
    w_ab = din("w_ab", [2048, D])
    w_sb = din("w_sb", [4096, D])
    w_out = din("w_out", [D, D])
    w_up = din("w_up", [D, 8192])
    w_down = din("w_down", [8192, D])
    g1 = din("norm1_g", [1, D])
    g2 = din("norm2_g", [1, D])
    gs = din("ssm_norm_g", [1, 4096])
    gq = din("q_norm_g", [128, 1])
    gk = din("k_norm_g", [128, 1])
    cw = din("conv_w", [128, 48, 4])
    cb = din("conv_b", [128, 48])
    dtb = din("dt_bias", [1, 64])
    alog = din("a_log", [1, 64])
    dsk = din("d_skip", [1, 64])
    relb = din("rel_bias", [32, 16])
    c_ident = din("c_ident", [128, 128])
    c_tri = din("c_tri", [128, 128])
    c_flags = din("c_flags", [128, 4])
    c_amask = din("c_amask", [128, 512])
    c_ohb = din("c_ohb", [32, 5 * 256])

    out_d = nc.dram_tensor("out", [OWN, 128, D], F32, kind="ExternalOutput").ap()
    skind = "ExternalOutput" if DEBUG else "Internal"
    kT_d = nc.dram_tensor("kT_d", [128, 4, SEQ], BF16, kind=skind).ap()
    v_d = nc.dram_tensor("v_d", [128, NB, 512], BF16, kind=skind).ap()
    kiT_d = nc.dram_tensor("kiT_d", [128, SEQ], BF16, kind=skind).ap()
    ssave_d = nc.dram_tensor("ssave_d", [OWN, 128, 4096], BF16, kind=skind).ap()

    attT_d = nc.dram_tensor("attT_d", [128, 16, OWN * 128], BF16, kind=skind).ap()
    ebz_d = nc.dram_tensor("ebz_d", [16, 5, 128, 256], F32, kind="Internal").ap()
    mT_d = nc.dram_tensor("mT_d", [128, 16, OWN * 128], BF16, kind=skind).ap()
    dtraw_d = nc.dram_tensor("dtraw_d", [NB, 128, 64], F32, kind="Internal").ap()
    dbg = {}
    if DEBUG:
        dbg["qT"] = nc.dram_tensor("dbg_qT", [128, 16, OWN * 128], BF16, kind="ExternalOutput").ap()
        dbg["qiT"] = nc.dram_tensor("dbg_qiT", [128, 8, OWN * 128], BF16, kind="ExternalOutput").ap()
        dbg["wtok"] = nc.dram_tensor("dbg_wtok", [128, OWN, 16], F32, kind="ExternalOutput").ap()
        dbg["score"] = nc.dram_tensor("dbg_score", [OWN, 128, SEQ], F32, kind="ExternalOutput").ap()
        dbg["thr"] = nc.dram_tensor("dbg_thr", [128, OWN, 4], F32, kind="ExternalOutput").ap()
        dbg["ynT"] = nc.dram_tensor("dbg_ynT", [128, 32, OWN * 128], BF16, kind="ExternalOutput").ap()
    scopes = [ExitStack()]

    sbn = [0]

    def sb(name, shape, dt=F32):
        sbn[0] += 1
        return scopes[-1].enter_context(nc.sbuf_tensor(f"{name}_{sbn[0]}", list(shape), dt))

    def push_scope():
        scopes.append(ExitStack())

    def pop_scope():
        barrier()
        scopes.pop().close()

    def barrier():
        prods = [E.p for E in (PE, ACT, DVE, POOL, SP)]
        for key, (slots, ctr) in dma_slots.items():
            prods += slots
        for E in (PE, ACT, DVE, POOL, SP):
            for p in prods:
                if p.cnt > 0 and p is not E.p:
                    E.wait((p, p.cnt))

    ident_f = sb("ident_f", [128, 128]); B_ident_f = Buf()
    ident_b = sb("ident_b", [128, 128], BF16); B_ident_b = Buf()
    ones_b = sb("ones_b", [128, 128], BF16); B_ones_b = Buf()
    tri_f = sb("tri_f", [128, 128]); B_tri = Buf()
    gq_t = sb("gq_t", [128, 1]); gk_t = sb("gk_t", [128, 1]); B_gqk = Buf()
    flags_t = sb("flags_t", [128, 4]); B_flags = Buf()
    cw_t = sb("cw_t", [128, 48, 4]); cb_t = sb("cb_t", [128, 48]); B_cw = Buf()
    dtb_bc = sb("dtb_bc", [128, 64]); A_bc = sb("A_bc", [128, 64]); dsk_bc = sb("dsk_bc", [128, 64]); B_ssmc = Buf()

    dma(SP, [lambda e: e.dma_start(out=ident_f[:], in_=c_ident)], writes=[B_ident_f])
    dma(SP, [lambda e: e.dma_start(out=tri_f[:], in_=c_tri)], writes=[B_tri])
    dma(SP, [lambda e: e.dma_start(out=gq_t[:], in_=gq), lambda e: e.dma_start(out=gk_t[:], in_=gk)], writes=[B_gqk])
    dma(SP, [lambda e: e.dma_start(out=flags_t[:], in_=c_flags)], writes=[B_flags])
    dma(SP, [lambda e: e.dma_start(out=cw_t[:], in_=cw), lambda e: e.dma_start(out=cb_t[:], in_=cb)], writes=[B_cw])
    dma(SP, [lambda e: e.dma_start(out=dtb_bc[:], in_=dtb.partition_broadcast(128)),
             lambda e: e.dma_start(out=A_bc[:], in_=alog.partition_broadcast(128)),
             lambda e: e.dma_start(out=dsk_bc[:], in_=dsk.partition_broadcast(128))], writes=[B_ssmc])
    op(DVE, lambda e: e.tensor_copy(out=ident_b[:], in_=ident_f[:]), reads=[B_ident_f], writes=[B_ident_b])
    op(DVE, lambda e: e.memset(ones_b[:], 1.0), writes=[B_ones_b])
    tri_b = sb("tri_b", [128, 128], BF16)
    op(DVE, lambda e: e.tensor_copy(out=tri_b[:], in_=tri_f[:]), reads=[B_tri], writes=[B_tri])
    op(ACT, lambda e: e.activation(out=A_bc[:], in_=A_bc[:], func=AF.Exp), reads=[B_ssmc], writes=[B_ssmc])
    op(DVE, lambda e: e.tensor_scalar(out=A_bc[:], in0=A_bc[:], scalar1=-1.0, scalar2=None, op0=ALU.mult), reads=[B_ssmc], writes=[B_ssmc])

    PS = [nc.alloc_psum_tensor(f"ps{i}", [128, 512], F32) for i in range(6)]
    BPS = [Buf() for i in range(6)]
    for b_ in BPS:
        b_.psum = True
    PT = [nc.alloc_psum_tensor(f"pt{i}", [128, 1024], BF16) for i in range(2)]
    BPT = [Buf() for i in range(2)]
    for b_ in BPT:
        b_.psum = True
    psc = [0]

    psn = [6]

    def next_ps():
        i = psc[0] % psn[0]
        psc[0] += 1
        return PS[i], BPS[i]

    ptc = [0]

    def next_pt():
        i = ptc[0] % 2
        ptc[0] += 1
        return PT[i], BPT[i]

    WS = []
    BWS = []
    wsc = [0]

    def alloc_ws(n):
        WS.clear(); BWS.clear()
        for i in range(n):
            WS.append(sb(f"ws{i}_{wsc[0]}", [128, 16, 512], BF16)); BWS.append(Buf())

    def load_w(src, pieces, kchunks=16, row0=0):
        i = wsc[0] % len(WS)
        wsc[0] += 1
        t, b = WS[i], BWS[i]
        fns = []
        off = 0
        for (c0, ncol) in pieces:
            for k0 in range(0, kchunks, 4):
                k1 = min(kchunks, k0 + 4)
                sv = src[row0 + k0 * 128:row0 + k1 * 128, c0:c0 + ncol].rearrange("(kc p) c -> p kc c", p=128)
                fns.append(lambda e, sv=sv, off=off, ncol=ncol, k0=k0, k1=k1: e.dma_start(out=t[:, k0:k1, off:off + ncol], in_=sv))
            off += ncol
        dma(POOL, fns, writes=[b], nslots=4)
        return t, b

    class Rot:
        def __init__(self, name, shape, dt, n):
            self.t = [sb(f"{name}{i}", shape, dt) for i in range(n)]
            self.b = [Buf() for i in range(n)]
            self.c = 0

        def next(self):
            i = self.c % len(self.t)
            self.c += 1
            return self.t[i], self.b[i]

    NR = {}

    def alloc_norm(gsrc):
        NR["xb"] = Rot("xb", [128, D], F32, 2)
        NR["hb"] = Rot("hb", [128, D], BF16, 2)
        NR["st"] = Rot("st", [128, 4], F32, 2)
        NR["g"] = sb("gbc", [128, D]); NR["Bg"] = Buf()
        dma(SP, [lambda e: e.dma_start(out=NR["g"][:], in_=gsrc.partition_broadcast(128))], writes=[NR["Bg"]])

    def norm_rows(src, rows, src_sb=None):
        gbc, Bg = NR["g"], NR["Bg"]
        hb, Bh = NR["hb"].next()
        st, Bs = NR["st"].next()
        if src_sb is None:
            xt, Bx = NR["xb"].next()
            dma(SP, [lambda e: e.dma_start(out=xt[0:rows, :], in_=src)], writes=[Bx])
        else:
            xt, Bx = src_sb
        junk, Bjunk = hb, Bh
        op(DVE, lambda e: e.memset(st[0:rows, 0:1], 0.0), writes=[Bs])
        op(ACT, lambda e: e.activation(out=junk[0:rows, :], in_=xt[0:rows, :], func=AF.Square, accum_out=st[0:rows, 0:1]),
           reads=[Bx, Bs], writes=[Bjunk, Bs])
        op(DVE, lambda e: e.tensor_scalar(out=st[0:rows, 1:2], in0=st[0:rows, 0:1], scalar1=1.0 / D, scalar2=EPS,
                                          op0=ALU.mult, op1=ALU.add), reads=[Bs], writes=[Bs])
        op(ACT, lambda e: e.activation(out=st[0:rows, 2:3], in_=st[0:rows, 1:2], func=AF.Sqrt), reads=[Bs], writes=[Bs])
        op(DVE, lambda e: e.reciprocal(out=st[0:rows, 3:4], in_=st[0:rows, 2:3]), reads=[Bs], writes=[Bs])
        op(DVE, lambda e: e.scalar_tensor_tensor(out=hb[0:rows, :], in0=xt[0:rows, :], scalar=st[0:rows, 3:4],
                                                 in1=gbc[0:rows, :], op0=ALU.mult, op1=ALU.mult),
           reads=[Bx, Bs, Bg], writes=[Bh])
        return hb, Bh, xt, Bx

    def transpose_rows(hb, Bh, rows, dst_fn, Bdst):
        for half in range(2):
            pt, Bp = next_pt()
            fns = []
            for q8 in range(8):
                kc = half * 8 + q8
                fns.append(lambda e, kc=kc, q8=q8: e.transpose(out=pt[:, q8 * 128:q8 * 128 + rows],
                                                               in_=hb[0:rows, kc * 128:(kc + 1) * 128],
                                                               identity=ident_b[0:rows, 0:rows]))
            pe_group(fns, reads=[Bh, B_ident_b], writes=[Bp])
            src = pt[:].rearrange("p (a b) -> p a b", a=8)[:, :, 0:rows]
            op(ACT, lambda e: e.copy(out=dst_fn(half * 8, 8), in_=src), reads=[Bp], writes=[Bdst])

    push_scope()
    hT_all = sb("hT_all", [128, 16, SEQ], BF16); B_hT = Buf()
    alloc_ws(2)
    push_scope()
    alloc_norm(g1)
    for blk in range(NB):
        hb, Bh, _, _ = norm_rows(x_all[blk * 128:(blk + 1) * 128, :], 128)
        transpose_rows(hb, Bh, 128, lambda kc0, n, blk=blk: hT_all[:, kc0:kc0 + n, blk * 128:(blk + 1) * 128], B_hT)
    pop_scope()
    push_scope()

    stg_r = Rot("stg", [128, 512], BF16, 2)
    sq_r = Rot("sq", [128, 512], BF16, 1)
    rs_r = Rot("rs", [128, 512], F32, 1)

    def headnorm_T(ps, Bp, gcol, Bgc, outt, Bout, post_scale):
        sq, Bsq = sq_r.next()
        op(ACT, lambda e: e.activation(out=sq[:], in_=ps[:], func=AF.Square), reads=[Bp], writes=[Bsq])
        ps2, Bp2 = next_ps()
        pe_group([lambda e: e.matmul(ps2[:], ones_b[:], sq[:], start=True, stop=True)], reads=[Bsq, B_ones_b], writes=[Bp2])
        rs, Brs = rs_r.next()
        op(DVE, lambda e: e.tensor_scalar(out=rs[:], in0=ps2[:], scalar1=1.0 / 128, scalar2=EPS, op0=ALU.mult, op1=ALU.add),
           reads=[Bp2], writes=[Brs])
        op(ACT, lambda e: e.activation(out=rs[:], in_=rs[:], func=AF.Sqrt), reads=[Brs], writes=[Brs])
        op(DVE, lambda e: e.reciprocal(out=rs[:], in_=rs[:]), reads=[Brs], writes=[Brs])
        if post_scale != 1.0:
            op(DVE, lambda e: e.tensor_scalar(out=rs[:], in0=rs[:], scalar1=post_scale, scalar2=None, op0=ALU.mult),
               reads=[Brs], writes=[Brs])
        op(DVE, lambda e: e.scalar_tensor_tensor(out=outt, in0=ps[:], scalar=gcol, in1=rs[:], op0=ALU.mult, op1=ALU.mult),
           reads=[Bp, Brs, Bgc], writes=[Bout])

    wk, Bwk = load_w(w_in, [(O_K, 512)])
    for T in range(8):
        for g in range(4):
            ps, Bp = next_ps()
            pe_group([lambda e, kc=kc: e.matmul(ps[:], wk[:, kc, g * 128:(g + 1) * 128], hT_all[:, kc, T * 512:(T + 1) * 512],
                                                start=(kc == 0), stop=(kc == 15)) for kc in range(16)],
                     reads=[Bwk, B_hT], writes=[Bp])
            stg, Bstg = stg_r.next()
            headnorm_T(ps, Bp, gk_t[:, 0:1], B_gqk, stg[:], Bstg, 1.0)
            dma(SP, [lambda e: e.dma_start(out=kT_d[:, g, T * 512:(T + 1) * 512], in_=stg[:])], reads=[Bstg])
    wv, Bwv = load_w(w_in, [(O_V, 512)])
    for blk in range(NB):
        ps, Bp = next_ps()
        pe_group([lambda e, kc=kc: e.matmul(ps[:], hT_all[:, kc, blk * 128:(blk + 1) * 128], wv[:, kc, 0:512],
                                            start=(kc == 0), stop=(kc == 15)) for kc in range(16)],
                 reads=[Bwv, B_hT], writes=[Bp])
        stg, Bstg = stg_r.next()
        op(ACT, lambda e: e.copy(out=stg[:], in_=ps[:]), reads=[Bp], writes=[Bstg])
        dma(SP, [lambda e: e.dma_start(out=v_d[:, blk, :], in_=stg[:])], reads=[Bstg])
    wki, Bwki = load_w(w_in, [(O_KI, 64), (O_KI, 64)])
    for T in range(8):
        ps, Bp = next_ps()
        pe_group([lambda e, kc=kc: e.matmul(ps[:], wki[:, kc, 0:128], hT_all[:, kc, T * 512:(T + 1) * 512],
                                            start=(kc == 0), stop=(kc == 15)) for kc in range(16)],
                 reads=[Bwki, B_hT], writes=[Bp])
        stg, Bstg = stg_r.next()
        op(ACT, lambda e: e.copy(out=stg[:], in_=ps[:]), reads=[Bp], writes=[Bstg])
        dma(SP, [lambda e: e.dma_start(out=kiT_d[:, T * 512:(T + 1) * 512], in_=stg[:])], reads=[Bstg])

    wdt, Bwdt = load_w(w_in, [(O_DT, 64)])
    dts_r = Rot("dts", [128, 64], F32, 2)
    B_dtraw = Buf()
    for blk in range(NB):
        ps, Bp = next_ps()
        pe_group([lambda e, kc=kc: e.matmul(ps[:, 0:64], hT_all[:, kc, blk * 128:(blk + 1) * 128], wdt[:, kc, 0:64],
                                            start=(kc == 0), stop=(kc == 15)) for kc in range(16)],
                 reads=[Bwdt, B_hT], writes=[Bp])
        dts, Bdts = dts_r.next()
        op(DVE, lambda e: e.tensor_copy(out=dts[:], in_=ps[:, 0:64]), reads=[Bp], writes=[Bdts])
        dma(SP, [lambda e: e.dma_start(out=dtraw_d[blk], in_=dts[:])], reads=[Bdts], writes=[B_dtraw])
    barrier()
    dtr_r = Rot("dtr", [128, 4, 8], F32, 2)

    pre_r = Rot("pre", [128, 5, 2, 516], BF16, 2)
    xc_r = Rot("xc", [128, 5, 512], BF16, 1)
    diag = sb("diag", [128, 5, 4, 128], BF16); B_diag = Buf()
    Sst = sb("Sst", [128, 512], F32); B_S = Buf()
    Ssel = sb("Ssel", [128, 512], F32); B_Ssel = Buf()
    dtw_r = Rot("dtw", [128, 8, 32], F32, 1)
    xw_r = Rot("xw", [128, 512], BF16, 2)
    bt_r = Rot("bt", [128, 128], BF16, 2)

    def conv_chunk(pre, Bpre, i, c, width, outt, Bout):
        ps, Bp = next_ps()
        def tap(kk):
            return pre[:, i, 0, 1 + kk:1 + kk + width] if kk % 2 == 1 else pre[:, i, 1, kk:kk + width]
        pe_group([lambda e, kk=kk: e.matmul(ps[:, 0:width], diag[:, i, kk, :], tap(kk),
                                            start=(kk == 0), stop=(kk == 3)) for kk in range(4)],
                 reads=[Bpre, B_diag], writes=[Bp])
        op(ACT, lambda e: e.activation(out=outt, in_=ps[:, 0:width], func=AF.Silu, bias=cb_t[:, c:c + 1]),
           reads=[Bp, B_cw], writes=[Bout])

    def build_diag(chunks):
        for i, c in enumerate(chunks):
            for kk in range(4):
                op(DVE, lambda e, i=i, c=c, kk=kk: e.tensor_scalar(out=diag[:, i, kk, :], in0=ident_f[:], scalar1=cw_t[:, c, kk:kk + 1],
                                                                 scalar2=None, op0=ALU.mult),
                   reads=[B_ident_f, B_cw], writes=[B_diag])

    ahl_r = Rot("ahl", [128, 2, 64], BF16, 2)

    def dt_math(src3, Bpd, g, nblk, dtw, Bdtw):
        n = nblk * 8
        v3 = lambda r: dtw[:, r, 0:n].rearrange("p (b h) -> p b h", h=8)
        bias3 = dtb_bc[:, g * 8:(g + 1) * 8].unsqueeze(1).broadcast_to([128, nblk, 8])
        A3 = A_bc[:, g * 8:(g + 1) * 8].unsqueeze(1).broadcast_to([128, nblk, 8])
        op(DVE, lambda e: e.tensor_tensor(out=v3(0), in0=src3, in1=bias3, op=ALU.add),
           reads=[Bpd, B_ssmc], writes=[Bdtw])
        op(ACT, lambda e: e.activation(out=dtw[:, 0, 0:n], in_=dtw[:, 0, 0:n], func=AF.Exp), reads=[Bdtw], writes=[Bdtw])
        op(ACT, lambda e: e.activation(out=dtw[:, 0, 0:n], in_=dtw[:, 0, 0:n], func=AF.Ln, bias=1.0), reads=[Bdtw], writes=[Bdtw])
        op(DVE, lambda e: e.tensor_tensor(out=v3(1), in0=v3(0), in1=A3, op=ALU.mult), reads=[Bdtw, B_ssmc], writes=[Bdtw])
        ahl, Bahl = ahl_r.next()
        op(DVE, lambda e: e.tensor_copy(out=ahl[:, 0, 0:n], in_=dtw[:, 1, 0:n]), reads=[Bdtw], writes=[Bahl])
        op(DVE, lambda e: e.tensor_tensor(out=ahl[:, 1, 0:n], in0=dtw[:, 1, 0:n], in1=ahl[:, 0, 0:n], op=ALU.subtract),
           reads=[Bdtw, Bahl], writes=[Bahl])
        pc, Bpc = next_ps()
        pe_group([lambda e: e.matmul(pc[:, 0:n], tri_b[:], ahl[:, 0, 0:n], start=True, stop=False),
                  lambda e: e.matmul(pc[:, 0:n], tri_b[:], ahl[:, 1, 0:n], start=False, stop=True),
                  lambda e: e.matmul(pc[:, 64:64 + n], ones_b[:], ahl[:, 0, 0:n], start=True, stop=False),
                  lambda e: e.matmul(pc[:, 64:64 + n], ones_b[:], ahl[:, 1, 0:n], start=False, stop=True)],
                 reads=[Bahl, B_tri, B_ones_b], writes=[Bpc])
        op(ACT, lambda e: e.copy(out=dtw[:, 2, 0:n], in_=pc[:, 0:n]), reads=[Bpc], writes=[Bdtw])
        op(DVE, lambda e: e.tensor_tensor(out=dtw[:, 3, 0:n], in0=pc[:, 64:64 + n], in1=dtw[:, 2, 0:n], op=ALU.subtract),
           reads=[Bpc, Bdtw], writes=[Bdtw])
        op(ACT, lambda e: e.activation(out=dtw[:, 3, 0:n], in_=dtw[:, 3, 0:n], func=AF.Exp), reads=[Bdtw], writes=[Bdtw])
        op(ACT, lambda e: e.activation(out=dtw[:, 4, 0:n], in_=pc[:, 64:64 + n], func=AF.Exp), reads=[Bpc], writes=[Bdtw])
        op(DVE, lambda e: e.tensor_tensor(out=dtw[:, 5, 0:n], in0=dtw[:, 0, 0:n], in1=dtw[:, 3, 0:n], op=ALU.mult),
           reads=[Bdtw], writes=[Bdtw])
        op(ACT, lambda e: e.activation(out=dtw[:, 6, 0:n], in_=dtw[:, 2, 0:n], func=AF.Exp), reads=[Bdtw], writes=[Bdtw])

    if STAGE >= 2:
        for g in range(8):
            wx, Bwx = load_w(w_in, [(O_X + g * 512, 512)])
            wb, Bwb = load_w(w_in, [(O_B + g * 128, 128)])
            chunks = [4 * g + i for i in range(4)] + [32 + g]
            build_diag(chunks)
            op(DVE, lambda e: e.memset(Sst[:], 0.0), writes=[B_S])
            pre_prev_box = [None]

            def stageA(T):
                    pre, Bpre = pre_r.next()
                    if T == 0:
                        op(DVE, lambda e: e.memset(pre[:, :, :, 0:4], 0.0), writes=[Bpre])
                    else:
                        pp, Bpp = pre_prev_box[0]
                        op(DVE, lambda e: e.tensor_copy(out=pre[:, :, 0, 0:4], in_=pp[:, :, 0, 512:516]), reads=[Bpp], writes=[Bpre])
                        op(DVE, lambda e: e.tensor_copy(out=pre[:, :, 1, 0:4], in_=pp[:, :, 1, 512:516]), reads=[Bpp], writes=[Bpre])
                    for i in range(5):
                        ps, Bp = next_ps()
                        wt, Bwt = (wx, Bwx) if i < 4 else (wb, Bwb)
                        c0 = i * 128 if i < 4 else 0
                        pe_group([lambda e, kc=kc: e.matmul(ps[:], wt[:, kc, c0:c0 + 128], hT_all[:, kc, T * 512:(T + 1) * 512],
                                                            start=(kc == 0), stop=(kc == 15)) for kc in range(16)],
                                 reads=[Bwt, B_hT], writes=[Bp])
                        op(ACT, lambda e: e.copy(out=pre[:, i, 0, 4:516], in_=ps[:]), reads=[Bp], writes=[Bpre])
                        op(DVE, lambda e: e.tensor_copy(out=pre[:, i, 1, 3:515], in_=ps[:]), reads=[Bp], writes=[Bpre])
                    pre_prev_box[0] = (pre, Bpre)
                    return pre, Bpre

            def tr_step(T, r, xc, Bxc, dtw, Bdtw):
                pt, Bpt = next_pt()
                pe_group([lambda e, i=i: e.transpose(out=pt[:, i * 128:(i + 1) * 128], in_=xc[:, i, r * 128:(r + 1) * 128],
                                                     identity=ident_b[:]) for i in range(5)],
                         reads=[Bxc, B_ident_b], writes=[Bpt])
                xw, Bxw = xw_r.next()
                bt, Bbt = bt_r.next()
                sc3 = dtw[:, 5, r * 8:(r + 1) * 8].unsqueeze(2).broadcast_to([128, 8, 64])
                op(DVE, lambda e: e.tensor_tensor(out=xw[:].rearrange("p (h d) -> p h d", h=8),
                                                  in0=pt[:, 0:512].rearrange("p (h d) -> p h d", h=8), in1=sc3, op=ALU.mult),
                   reads=[Bpt, Bdtw], writes=[Bxw])
                op(ACT, lambda e: e.copy(out=bt[:], in_=pt[:, 512:640]), reads=[Bpt], writes=[Bbt])
                return xw, Bxw, bt, Bbt

            def st_step(T, r, xw, Bxw, bt, Bbt, dtw, Bdtw):
                if r == 0:
                    op(DVE, lambda e: e.tensor_scalar(out=Ssel[:], in0=Sst[:], scalar1=flags_t[:, 0:1], scalar2=None, op0=ALU.mult),
                       reads=[B_S, B_flags], writes=[B_Ssel])
                else:
                    op(DVE, lambda e: e.scalar_tensor_tensor(out=Ssel[:], in0=Sst[:], scalar=flags_t[:, r:r + 1], in1=Ssel[:],
                                                             op0=ALU.mult, op1=ALU.add), reads=[B_S, B_flags, B_Ssel], writes=[B_Ssel])
                ps, Bp = next_ps()
                pe_group([lambda e: e.matmul(ps[:], bt[:], xw[:], start=True, stop=True)], reads=[Bbt, Bxw], writes=[Bp])
                cd3 = dtw[:, 4, r * 8:(r + 1) * 8].unsqueeze(2).broadcast_to([128, 8, 64])
                op(DVE, lambda e: e.tensor_tensor(out=Sst[:].rearrange("p (h d) -> p h d", h=8),
                                                  in0=Sst[:].rearrange("p (h d) -> p h d", h=8), in1=cd3, op=ALU.mult),
                   reads=[B_S, Bdtw], writes=[B_S])
                op(DVE, lambda e: e.tensor_tensor(out=Sst[:], in0=Sst[:], in1=ps[:], op=ALU.add), reads=[B_S, Bp], writes=[B_S])

            def stageB1(T, pre, Bpre):
                xc, Bxc = xc_r.next()
                for i in range(5):
                    conv_chunk(pre, Bpre, i, chunks[i], 512, xc[:, i, :], Bxc)
                dtr, Bdtr = dtr_r.next()
                with nc.allow_non_contiguous_dma(reason="small dt slices"):
                    dma(SP, [lambda e: e.dma_start(out=dtr[:], in_=dtraw_d[T * 4:(T + 1) * 4, :, g * 8:(g + 1) * 8].rearrange("b p h -> p b h"))],
                        reads=[B_dtraw], writes=[Bdtr])
                dtw, Bdtw = dtw_r.next()
                dt_math(dtr[:], Bdtr, g, 4, dtw, Bdtw)
                t0 = tr_step(T, 0, xc, Bxc, dtw, Bdtw)
                t1 = tr_step(T, 1, xc, Bxc, dtw, Bdtw)
                return xc, Bxc, dtw, Bdtw, t0, t1

            def stageB2(T, xc, Bxc, dtw, Bdtw, t0, t1):
                st_step(T, 0, *t0, dtw, Bdtw)
                t2 = tr_step(T, 2, xc, Bxc, dtw, Bdtw)
                st_step(T, 1, *t1, dtw, Bdtw)
                t3 = tr_step(T, 3, xc, Bxc, dtw, Bdtw)
                st_step(T, 2, *t2, dtw, Bdtw)
                st_step(T, 3, *t3, dtw, Bdtw)
                stg, Bstg = stg_r.next()
                op(DVE, lambda e: e.tensor_copy(out=stg[:], in_=Ssel[:]), reads=[B_Ssel], writes=[Bstg])
                dma(SP, [lambda e: e.dma_start(out=ssave_d[T, :, g * 512:(g + 1) * 512], in_=stg[:])], reads=[Bstg])

            pendA = stageA(0)
            for T in range(8):
                ctxB = stageB1(T, *pendA)
                pendA = stageA(T + 1) if T + 1 < 8 else None
                stageB2(T, *ctxB)

    pop_scope()
    pop_scope()
    def build_hT_own():
        hT = sb("hT_own", [128, 16, OWN, 132], BF16); B = Buf()
        op(DVE, lambda e: e.memset(hT[:, :, :, 0:1], 0.0), writes=[B])
        push_scope()
        alloc_norm(g1)
        for m in range(OWN):
            hb, Bh, _, _ = norm_rows(x_own[m], 128)
            transpose_rows(hb, Bh, 128, lambda kc0, n, m=m: hT[:, kc0:kc0 + n, m, 4:132], B)
        hb, Bh, _, _ = norm_rows(x_halo, OWN * HALO)
        for half in range(2):
            pt, Bp = next_pt()
            pe_group([lambda e, q8=q8: e.transpose(out=pt[:, q8 * 128:q8 * 128 + 24], in_=hb[0:24, (half * 8 + q8) * 128:(half * 8 + q8 + 1) * 128],
                                                   identity=ident_b[0:24, 0:24]) for q8 in range(8)], reads=[Bh, B_ident_b], writes=[Bp])
            for m in range(OWN):
                src = pt[:].rearrange("p (a b) -> p a b", a=8)[:, :, m * 3:m * 3 + 3]
                op(DVE, lambda e: e.tensor_copy(out=hT[:, half * 8:half * 8 + 8, m, 1:4], in_=src), reads=[Bp], writes=[B])
        pop_scope()
        return hT, B

    push_scope()
    qT = sb("qT", [128, 16, OWN * 128], BF16); B_qT = Buf()
    qiT = sb("qiT", [128, 8, OWN * 128], BF16); B_qiT = Buf()
    wtok = sb("wtok", [128, OWN, 16]); B_wtok = Buf()
    push_scope()
    hT_own, B_hTo = build_hT_own()
    alloc_ws(2)
    sq_r = Rot("sq", [128, 512], BF16, 1)
    rs_r = Rot("rs", [128, 512], F32, 1)

    def own_rhs(kc, half):
        return hT_own[:, kc, 4 * half:4 * half + 4, 4:132]

    for hq in range(4):
        wq, Bwq = load_w(w_in, [(O_Q + hq * 512, 512)])
        for hh in range(4):
            h = hq * 4 + hh
            for half in range(2):
                ps, Bp = next_ps()
                pe_group([lambda e, kc=kc: e.matmul(ps[:], wq[:, kc, hh * 128:(hh + 1) * 128], own_rhs(kc, half),
                                                    start=(kc == 0), stop=(kc == 15)) for kc in range(16)],
                         reads=[Bwq, B_hTo], writes=[Bp])
                headnorm_T(ps, Bp, gq_t[:, 0:1], B_gqk, qT[:, h, half * 512:(half + 1) * 512], B_qT, 128.0 ** -0.5)
    for c2 in range(2):
        wqi, Bwqi = load_w(w_in, [(O_QI + c2 * 512, 512)])
        for cc in range(4):
            for half in range(2):
                ps, Bp = next_ps()
                pe_group([lambda e, kc=kc: e.matmul(ps[:], wqi[:, kc, cc * 128:(cc + 1) * 128], own_rhs(kc, half),
                                                    start=(kc == 0), stop=(kc == 15)) for kc in range(16)],
                         reads=[Bwqi, B_hTo], writes=[Bp])
                op(ACT, lambda e: e.activation(out=qiT[:, c2 * 4 + cc, half * 512:(half + 1) * 512], in_=ps[:], func=AF.Copy, scale=0.125),
                   reads=[Bp], writes=[B_qiT])
    ww, Bww = load_w(w_in, [(O_WI, 16)])
    for m in range(OWN):
        ps, Bp = next_ps()
        pe_group([lambda e, kc=kc: e.matmul(ps[:, 0:16], hT_own[:, kc, m, 4:132], ww[:, kc, 0:16],
                                            start=(kc == 0), stop=(kc == 15)) for kc in range(16)],
                 reads=[Bww, B_hTo], writes=[Bp])
        op(DVE, lambda e: e.tensor_scalar(out=wtok[:, m, :], in0=ps[:, 0:16], scalar1=0.25, scalar2=None, op0=ALU.mult),
           reads=[Bp], writes=[B_wtok])
    if DEBUG:
        dma(SP, [lambda e: e.dma_start(out=dbg["qT"], in_=qT[:])], reads=[B_qT])
        dma(SP, [lambda e: e.dma_start(out=dbg["qiT"], in_=qiT[:])], reads=[B_qiT])
        dma(SP, [lambda e: e.dma_start(out=dbg["wtok"], in_=wtok[:])], reads=[B_wtok])
    pop_scope()

    kT = sb("kT", [128, 4, SEQ], BF16); B_kT = Buf()
    Vs = sb("Vs", [128, NB, 512], BF16); B_V = Buf()
    kiT = sb("kiT", [128, SEQ], BF16); B_kiT = Buf()
    dma(SP, [lambda e: e.dma_start(out=kT[:], in_=kT_d)], writes=[B_kT])
    dma(SP, [lambda e: e.dma_start(out=Vs[:], in_=v_d)], writes=[B_V])
    dma(SP, [lambda e: e.dma_start(out=kiT[:], in_=kiT_d)], writes=[B_kiT])
    EB = sb("EB", [128, 5, 16, 128], BF16); B_EB = Buf()
    push_scope()
    relb_t = sb("relb_t", [32, 16]); ohb_t = sb("ohb_t", [32, 1280]); B_rb = Buf()
    rbh = sb("rbh", [32, 2, 16], BF16); ohb_b = sb("ohb_b", [32, 1280], BF16)
    Fv = sb("Fv", [16, 1280]); B_Fv = Buf()
    EBf = sb("EBf", [128, 16, 128]); B_EBf = Buf()
    dma(SP, [lambda e: e.dma_start(out=relb_t[:], in_=relb), lambda e: e.dma_start(out=ohb_t[:], in_=c_ohb)], writes=[B_rb])
    op(DVE, lambda e: e.tensor_copy(out=rbh[:, 0, :], in_=relb_t[:]), reads=[B_rb], writes=[B_rb])
    op(DVE, lambda e: e.tensor_tensor(out=rbh[:, 1, :], in0=relb_t[:], in1=rbh[:, 0, :], op=ALU.subtract), reads=[B_rb], writes=[B_rb])
    op(DVE, lambda e: e.tensor_copy(out=ohb_b[:], in_=ohb_t[:]), reads=[B_rb], writes=[B_rb])
    for c3 in range(3 if STAGE >= 4 else 0):
        n0, n1 = c3 * 512, min(1280, c3 * 512 + 512)
        ps, Bp = next_ps()
        pe_group([lambda e: e.matmul(ps[0:16, 0:n1 - n0], rbh[:, 0, :], ohb_b[:, n0:n1], start=True, stop=False),
                  lambda e: e.matmul(ps[0:16, 0:n1 - n0], rbh[:, 1, :], ohb_b[:, n0:n1], start=False, stop=True)],
                 reads=[B_rb], writes=[Bp])
        op(ACT, lambda e: e.activation(out=Fv[:, n0:n1], in_=ps[0:16, 0:n1 - n0], func=AF.Exp), reads=[Bp], writes=[B_Fv])
    for kb in range(5 if STAGE >= 4 else 0):
        for r0 in range(0, 128, 32):
            src = Fv[:, kb * 256:(kb + 1) * 256].unsqueeze(1).broadcast_to([16, 32, 256])
            dma(SP, [lambda e: e.dma_start(out=ebz_d[:, kb, r0:r0 + 32, :], in_=src)], reads=[B_Fv], writes=[B_EBf])
    barrier()
    for kb in range(5 if STAGE >= 4 else 0):
        srcs = []
        for h in range(16):
            base = ebz_d[h, kb]
            srcs.append(bass.AP(tensor=base.tensor, offset=base.offset + 127, ap=[[255, 128], [1, 128]]))
        dma(SP, [lambda e, h=h: e.dma_start(out=EBf[:, h, :], in_=srcs[h]) for h in range(16)], reads=[B_EBf], writes=[B_EBf])
        op(DVE, lambda e: e.tensor_copy(out=EB[:, kb, :, :], in_=EBf[:]), reads=[B_EBf], writes=[B_EB])
    pop_scope()

    score = sb("score", [128, SEQ]); B_score = Buf()
    sel01 = sb("sel01", [128, SEQ], BF16); B_sel = Buf()
    selT = sb("selT", [128, NB, 128], BF16); B_selT = Buf()
    Dg = sb("Dg", [128, 16, 128], BF16); B_Dg = Buf()
    amask_t = sb("amask_t", [128, 512]); B_am = Buf()
    bs = sb("bs", [128, 16]); B_bs = Buf()
    half_c = sb("half_c", [128, 1]); B_hc = Buf()
    R_r = Rot("Rr", [128, 512], BF16, 3)
    E_r = Rot("Er", [128, 512], BF16, 2)
    P_r = Rot("Pr", [128, 512], BF16, 3)
    rec_r = Rot("rec", [128, 512], F32, 1)
    ao_r = Rot("ao", [128, 4, 128], BF16, 2)
    dma(SP, [lambda e: e.dma_start(out=amask_t[:], in_=c_amask)], writes=[B_am])
    p2t = sb("p2t", [128, 32]); B_p2 = Buf()
    dk = sb("dk", [128, 32]); B_dk = Buf()
    cntt = sb("cntt", [128, 32]); B_cnt = Buf()
    for kk_ in range(32):
        op(DVE, lambda e, kk_=kk_: e.memset(p2t[:, kk_:kk_ + 1], 2.0 ** -(kk_ + 1)), writes=[B_p2])
    op(DVE, lambda e: e.memset(half_c[:], 0.5), writes=[B_hc])
    psn[0] = 4
    NIT = 24

    def sc_init(m):
        nkb = 4 * (m + 1)
        Lk = nkb * 128
        for h in range(16):
            op(DVE, lambda e, h=h: e.tensor_scalar(out=Dg[:, h, :], in0=ident_f[:], scalar1=wtok[:, m, h:h + 1], scalar2=None, op0=ALU.mult),
               reads=[B_ident_f, B_wtok], writes=[B_Dg])
        for kt in range(m + 1):
            scp, Bscp = PS[4 + kt % 2], BPS[4 + kt % 2]

            def sc_front(h):
                po = (h % 2) * 64
                ps, Bp = next_ps()
                pe_group([lambda e: e.matmul(ps[:], qiT[po:po + 64, h // 2, m * 128:(m + 1) * 128], kiT[po:po + 64, kt * 512:(kt + 1) * 512],
                                             start=True, stop=True)], reads=[B_qiT, B_kiT], writes=[Bp])
                R, BR = R_r.next()
                op(ACT, lambda e: e.activation(out=R[:], in_=ps[:], func=AF.Relu), reads=[Bp], writes=[BR])
                return R, BR

            pend = {h: sc_front(h) for h in range(2)}
            for h in range(16):
                R, BR = pend.pop(h)
                if h + 2 < 16:
                    pend[h + 2] = sc_front(h + 2)
                pe_group([lambda e: e.matmul(scp[:], Dg[:, h, :], R[:], start=(h == 0), stop=(h == 15))], reads=[B_Dg, BR], writes=[Bscp])
            if kt < m:
                op(ACT, lambda e: e.copy(out=score[:, kt * 512:(kt + 1) * 512], in_=scp[:]), reads=[Bscp], writes=[B_score])
            else:
                op(DVE, lambda e: e.tensor_tensor(out=score[:, kt * 512:(kt + 1) * 512], in0=scp[:], in1=amask_t[:], op=ALU.add),
                   reads=[Bscp, B_am], writes=[B_score])
                jk, Bjk = rec_r.next()
                op(DVE, lambda e: e.tensor_tensor(out=jk[:], in0=scp[:], in1=amask_t[:], op=ALU.subtract), reads=[Bscp, B_am], writes=[Bjk])
                op(DVE, lambda e: e.tensor_reduce(out=bs[:, 0:1], in_=jk[:], axis=AX.X, op=ALU.min), reads=[Bjk], writes=[B_bs])
        op(DVE, lambda e: e.tensor_reduce(out=bs[:, 1:2], in_=score[:, 0:Lk], axis=AX.X, op=ALU.max), reads=[B_score], writes=[B_bs])
        if m > 0:
            op(DVE, lambda e: e.tensor_reduce(out=bs[:, 2:3], in_=score[:, 0:Lk - 512], axis=AX.X, op=ALU.min), reads=[B_score], writes=[B_bs])
            op(DVE, lambda e: e.tensor_tensor(out=bs[:, 0:1], in0=bs[:, 0:1], in1=bs[:, 2:3], op=ALU.min), reads=[B_bs], writes=[B_bs])
        op(DVE, lambda e: e.tensor_scalar(out=bs[:, 3:4], in0=bs[:, 0:1], scalar1=-1.0, scalar2=None, op0=ALU.add), reads=[B_bs], writes=[B_bs])
        op(DVE, lambda e: e.scalar_tensor_tensor(out=bs[:, 4:5], in0=bs[:, 1:2], scalar=1.0, in1=bs[:, 3:4], op0=ALU.add, op1=ALU.subtract),
           reads=[B_bs], writes=[B_bs])
        op(DVE, lambda e: e.tensor_scalar(out=dk[:, 0:NIT + 1], in0=p2t[:, 0:NIT + 1], scalar1=bs[:, 4:5], scalar2=None, op0=ALU.mult),
           reads=[B_bs, B_p2], writes=[B_dk])
        op(DVE, lambda e: e.memset(cntt[:], 0.0), writes=[B_cnt])
        op(DVE, lambda e: e.tensor_tensor(out=bs[:, 6:7], in0=bs[:, 3:4], in1=dk[:, 0:1], op=ALU.add), reads=[B_bs, B_dk], writes=[B_bs])
        jkb, Bjkb = sel01, B_sel

    def bis_iter(m, it):
        Lk = 512 * (m + 1)
        jkb, Bjkb = sel01, B_sel
        op(DVE, lambda e: e.tensor_scalar(out=jkb[:, 0:Lk], in0=score[:, 0:Lk], scalar1=bs[:, 6:7], scalar2=0.0,
                                          op0=ALU.is_ge, op1=ALU.add, accum_out=cntt[:, it:it + 1]),
           reads=[B_score, B_bs], writes=[Bjkb, B_cnt])
        op(DVE, lambda e: e.scalar_tensor_tensor(out=bs[:, 7:8], in0=cntt[:, it:it + 1], scalar=255.5, in1=dk[:, it:it + 1],
                                                 op0=ALU.is_ge, op1=ALU.mult), reads=[B_cnt, B_dk], writes=[B_bs])
        if it < NIT - 1:
            op(DVE, lambda e: e.scalar_tensor_tensor(out=bs[:, 6:7], in0=bs[:, 3:4], scalar=bs[:, 7:8], in1=dk[:, it + 1:it + 2],
                                                     op0=ALU.add, op1=ALU.add), reads=[B_bs, B_dk], writes=[B_bs])
        op(DVE, lambda e: e.tensor_tensor(out=bs[:, 3:4], in0=bs[:, 3:4], in1=bs[:, 7:8], op=ALU.add), reads=[B_bs], writes=[B_bs])

    def bis_fin(m):
        nkb = 4 * (m + 1)
        Lk = nkb * 128
        op(DVE, lambda e: e.tensor_scalar(out=sel01[:, 0:Lk], in0=score[:, 0:Lk], scalar1=bs[:, 3:4], scalar2=None, op0=ALU.is_ge),
           reads=[B_score, B_bs], writes=[B_sel])
        if DEBUG:
            dma(SP, [lambda e: e.dma_start(out=dbg["score"][m, :, 0:Lk], in_=score[:, 0:Lk])], reads=[B_score])
            dma(SP, [lambda e: e.dma_start(out=dbg["thr"][:, m, :], in_=bs[:, 3:7])], reads=[B_bs])
        for k8 in range(0, nkb, 8):
            nn = min(8, nkb - k8)
            pt, Bp = next_pt()
            pe_group([lambda e, q=q: e.transpose(out=pt[:, q * 128:(q + 1) * 128], in_=sel01[:, (k8 + q) * 128:(k8 + q + 1) * 128],
                                                 identity=ident_b[:]) for q in range(nn)], reads=[B_sel, B_ident_b], writes=[Bp])
            op(ACT, lambda e: e.copy(out=selT[:, k8:k8 + nn, :], in_=pt[:, 0:nn * 128].rearrange("p (a b) -> p a b", b=128)),
               reads=[Bp], writes=[B_selT])

    def attention(m, hook):
        nkb = 4 * (m + 1)
        for g4 in range(4 if STAGE >= 6 else 0):
            op_ps, Bop = PS[4], BPS[4]
            sm_ps, Bsm = PS[5], BPS[5]
            def att_front(kb):
                ps, Bp = next_ps()
                pe_group([lambda e: e.matmul(ps[:], kT[:, g4, kb * 128:(kb + 1) * 128], qT[:, 4 * g4:4 * g4 + 4, m * 128:(m + 1) * 128],
                                             start=True, stop=True)], reads=[B_kT, B_qT], writes=[Bp])
                E, BE = E_r.next()
                op(ACT, lambda e: e.activation(out=E[:], in_=ps[:], func=AF.Exp), reads=[Bp], writes=[BE])
                P, BP = P_r.next()
                selb = selT[:, kb, :].unsqueeze(1).broadcast_to([128, 4, 128])
                op(DVE, lambda e: e.tensor_tensor(out=P[:].rearrange("p (r t) -> p r t", r=4), in0=E[:].rearrange("p (r t) -> p r t", r=4),
                                                  in1=selb, op=ALU.mult), reads=[BE, B_selT], writes=[BP])
                kbrel = kb - (nkb - 5)
                if kbrel >= 0:
                    op(DVE, lambda e: e.tensor_tensor(out=P[:].rearrange("p (r t) -> p r t", r=4), in0=P[:].rearrange("p (r t) -> p r t", r=4),
                                                      in1=EB[:, kbrel, 4 * g4:4 * g4 + 4, :], op=ALU.mult), reads=[BP, B_EB], writes=[BP])
                return P, BP

            pend = {kb: att_front(kb) for kb in range(min(2, nkb))}
            for kb in range(nkb):
                P, BP = pend.pop(kb)
                if kb + 2 < nkb:
                    pend[kb + 2] = att_front(kb + 2)
                pe_group([lambda e: e.matmul(op_ps[:], Vs[:, kb, g4 * 128:(g4 + 1) * 128], P[:], start=(kb == 0), stop=(kb == nkb - 1))],
                         reads=[B_V, BP], writes=[Bop])
                pe_group([lambda e: e.matmul(sm_ps[:], ones_b[:], P[:], start=(kb == 0), stop=(kb == nkb - 1))],
                         reads=[B_ones_b, BP], writes=[Bsm])
                hook()
            if STAGE < 8:
                continue
            rec, Brec = rec_r.next()
            op(ACT, lambda e: e.copy(out=rec[:], in_=sm_ps[:]), reads=[Bsm], writes=[Brec])
            op(DVE, lambda e: e.reciprocal(out=rec[:], in_=rec[:]), reads=[Brec], writes=[Brec])
            ao, Bao = ao_r.next()
            op(DVE, lambda e: e.tensor_tensor(out=ao[:].rearrange("p r t -> p (r t)"), in0=op_ps[:], in1=rec[:], op=ALU.mult),
               reads=[Bop, Brec], writes=[Bao])
            dma(SP, [lambda e: e.dma_start(out=attT_d[:, 4 * g4:4 * g4 + 4, m * 128:(m + 1) * 128], in_=ao[:])], reads=[Bao])

    if STAGE >= 5:
        sc_init(0)
        for it in range(NIT):
            bis_iter(0, it)
        bis_fin(0)
        for m in range(OWN):
            todo = []
            if m + 1 < OWN:
                sc_init(m + 1)
                todo = list(range(NIT))

            def hook():
                if todo:
                    bis_iter(m + 1, todo.pop(0))

            attention(m, hook)
            while todo:
                bis_iter(m + 1, todo.pop(0))
            if m + 1 < OWN:
                bis_fin(m + 1)
    psn[0] = 6
    pop_scope()

    push_scope()
    ynT_all = sb("ynT_all", [128, 32, OWN * 128], BF16); B_ynT = Buf()
    hT_own, B_hTo = build_hT_own()
    push_scope()
    alloc_ws(3)
    diag = sb("diag6", [128, 6, 4, 128], BF16); B_diag = Buf()
    negmT = sb("negmT", [128, 128]); B_negm = Buf()
    op(DVE, lambda e: e.tensor_scalar(out=negmT[:], in0=tri_f[:], scalar1=-1.0, scalar2=-NEG, op0=ALU.add, op1=ALU.mult),
       reads=[B_tri], writes=[B_negm])
    gsg = sb("gsg", [128, 512]); B_gsg = Buf()
    pre6 = sb("pre6", [128, 6, 2, 132], BF16); B_pre6 = Buf()
    xc6_r = Rot("xc6", [128, 6, 128], BF16, 2)
    zs_r = Rot("zs", [128, 512], F32, 2)
    dtw_r = Rot("dtwb", [128, 8, 32], F32, 2)
    ahl_r = Rot("ahlb", [128, 2, 64], BF16, 1)
    xd = sb("xd", [128, 512], BF16); B_xd = Buf()
    xdsk = sb("xdsk", [128, 512]); B_xdsk = Buf()
    Sg = sb("Sg", [128, 512], BF16); B_Sg = Buf()
    cbm = sb("cbm", [128, 128]); B_cbm = Buf()
    Rm = sb("Rm", [128, 8, 128]); B_Rm = Buf()
    Rhl = sb("Rhl", [128, 2, 8, 128], BF16); B_Rhl = Buf()
    seg = sb("seg", [128, 8, 128]); B_seg = Buf()
    eab = sb("eab", [128, 8, 128]); B_eab = Buf()
    Mt = sb("Mt", [128, 8, 128], BF16); B_Mt = Buf()
    CE = sb("CE", [128, 8, 128], BF16); B_CE = Buf()
    y3 = sb("y3", [128, 512]); B_y3 = Buf()
    ynb = sb("ynb", [128, 512], BF16); B_ynb = Buf()
    nst = sb("nst", [128, 16]); B_nst = Buf()
    dtown = sb("dtown", [128, OWN, 64]); B_dtown = Buf()
    wdt, Bwdt = load_w(w_in, [(O_DT, 64)])
    for m in range(OWN):
        ps, Bp = next_ps()
        pe_group([lambda e, kc=kc: e.matmul(ps[:, 0:64], hT_own[:, kc, m, 4:132], wdt[:, kc, 0:64],
                                            start=(kc == 0), stop=(kc == 15)) for kc in range(16)],
                 reads=[Bwdt, B_hTo], writes=[Bp])
        op(DVE, lambda e: e.tensor_copy(out=dtown[:, m, :], in_=ps[:, 0:64]), reads=[Bp], writes=[B_dtown])
    for g in range(8 if STAGE >= 9 else 0):
        wz, Bwz = load_w(w_in, [(O_Z + g * 512, 512)])
        wx, Bwx = load_w(w_in, [(O_X + g * 512, 512)])
        wbc, Bwbc = load_w(w_in, [(O_B + g * 128, 128), (O_C + g * 128, 128)])
        chunks = [4 * g + i for i in range(4)] + [32 + g, 40 + g]
        build_diag(chunks)
        dma(SP, [lambda e: e.dma_start(out=gsg[:], in_=gs[:, g * 512:(g + 1) * 512].partition_broadcast(128))], writes=[B_gsg])
        def stageA2(m):
                xc6, B_xc6 = xc6_r.next()
                zs, B_zs = zs_r.next()
                for i in range(6):
                    ps, Bp = next_ps()
                    wt, Bwt, c0 = (wx, Bwx, i * 128) if i < 4 else (wbc, Bwbc, (i - 4) * 128)
                    pe_group([lambda e, kc=kc: e.matmul(ps[:, 0:132], wt[:, kc, c0:c0 + 128], hT_own[:, kc, m, 0:132],
                                                        start=(kc == 0), stop=(kc == 15)) for kc in range(16)],
                             reads=[Bwt, B_hTo], writes=[Bp])
                    op(ACT, lambda e: e.copy(out=pre6[:, i, 0, 0:132], in_=ps[:, 0:132]), reads=[Bp], writes=[B_pre6])
                    op(DVE, lambda e: e.tensor_copy(out=pre6[:, i, 1, 0:131], in_=ps[:, 1:132]), reads=[Bp], writes=[B_pre6])
                for i in range(6):
                    conv_chunk(pre6, B_pre6, i, chunks[i], 128, xc6[:, i, :], B_xc6)
                if SUB < 11:
                    return
                ps, Bp = next_ps()
                pe_group([lambda e, kc=kc: e.matmul(ps[:], hT_own[:, kc, m, 4:132], wz[:, kc, 0:512],
                                                    start=(kc == 0), stop=(kc == 15)) for kc in range(16)],
                         reads=[Bwz, B_hTo], writes=[Bp])
                op(ACT, lambda e: e.activation(out=zs[:], in_=ps[:], func=AF.Silu), reads=[Bp], writes=[B_zs])
                dtw, Bdtw = dtw_r.next()
                dt_math(dtown[:, m:m + 1, g * 8:(g + 1) * 8], B_dtown, g, 1, dtw, Bdtw)
                return xc6, B_xc6, zs, B_zs, dtw, Bdtw

        def stageB2(m, xc6, B_xc6, zs, B_zs, dtw, Bdtw):
                if SUB < 12:
                    return
                pt, Bpt = next_pt()
                pe_group([lambda e, i=i: e.transpose(out=pt[:, i * 128:(i + 1) * 128], in_=xc6[:, i, :], identity=ident_b[:]) for i in range(4)],
                         reads=[B_xc6, B_ident_b], writes=[Bpt])
                dt3 = dtw[:, 0, 0:8].unsqueeze(2).broadcast_to([128, 8, 64])
                dk3 = dsk_bc[:, g * 8:(g + 1) * 8].unsqueeze(2).broadcast_to([128, 8, 64])
                pt3 = pt[:, 0:512].rearrange("p (h d) -> p h d", h=8)
                op(DVE, lambda e: e.tensor_tensor(out=xd[:].rearrange("p (h d) -> p h d", h=8), in0=pt3, in1=dt3, op=ALU.mult),
                   reads=[Bpt, Bdtw], writes=[B_xd])
                if SUB != 132:
                    op(DVE, lambda e: e.tensor_tensor(out=xdsk[:].rearrange("p (h d) -> p h d", h=8), in0=pt3, in1=dk3, op=ALU.mult),
                       reads=[Bpt, B_ssmc], writes=[B_xdsk])
                if SUB != 131:
                    dma(SP, [lambda e: e.dma_start(out=Sg[:], in_=ssave_d[m, :, g * 512:(g + 1) * 512])], writes=[B_Sg])
                if SUB < 13 or SUB in (131, 132):
                    return
                ps, Bp = next_ps()
                pe_group([lambda e: e.matmul(ps[:, 0:128], xc6[:, 4, :], xc6[:, 5, :], start=True, stop=True)], reads=[B_xc6], writes=[Bp])
                op(DVE, lambda e: e.tensor_tensor(out=cbm[:], in0=ps[:, 0:128], in1=tri_f[:], op=ALU.mult), reads=[Bp, B_tri], writes=[B_cbm])
                if SUB == 133:
                    return
                id3 = ident_f[:].unsqueeze(1).broadcast_to([128, 8, 128])
                ac3 = dtw[:, 2, 0:8].unsqueeze(2).broadcast_to([128, 8, 128])
                for h in range(8):
                    op(DVE, lambda e, h=h: e.tensor_scalar(out=Rm[:, h, :], in0=ident_f[:], scalar1=dtw[:, 2, h:h + 1], scalar2=None, op0=ALU.mult),
                       reads=[B_ident_f, Bdtw], writes=[B_Rm])
                op(DVE, lambda e: e.tensor_copy(out=Rhl[:, 0], in_=Rm[:]), reads=[B_Rm], writes=[B_Rhl])
                op(DVE, lambda e: e.tensor_tensor(out=Rhl[:, 1], in0=Rm[:], in1=Rhl[:, 0], op=ALU.subtract), reads=[B_Rm, B_Rhl], writes=[B_Rhl])
                if SUB == 134:
                    return
                abc = []
                for hb2 in range(2):
                    pa, Bpa = next_ps()
                    pe_group([lambda e: e.matmul(pa[:], ones_b[:], Rhl[:, 0, 4 * hb2:4 * hb2 + 4, :], start=True, stop=False),
                              lambda e: e.matmul(pa[:], ones_b[:], Rhl[:, 1, 4 * hb2:4 * hb2 + 4, :], start=False, stop=True)],
                             reads=[B_Rhl, B_ones_b], writes=[Bpa])
                    abc.append((pa, Bpa))
                    pa3 = pa[:].rearrange("p (h l) -> p h l", h=4)
                    nm3 = negmT[:].unsqueeze(1).broadcast_to([128, 4, 128])
                    op(DVE, lambda e: e.tensor_tensor(out=seg[:, 4 * hb2:4 * hb2 + 4, :], in0=pa3, in1=nm3, op=ALU.add),
                       reads=[Bpa, B_negm], writes=[B_seg])
                    if SUB != 135:
                        op(ACT, lambda e: e.activation(out=eab[:, 4 * hb2:4 * hb2 + 4, :], in_=pa3, func=AF.Exp), reads=[Bpa], writes=[B_eab])
                if SUB < 14 or SUB in (133, 134, 135):
                    return
                op(DVE, lambda e: e.tensor_scalar(out=nst[:, 0:8], in0=dtw[:, 2, 0:8], scalar1=-1.0, scalar2=None, op0=ALU.mult),
                   reads=[Bdtw], writes=[B_nst])
                for h in range(8):
                    op(ACT, lambda e, h=h: e.activation(out=seg[:, h, :], in_=seg[:, h, :], func=AF.Exp, bias=nst[:, h:h + 1]),
                       reads=[B_seg, B_nst], writes=[B_seg])
                cb3 = cbm[:].unsqueeze(1).broadcast_to([128, 8, 128])
                op(DVE, lambda e: e.tensor_tensor(out=Mt[:], in0=seg[:], in1=cb3, op=ALU.mult), reads=[B_seg, B_cbm], writes=[B_Mt])
                c3_ = xc6[:, 5, :].unsqueeze(1).broadcast_to([128, 8, 128])
                op(DVE, lambda e: e.tensor_tensor(out=CE[:], in0=eab[:], in1=c3_, op=ALU.mult), reads=[B_eab, B_xc6], writes=[B_CE])
                if SUB < 15:
                    return
                psy, Bpy = next_ps()
                fns = []
                for h in range(8):
                    fns.append(lambda e, h=h: e.matmul(psy[:, h * 64:(h + 1) * 64], Mt[:, h, :], xd[:, h * 64:(h + 1) * 64], start=True, stop=False))
                    fns.append(lambda e, h=h: e.matmul(psy[:, h * 64:(h + 1) * 64], CE[:, h, :], Sg[:, h * 64:(h + 1) * 64], start=False, stop=True))
                pe_group(fns, reads=[B_Mt, B_CE, B_xd, B_Sg], writes=[Bpy])
                op(DVE, lambda e: e.tensor_tensor(out=y3[:], in0=psy[:], in1=xdsk[:], op=ALU.add), reads=[Bpy, B_xdsk], writes=[B_y3])
                op(DVE, lambda e: e.tensor_tensor(out=y3[:], in0=y3[:], in1=zs[:], op=ALU.mult), reads=[B_y3, B_zs], writes=[B_y3])
                if SUB < 16:
                    return
                op(DVE, lambda e: e.memset(nst[:, 8:9], 0.0), writes=[B_nst])
                op(ACT, lambda e: e.activation(out=ynb[:], in_=y3[:], func=AF.Square, accum_out=nst[:, 8:9]), reads=[B_y3, B_nst], writes=[B_ynb, B_nst])
                op(DVE, lambda e: e.tensor_scalar(out=nst[:, 9:10], in0=nst[:, 8:9], scalar1=1.0 / 512, scalar2=EPS, op0=ALU.mult, op1=ALU.add),
                   reads=[B_nst], writes=[B_nst])
                op(ACT, lambda e: e.activation(out=nst[:, 10:11], in_=nst[:, 9:10], func=AF.Sqrt), reads=[B_nst], writes=[B_nst])
                op(DVE, lambda e: e.reciprocal(out=nst[:, 11:12], in_=nst[:, 10:11]), reads=[B_nst], writes=[B_nst])
                op(DVE, lambda e: e.scalar_tensor_tensor(out=ynb[:], in0=y3[:], scalar=nst[:, 11:12], in1=gsg[:], op0=ALU.mult, op1=ALU.mult),
                   reads=[B_y3, B_nst, B_gsg], writes=[B_ynb])
                if SUB < 17:
                    return
                pt, Bpt = next_pt()
                pe_group([lambda e, i=i: e.transpose(out=pt[:, i * 128:(i + 1) * 128], in_=ynb[:, i * 128:(i + 1) * 128], identity=ident_b[:])
                          for i in range(4)], reads=[B_ynb, B_ident_b], writes=[Bpt])
                op(ACT, lambda e: e.copy(out=ynT_all[:, 4 * g:4 * g + 4, m * 128:(m + 1) * 128],
                                         in_=pt[:, 0:512].rearrange("p (a b) -> p a b", a=4)), reads=[Bpt], writes=[B_ynT])


        pendA2 = stageA2(0)
        for m in range(OWN):
            nxtA2 = stageA2(m + 1) if m + 1 < OWN else None
            stageB2(m, *pendA2)
            pendA2 = nxtA2
    if DEBUG:
        dma(SP, [lambda e: e.dma_start(out=dbg["ynT"], in_=ynT_all[:])], reads=[B_ynT])
    pop_scope()
    mergedT = sb("mergedT", [128, 16, OWN * 128], BF16); B_mT = Buf()
    sg_r = Rot("sg", [128, 512], F32, 2)
    tmpm_r = Rot("tmpm", [128, 512], F32, 2)
    push_scope()
    attT = sb("attT", [128, 16, OWN * 128], BF16); B_attT = Buf()
    dma(SP, [lambda e: e.dma_start(out=attT[:], in_=attT_d)], writes=[B_attT])
    alloc_ws(2)
    for cg in range(4 if STAGE >= 10 else 0):
        wab, Bwab = load_w(w_ab, [(cg * 512, 512)])
        wga, Bwga = load_w(w_in, [(O_GA + cg * 512, 512)])
        for cc in range(4):
            ct = cg * 4 + cc
            for half in range(2):
                pg, Bpg = next_ps()
                pe_group([lambda e, kc=kc: e.matmul(pg[:], wga[:, kc, cc * 128:(cc + 1) * 128], own_rhs(kc, half),
                                                    start=(kc == 0), stop=(kc == 15)) for kc in range(16)],
                         reads=[Bwga, B_hTo], writes=[Bpg])
                sg, Bsg = sg_r.next()
                op(ACT, lambda e: e.activation(out=sg[:], in_=pg[:], func=AF.Sigmoid), reads=[Bpg], writes=[Bsg])
                pa, Bpa = next_ps()
                pe_group([lambda e, kc=kc: e.matmul(pa[:], wab[:, kc, cc * 128:(cc + 1) * 128], attT[:, kc, half * 512:(half + 1) * 512],
                                                    start=(kc == 0), stop=(kc == 15)) for kc in range(16)],
                         reads=[Bwab, B_attT], writes=[Bpa])
                op(DVE, lambda e: e.tensor_tensor(out=mergedT[:, ct, half * 512:(half + 1) * 512], in0=pa[:], in1=sg[:], op=ALU.mult),
                   reads=[Bpa, Bsg], writes=[B_mT])
    pop_scope()
    push_scope()
    alloc_ws(3)
    for cg in range(4 if STAGE >= 10 else 0):
        wsb0, Bwsb0 = load_w(w_sb, [(cg * 512, 512)], row0=0)
        wsb1, Bwsb1 = load_w(w_sb, [(cg * 512, 512)], row0=2048)
        wgs, Bwgs = load_w(w_in, [(O_GS + cg * 512, 512)])
        for cc in range(4):
            ct = cg * 4 + cc
            for half in range(2):
                pg, Bpg = next_ps()
                pe_group([lambda e, kc=kc: e.matmul(pg[:], wgs[:, kc, cc * 128:(cc + 1) * 128], own_rhs(kc, half),
                                                    start=(kc == 0), stop=(kc == 15)) for kc in range(16)],
                         reads=[Bwgs, B_hTo], writes=[Bpg])
                sg, Bsg = sg_r.next()
                op(ACT, lambda e: e.activation(out=sg[:], in_=pg[:], func=AF.Sigmoid), reads=[Bpg], writes=[Bsg])
                py, Bpy = next_ps()
                pe_group([lambda e, kc=kc: e.matmul(py[:], (wsb0 if kc < 16 else wsb1)[:, kc % 16, cc * 128:(cc + 1) * 128],
                                                    ynT_all[:, kc, half * 512:(half + 1) * 512],
                                                    start=(kc == 0), stop=(kc == 31)) for kc in range(32)],
                         reads=[Bwsb0, Bwsb1, B_ynT], writes=[Bpy])
                tm, Btm = tmpm_r.next()
                op(DVE, lambda e: e.tensor_tensor(out=tm[:], in0=py[:], in1=sg[:], op=ALU.mult), reads=[Bpy, Bsg], writes=[Btm])
                op(DVE, lambda e: e.tensor_tensor(out=mergedT[:, ct, half * 512:(half + 1) * 512], in0=tm[:],
                                                  in1=mergedT[:, ct, half * 512:(half + 1) * 512], op=ALU.add),
                   reads=[Btm, B_mT], writes=[B_mT])
    pop_scope()
    dma(SP, [lambda e: e.dma_start(out=mT_d, in_=mergedT[:])], reads=[B_mT])
    pop_scope()

    push_scope()
    x1acc = sb("x1acc", [128, OWN, D]); Bx1 = [Buf() for _ in range(OWN)]
    h2T = sb("h2T", [128, 16, OWN * 128], BF16); B_h2T = Buf()
    alloc_ws(3)
    for m in range(OWN):
        dma(SP, [lambda e: e.dma_start(out=x1acc[:, m, :], in_=x_own[m])], writes=[Bx1[m]])
    push_scope()
    mT = sb("mT", [128, 16, OWN * 128], BF16); B_mTl = Buf()
    dma(SP, [lambda e: e.dma_start(out=mT[:], in_=mT_d)], writes=[B_mTl])
    for ct in range(4 if STAGE >= 10 else 0):
        wo, Bwo = load_w(w_out, [(ct * 512, 512)])
        for blk in range(OWN):
            ps, Bp = next_ps()
            pe_group([lambda e, kc=kc: e.matmul(ps[:], mT[:, kc, blk * 128:(blk + 1) * 128], wo[:, kc, 0:512],
                                                start=(kc == 0), stop=(kc == 15)) for kc in range(16)],
                     reads=[Bwo, B_mTl], writes=[Bp])
            op(DVE, lambda e: e.tensor_tensor(out=x1acc[:, blk, ct * 512:(ct + 1) * 512], in0=x1acc[:, blk, ct * 512:(ct + 1) * 512],
                                              in1=ps[:], op=ALU.add), reads=[Bp, Bx1[blk]], writes=[Bx1[blk]])
    pop_scope()
    push_scope()
    alloc_norm(g2)
    for m in range(OWN):
        hb, Bh, _, _ = norm_rows(None, 128, src_sb=(x1acc[:, m, :], Bx1[m]))
        transpose_rows(hb, Bh, 128, lambda kc0, n, m=m: h2T[:, kc0:kc0 + n, m * 128:(m + 1) * 128], B_h2T)
    pop_scope()
    uT_r = Rot("uT", [128, 4, OWN * 128], BF16, 2)
    tmp_r = Rot("tmpf", [128, 512], F32, 2)
    for fg in range(16):
        wu, Bwu = load_w(w_up, [(fg * 512, 512)])
        uT, BuT = uT_r.next()
        for fc in range(4):
            for half in range(2):
                ps, Bp = next_ps()
                pe_group([lambda e, kc=kc: e.matmul(ps[:], wu[:, kc, fc * 128:(fc + 1) * 128], h2T[:, kc, half * 512:(half + 1) * 512],
                                                    start=(kc == 0), stop=(kc == 15)) for kc in range(16)],
                         reads=[Bwu, B_h2T], writes=[Bp])
                tmp, Btmp = tmp_r.next()
                op(ACT, lambda e: e.activation(out=tmp[:], in_=ps[:], func=AF.Relu), reads=[Bp], writes=[Btmp])
                op(DVE, lambda e: e.tensor_tensor(out=uT[:, fc, half * 512:(half + 1) * 512], in0=tmp[:], in1=tmp[:], op=ALU.mult),
                   reads=[Btmp], writes=[BuT])
        i = wsc[0] % len(WS)
        wsc[0] += 1
        wdt, Bwd = WS[i], BWS[i]
        wdv = wdt[:].rearrange("p a b -> p (a b)").rearrange("p (k c) -> p k c", k=4)
        svd = w_down[fg * 512:(fg + 1) * 512, :].rearrange("(kc p) c -> p kc c", p=128)
        dma(POOL, [lambda e: e.dma_start(out=wdv, in_=svd)], writes=[Bwd], nslots=4)
        for blk in range(OWN):
            for ct in range(4):
                ps, Bp = next_ps()
                pe_group([lambda e, fc=fc: e.matmul(ps[:], uT[:, fc, blk * 128:(blk + 1) * 128], wdv[:, fc, ct * 512:(ct + 1) * 512],
                                                    start=(fc == 0), stop=(fc == 3)) for fc in range(4)],
                         reads=[BuT, Bwd], writes=[Bp])
                op(DVE, lambda e: e.tensor_tensor(out=x1acc[:, blk, ct * 512:(ct + 1) * 512], in0=x1acc[:, blk, ct * 512:(ct + 1) * 512],
                                                  in1=ps[:], op=ALU.add), reads=[Bp, Bx1[blk]], writes=[Bx1[blk]])
    for m in range(OWN):
        dma(SP, [lambda e: e.dma_start(out=out_d[m], in_=x1acc[:, m, :])], reads=[Bx1[m]])
    pop_scope()
    for key, (slots, ctr) in dma_slots.items():
        for s in slots:
            if s.cnt > 0:
                SP.wait((s, s.cnt))
    for E in (PE, ACT, DVE):
        if E.p.cnt > 0:
            SP.wait((E.p, E.p.cnt))
    return nc


_CONST = {}


def _consts(j):
    if j in _CONST:
        return _CONST[j]
    ident = np.eye(128, dtype=np.float32)
    tri = np.triu(np.ones((128, 128), np.float32))
    flags = np.zeros((128, 4), np.float32); flags[:, j] = 1.0
    t = np.arange(128)
    am = np.zeros((128, 4, 128), np.float32)
    for r in range(4):
        if r == j:
            am[:, r, :] = np.where(t[None, :] <= t[:, None], 0.0, NEG)
        elif r > j:
            am[:, r, :] = NEG
    ohb = np.zeros((32, 5 * 256), np.float32)
    for kbrel in range(5):
        delta = j + 1 - kbrel
        for v in range(255):
            dd = 128 * delta + (v - 127)
            n = max(dd, 0)
            if n < 16:
                bkt = n
            else:
                bkt = min(31, 16 + int(np.float32(np.log(np.float32(max(n, 1)) / np.float32(16)) / np.float32(np.log(8.0)) * np.float32(16))))
            ohb[bkt, kbrel * 256 + v] += 1.0
            ohb[31, kbrel * 256 + v] -= 1.0
    _CONST[j] = dict(c_ident=ident, c_tri=tri, c_flags=flags, c_amask=am.reshape(128, 512), c_ohb=ohb)
    return _CONST[j]


def make_in_maps(inputs):
    x = np.ascontiguousarray(inputs["x"], dtype=np.float32)
    shared = dict(
        w_in=np.ascontiguousarray(inputs["w_in"][0]),
        w_ab=np.ascontiguousarray(inputs["w_att_branch"][0]),
        w_sb=np.ascontiguousarray(inputs["w_ssm_branch"][0]),
        w_out=np.ascontiguousarray(inputs["w_out"][0]),
        w_up=np.ascontiguousarray(inputs["w_up"][0]),
        w_down=np.ascontiguousarray(inputs["w_down"][0]),
        norm1_g=np.ascontiguousarray(inputs["norm1_g"].reshape(1, D)),
        norm2_g=np.ascontiguousarray(inputs["norm2_g"].reshape(1, D)),
        ssm_norm_g=np.ascontiguousarray(inputs["ssm_norm_g"].reshape(1, 4096)),
        q_norm_g=np.ascontiguousarray(inputs["q_norm_g"].reshape(128, 1)),
        k_norm_g=np.ascontiguousarray(inputs["k_norm_g"].reshape(128, 1)),
        conv_w=np.ascontiguousarray(inputs["conv_w"][0].T.reshape(48, 128, 4).transpose(1, 0, 2)),
        conv_b=np.ascontiguousarray(inputs["conv_b"][0].reshape(48, 128).T),
        dt_bias=np.ascontiguousarray(inputs["dt_bias"].reshape(1, 64)),
        a_log=np.ascontiguousarray(inputs["a_log"].reshape(1, 64)),
        d_skip=np.ascontiguousarray(inputs["d_skip"].reshape(1, 64)),
        rel_bias=np.ascontiguousarray(inputs["rel_bias"]),
    )
    maps = []
    for c in range(8):
        b, j = c // 4, c % 4
        xb = x[b].reshape(NB, 128, D)
        own = np.ascontiguousarray(xb[j::4])
        halo = np.zeros((OWN, HALO, D), np.float32)
        for m in range(OWN):
            t0 = (4 * m + j) * 128
            if t0 >= HALO:
                halo[m] = x[b, t0 - HALO:t0]
        m_ = dict(shared)
        m_.update(x_all=x[b], x_own=own, x_halo=halo.reshape(OWN * HALO, D))
        m_.update(_consts(j))
        maps.append(m_)
    return maps


_NC = None


def kernel(**inputs):
    global _NC
    if _NC is None:
        _NC = build_program()
    maps = make_in_maps(inputs)
    res = run_bass_kernel_spmd(_NC, maps, core_ids=list(range(8)))
    out = np.zeros((2, SEQ, D), np.float32)
    for c in range(8):
        b, j = c // 4, c % 4
        o = res.results[c]["out"]
        out[b].reshape(NB, 128, D)[j::4] = o
    return out
```

```python
from contextlib import ExitStack
import numpy as np
import concourse.bass as bass
import concourse.mybir as mybir
from concourse.bass_utils import run_bass_kernel_spmd

F32 = mybir.dt.float32
BF16 = mybir.dt.bfloat16
AF = mybir.ActivationFunctionType
ALU = mybir.AluOpType
AX = mybir.AxisListType

D = 2048
SEQ = 4096
NB = 32
OWN = 8
HALO = 3
BW = 128 + HALO
IN_DIM = 18576
O_GA, O_GS, O_Q, O_K, O_V, O_QI, O_KI, O_WI, O_Z, O_X, O_B, O_C, O_DT = (
    0, 2048, 4096, 6144, 6656, 7168, 8192, 8256, 8272, 12368, 16464, 17488, 18512)
EPS = 1e-6
NEG = -30000.0
DEBUG = False
STAGE = 99

SUB = 99


class Prod:
    def __init__(self, nc, name):
        self.sem = nc.alloc_semaphore(name)
        self.cnt = 0
        self.name = name


class Eng:
    def __init__(self, nc, e, name, selfsync=True):
        self.e = e
        self.p = Prod(nc, "s_" + name)
        self.seen = {}
        self.selfsync = selfsync
        self.name = name

    def wait(self, tok):
        if tok is None:
            return
        prod, val = tok
        if prod is self.p and not self.selfsync:
            return
        if self.seen.get(prod, 0) >= val:
            return
        self.seen[prod] = val
        self.e.wait_ge(prod.sem, val)


class Buf:
    def __init__(self, name="b"):
        self.name = name
        self.w = None
        self.rs = {}


def build_program():
    nc = bass.Bass("TRN2", target_bir_lowering=False)
    PE = Eng(nc, nc.tensor, "pe", selfsync=False)
    ACT = Eng(nc, nc.scalar, "act")
    DVE = Eng(nc, nc.vector, "dve")
    POOL = Eng(nc, nc.gpsimd, "pool")
    SP = Eng(nc, nc.sync, "sp")

    def deps(E, reads, writes):
        for b in reads:
            E.wait(b.w)
        for b in writes:
            E.wait(b.w)
            for t in list(b.rs.items()):
                E.wait(t)

    def commit(tok, reads, writes):
        prod, val = tok
        for b in reads:
            b.rs[prod] = val
        for b in writes:
            b.w = tok
            b.rs = {}

    def op(E, fn, reads=(), writes=()):
        extra = [b for b in reads if getattr(b, "psum", False) and b not in writes]
        if extra:
            writes = list(writes) + extra
        deps(E, reads, writes)
        inst = fn(E.e)
        E.p.cnt += 1
        inst.then_inc(E.p.sem, 1)
        commit((E.p, E.p.cnt), reads, writes)
        return inst

    def pe_group(fns, reads=(), writes=()):
        deps(PE, reads, writes)
        inst = None
        for fn in fns:
            inst = fn(nc.tensor)
        PE.p.cnt += 1
        inst.then_inc(PE.p.sem, 1)
        commit((PE.p, PE.p.cnt), reads, writes)

    dma_slots = {}

    def dma(E, fns, reads=(), writes=(), nslots=6):
        key = E.name
        if key not in dma_slots:
            dma_slots[key] = ([Prod(nc, f"d_{key}{i}") for i in range(nslots)], [0])
        slots, ctr = dma_slots[key]
        slot = slots[ctr[0] % len(slots)]
        ctr[0] += 1
        deps(E, reads, writes)
        if slot.cnt > 0:
            E.wait((slot, slot.cnt))
        for fn in fns:
            fn(E.e).then_inc(slot.sem, 16)
            slot.cnt += 16
        commit((slot, slot.cnt), reads, writes)

    def din(name, shape, dt=F32):
        return nc.dram_tensor(name, list(shape), dt, kind="ExternalInput").ap()

    x_all = din("x_all", [SEQ, D])
    x_own = din("x_own", [OWN, 128, D])
    x_halo = din("x_halo", [OWN * HALO, D])
    w_in = din("w_in", [D, IN_DIM])
    w_ab = din("w_ab", [2048, D])
    w_sb = din("w_sb", [4096, D])
    w_out = din("w_out", [D, D])
    w_up = din("w_up", [D, 8192])
    w_down = din("w_down", [8192, D])
    g1 = din("norm1_g", [1, D])
    g2 = din("norm2_g", [1, D])
    gs = din("ssm_norm_g", [1, 4096])
    gq = din("q_norm_g", [128, 1])
    gk = din("k_norm_g", [128, 1])
    cw = din("conv_w", [128, 48, 4])
    cb = din("conv_b", [128, 48])
    dtb = din("dt_bias", [1, 64])
    alog = din("a_log", [1, 64])
    dsk = din("d_skip", [1, 64])
    relb = din("rel_bias", [32, 16])
    c_ident = din("c_ident", [128, 128])
    c_tri = din("c_tri", [128, 128])
    c_flags = din("c_flags", [128, 4])
    c_amask = din("c_amask", [128, 512])
    c_ohb = din("c_ohb", [32, 5 * 256])

    out_d = nc.dram_tensor("out", [OWN, 128, D], F32, kind="ExternalOutput").ap()
    skind = "ExternalOutput" if DEBUG else "Internal"
    kT_d = nc.dram_tensor("kT_d", [128, 4, SEQ], BF16, kind=skind).ap()
    v_d = nc.dram_tensor("v_d", [128, NB, 512], BF16, kind=skind).ap()
    kiT_d = nc.dram_tensor("kiT_d", [128, SEQ], BF16, kind=skind).ap()
    ssave_d = nc.dram_tensor("ssave_d", [OWN, 128, 4096], BF16, kind=skind).ap()

    attT_d = nc.dram_tensor("attT_d", [128, 16, OWN * 128], BF16, kind=skind).ap()
    ebz_d = nc.dram_tensor("ebz_d", [16, 5, 128, 256], F32, kind="Internal").ap()
    mT_d = nc.dram_tensor("mT_d", [128, 16, OWN * 128], BF16, kind=skind).ap()
    dtraw_d = nc.dram_tensor("dtraw_d", [NB, 128, 64], F32, kind="Internal").ap()
    dbg = {}
    if DEBUG:
        dbg["qT"] = nc.dram_tensor("dbg_qT", [128, 16, OWN * 128], BF16, kind="ExternalOutput").ap()
        dbg["qiT"] = nc.dram_tensor("dbg_qiT", [128, 8, OWN * 128], BF16, kind="ExternalOutput").ap()
        dbg["wtok"] = nc.dram_tensor("dbg_wtok", [128, OWN, 16], F32, kind="ExternalOutput").ap()
        dbg["score"] = nc.dram_tensor("dbg_score", [OWN, 128, SEQ], F32, kind="ExternalOutput").ap()
        dbg["thr"] = nc.dram_tensor("dbg_thr", [128, OWN, 4], F32, kind="ExternalOutput").ap()
        dbg["ynT"] = nc.dram_tensor("dbg_ynT", [128, 32, OWN * 128], BF16, kind="ExternalOutput").ap()
    scopes = [ExitStack()]

    sbn = [0]

    def sb(name, shape, dt=F32):
        sbn[0] += 1
        return scopes[-1].enter_context(nc.sbuf_tensor(f"{name}_{sbn[0]}", list(shape), dt))

    def push_scope():
        scopes.append(ExitStack())

    def pop_scope():
        barrier()
        scopes.pop().close()

    def barrier():
        prods = [E.p for E in (PE, ACT, DVE, POOL, SP)]
        for key, (slots, ctr) in dma_slots.items():
            prods += slots
        for E in (PE, ACT, DVE, POOL, SP):
            for p in prods:
                if p.cnt > 0 and p is not E.p:
                    E.wait((p, p.cnt))

    ident_f = sb("ident_f", [128, 128]); B_ident_f = Buf()
    ident_b = sb("ident_b", [128, 128], BF16); B_ident_b = Buf()
    ones_b = sb("ones_b", [128, 128], BF16); B_ones_b = Buf()
    tri_f = sb("tri_f", [128, 128]); B_tri = Buf()
    gq_t = sb("gq_t", [128, 1]); gk_t = sb("gk_t", [128, 1]); B_gqk = Buf()
    flags_t = sb("flags_t", [128, 4]); B_flags = Buf()
    cw_t = sb("cw_t", [128, 48, 4]); cb_t = sb("cb_t", [128, 48]); B_cw = Buf()
    dtb_bc = sb("dtb_bc", [128, 64]); A_bc = sb("A_bc", [128, 64]); dsk_bc = sb("dsk_bc", [128, 64]); B_ssmc = Buf()

    dma(SP, [lambda e: e.dma_start(out=ident_f[:], in_=c_ident)], writes=[B_ident_f])
    dma(SP, [lambda e: e.dma_start(out=tri_f[:], in_=c_tri)], writes=[B_tri])
    dma(SP, [lambda e: e.dma_start(out=gq_t[:], in_=gq), lambda e: e.dma_start(out=gk_t[:], in_=gk)], writes=[B_gqk])
    dma(SP, [lambda e: e.dma_start(out=flags_t[:], in_=c_flags)], writes=[B_flags])
    dma(SP, [lambda e: e.dma_start(out=cw_t[:], in_=cw), lambda e: e.dma_start(out=cb_t[:], in_=cb)], writes=[B_cw])
    dma(SP, [lambda e: e.dma_start(out=dtb_bc[:], in_=dtb.partition_broadcast(128)),
             lambda e: e.dma_start(out=A_bc[:], in_=alog.partition_broadcast(128)),
             lambda e: e.dma_start(out=dsk_bc[:], in_=dsk.partition_broadcast(128))], writes=[B_ssmc])
    op(DVE, lambda e: e.tensor_copy(out=ident_b[:], in_=ident_f[:]), reads=[B_ident_f], writes=[B_ident_b])
    op(DVE, lambda e: e.memset(ones_b[:], 1.0), writes=[B_ones_b])
    tri_b = sb("tri_b", [128, 128], BF16)
    op(DVE, lambda e: e.tensor_copy(out=tri_b[:], in_=tri_f[:]), reads=[B_tri], writes=[B_tri])
    op(ACT, lambda e: e.activation(out=A_bc[:], in_=A_bc[:], func=AF.Exp), reads=[B_ssmc], writes=[B_ssmc])
    op(DVE, lambda e: e.tensor_scalar(out=A_bc[:], in0=A_bc[:], scalar1=-1.0, scalar2=None, op0=ALU.mult), reads=[B_ssmc], writes=[B_ssmc])

    PS = [nc.alloc_psum_tensor(f"ps{i}", [128, 512], F32) for i in range(6)]
    BPS = [Buf() for i in range(6)]
    for b_ in BPS:
        b_.psum = True
    PT = [nc.alloc_psum_tensor(f"pt{i}", [128, 1024], BF16) for i in range(2)]
    BPT = [Buf() for i in range(2)]
    for b_ in BPT:
        b_.psum = True
    psc = [0]

    psn = [6]

    def next_ps():
        i = psc[0] % psn[0]
        psc[0] += 1
        return PS[i], BPS[i]

    ptc = [0]

    def next_pt():
        i = ptc[0] % 2
        ptc[0] += 1
        return PT[i], BPT[i]

    WS = []
    BWS = []
    wsc = [0]

    def alloc_ws(n):
        WS.clear(); BWS.clear()
        for i in range(n):
            WS.append(sb(f"ws{i}_{wsc[0]}", [128, 16, 512], BF16)); BWS.append(Buf())

    def load_w(src, pieces, kchunks=16, row0=0):
        i = wsc[0] % len(WS)
        wsc[0] += 1
        t, b = WS[i], BWS[i]
        fns = []
        off = 0
        for (c0, ncol) in pieces:
            for k0 in range(0, kchunks, 4):
                k1 = min(kchunks, k0 + 4)
                sv = src[row0 + k0 * 128:row0 + k1 * 128, c0:c0 + ncol].rearrange("(kc p) c -> p kc c", p=128)
                fns.append(lambda e, sv=sv, off=off, ncol=ncol, k0=k0, k1=k1: e.dma_start(out=t[:, k0:k1, off:off + ncol], in_=sv))
            off += ncol
        dma(POOL, fns, writes=[b], nslots=4)
        return t, b

    class Rot:
        def __init__(self, name, shape, dt, n):
            self.t = [sb(f"{name}{i}", shape, dt) for i in range(n)]
            self.b = [Buf() for i in range(n)]
            self.c = 0

        def next(self):
            i = self.c % len(self.t)
            self.c += 1
            return self.t[i], self.b[i]

    NR = {}

    def alloc_norm(gsrc):
        NR["xb"] = Rot("xb", [128, D], F32, 2)
        NR["hb"] = Rot("hb", [128, D], BF16, 2)
        NR["st"] = Rot("st", [128, 4], F32, 2)
        NR["g"] = sb("gbc", [128, D]); NR["Bg"] = Buf()
        dma(SP, [lambda e: e.dma_start(out=NR["g"][:], in_=gsrc.partition_broadcast(128))], writes=[NR["Bg"]])

    def norm_rows(src, rows, src_sb=None):
        gbc, Bg = NR["g"], NR["Bg"]
        hb, Bh = NR["hb"].next()
        st, Bs = NR["st"].next()
        if src_sb is None:
            xt, Bx = NR["xb"].next()
            dma(SP, [lambda e: e.dma_start(out=xt[0:rows, :], in_=src)], writes=[Bx])
        else:
            xt, Bx = src_sb
        junk, Bjunk = hb, Bh
        op(DVE, lambda e: e.memset(st[0:rows, 0:1], 0.0), writes=[Bs])
        op(ACT, lambda e: e.activation(out=junk[0:rows, :], in_=xt[0:rows, :], func=AF.Square, accum_out=st[0:rows, 0:1]),
           reads=[Bx, Bs], writes=[Bjunk, Bs])
        op(DVE, lambda e: e.tensor_scalar(out=st[0:rows, 1:2], in0=st[0:rows, 0:1], scalar1=1.0 / D, scalar2=EPS,
                                          op0=ALU.mult, op1=ALU.add), reads=[Bs], writes=[Bs])
        op(ACT, lambda e: e.activation(out=st[0:rows, 2:3], in_=st[0:rows, 1:2], func=AF.Sqrt), reads=[Bs], writes=[Bs])
        op(DVE, lambda e: e.reciprocal(out=st[0:rows, 3:4], in_=st[0:rows, 2:3]), reads=[Bs], writes=[Bs])
        op(DVE, lambda e: e.scalar_tensor_tensor(out=hb[0:rows, :], in0=xt[0:rows, :], scalar=st[0:rows, 3:4],
                                                 in1=gbc[0:rows, :], op0=ALU.mult, op1=ALU.mult),
           reads=[Bx, Bs, Bg], writes=[Bh])
        return hb, Bh, xt, Bx

    def transpose_rows(hb, Bh, rows, dst_fn, Bdst):
        for half in range(2):
            pt, Bp = next_pt()
            fns = []
            for q8 in range(8):
                kc = half * 8 + q8
                fns.append(lambda e, kc=kc, q8=q8: e.transpose(out=pt[:, q8 * 128:q8 * 128 + rows],
                                                               in_=hb[0:rows, kc * 128:(kc + 1) * 128],
                                                               identity=ident_b[0:rows, 0:rows]))
            pe_group(fns, reads=[Bh, B_ident_b], writes=[Bp])
            src = pt[:].rearrange("p (a b) -> p a b", a=8)[:, :, 0:rows]
            op(ACT, lambda e: e.copy(out=dst_fn(half * 8, 8), in_=src), reads=[Bp], writes=[Bdst])

    push_scope()
    hT_all = sb("hT_all", [128, 16, SEQ], BF16); B_hT = Buf()
    alloc_ws(2)
    push_scope()
    alloc_norm(g1)
    for blk in range(NB):
        hb, Bh, _, _ = norm_rows(x_all[blk * 128:(blk + 1) * 128, :], 128)
        transpose_rows(hb, Bh, 128, lambda kc0, n, blk=blk: hT_all[:, kc0:kc0 + n, blk * 128:(blk + 1) * 128], B_hT)
    pop_scope()
    push_scope()

    stg_r = Rot("stg", [128, 512], BF16, 2)
    push_scope()
    sq_r = Rot("sq", [128, 512], BF16, 1)
    rs_r = Rot("rs", [128, 512], F32, 1)

    def headnorm_T(ps, Bp, gcol, Bgc, outt, Bout, post_scale):
        sq, Bsq = sq_r.next()
        op(ACT, lambda e: e.activation(out=sq[:], in_=ps[:], func=AF.Square), reads=[Bp], writes=[Bsq])
        ps2, Bp2 = next_ps()
        pe_group([lambda e: e.matmul(ps2[:], ones_b[:], sq[:], start=True, stop=True)], reads=[Bsq, B_ones_b], writes=[Bp2])
        rs, Brs = rs_r.next()
        op(DVE, lambda e: e.tensor_scalar(out=rs[:], in0=ps2[:], scalar1=1.0 / 128, scalar2=EPS, op0=ALU.mult, op1=ALU.add),
           reads=[Bp2], writes=[Brs])
        op(ACT, lambda e: e.activation(out=rs[:], in_=rs[:], func=AF.Sqrt), reads=[Brs], writes=[Brs])
        op(DVE, lambda e: e.reciprocal(out=rs[:], in_=rs[:]), reads=[Brs], writes=[Brs])
        if post_scale != 1.0:
            op(DVE, lambda e: e.tensor_scalar(out=rs[:], in0=rs[:], scalar1=post_scale, scalar2=None, op0=ALU.mult),
               reads=[Brs], writes=[Brs])
        op(DVE, lambda e: e.scalar_tensor_tensor(out=outt, in0=ps[:], scalar=gcol, in1=rs[:], op0=ALU.mult, op1=ALU.mult),
           reads=[Bp, Brs, Bgc], writes=[Bout])

    wk, Bwk = load_w(w_in, [(O_K, 512)])
    for T in range(8):
        for g in range(4):
            ps, Bp = next_ps()
            pe_group([lambda e, kc=kc: e.matmul(ps[:], wk[:, kc, g * 128:(g + 1) * 128], hT_all[:, kc, T * 512:(T + 1) * 512],
                                                start=(kc == 0), stop=(kc == 15)) for kc in range(16)],
                     reads=[Bwk, B_hT], writes=[Bp])
            stg, Bstg = stg_r.next()
            headnorm_T(ps, Bp, gk_t[:, 0:1], B_gqk, stg[:], Bstg, 1.0)
            dma(SP, [lambda e: e.dma_start(out=kT_d[:, g, T * 512:(T + 1) * 512], in_=stg[:])], reads=[Bstg])
    wv, Bwv = load_w(w_in, [(O_V, 512)])
    for blk in range(NB):
        ps, Bp = next_ps()
        pe_group([lambda e, kc=kc: e.matmul(ps[:], hT_all[:, kc, blk * 128:(blk + 1) * 128], wv[:, kc, 0:512],
                                            start=(kc == 0), stop=(kc == 15)) for kc in range(16)],
                 reads=[Bwv, B_hT], writes=[Bp])
        stg, Bstg = stg_r.next()
        op(ACT, lambda e: e.copy(out=stg[:], in_=ps[:]), reads=[Bp], writes=[Bstg])
        dma(SP, [lambda e: e.dma_start(out=v_d[:, blk, :], in_=stg[:])], reads=[Bstg])
    wki, Bwki = load_w(w_in, [(O_KI, 64), (O_KI, 64)])
    for T in range(8):
        ps, Bp = next_ps()
        pe_group([lambda e, kc=kc: e.matmul(ps[:], wki[:, kc, 0:128], hT_all[:, kc, T * 512:(T + 1) * 512],
                                            start=(kc == 0), stop=(kc == 15)) for kc in range(16)],
                 reads=[Bwki, B_hT], writes=[Bp])
        stg, Bstg = stg_r.next()
        op(ACT, lambda e: e.copy(out=stg[:], in_=ps[:]), reads=[Bp], writes=[Bstg])
        dma(SP, [lambda e: e.dma_start(out=kiT_d[:, T * 512:(T + 1) * 512], in_=stg[:])], reads=[Bstg])

    wdt, Bwdt = load_w(w_in, [(O_DT, 64)])
    dts_r = Rot("dts", [128, 64], F32, 2)
    B_dtraw = Buf()
    for blk in range(NB):
        ps, Bp = next_ps()
        pe_group([lambda e, kc=kc: e.matmul(ps[:, 0:64], hT_all[:, kc, blk * 128:(blk + 1) * 128], wdt[:, kc, 0:64],
                                            start=(kc == 0), stop=(kc == 15)) for kc in range(16)],
                 reads=[Bwdt, B_hT], writes=[Bp])
        dts, Bdts = dts_r.next()
        op(DVE, lambda e: e.tensor_copy(out=dts[:], in_=ps[:, 0:64]), reads=[Bp], writes=[Bdts])
        dma(SP, [lambda e: e.dma_start(out=dtraw_d[blk], in_=dts[:])], reads=[Bdts], writes=[B_dtraw])
    pop_scope()
    dtr_r = Rot("dtr", [128, 4, 8], F32, 2)

    pre_r = Rot("pre", [128, 5, 2, 516], BF16, 2)
    xc_r = Rot("xc", [128, 5, 512], BF16, 1)
    diag = sb("diag", [128, 5, 4, 128], BF16); B_diag = Buf()
    Sst = sb("Sst", [128, 512], F32); B_S = Buf()
    Ssel = sb("Ssel", [128, 512], F32); B_Ssel = Buf()
    dtw_r = Rot("dtw", [128, 8, 32], F32, 2)
    xw_r = Rot("xw", [128, 512], BF16, 2)
    bt_r = Rot("bt", [128, 128], BF16, 2)

    def conv_chunk(pre, Bpre, i, c, width, outt, Bout):
        ps, Bp = next_ps()
        def tap(kk):
            return pre[:, i, 0, 1 + kk:1 + kk + width] if kk % 2 == 1 else pre[:, i, 1, kk:kk + width]
        pe_group([lambda e, kk=kk: e.matmul(ps[:, 0:width], diag[:, i, kk, :], tap(kk),
                                            start=(kk == 0), stop=(kk == 3)) for kk in range(4)],
                 reads=[Bpre, B_diag], writes=[Bp])
        op(ACT, lambda e: e.activation(out=outt, in_=ps[:, 0:width], func=AF.Silu, bias=cb_t[:, c:c + 1]),
           reads=[Bp, B_cw], writes=[Bout])

    def build_diag(chunks):
        for i, c in enumerate(chunks):
            for kk in range(4):
                op(DVE, lambda e, i=i, c=c, kk=kk: e.tensor_scalar(out=diag[:, i, kk, :], in0=ident_f[:], scalar1=cw_t[:, c, kk:kk + 1],
                                                                 scalar2=None, op0=ALU.mult),
                   reads=[B_ident_f, B_cw], writes=[B_diag])

    ahl_r = Rot("ahl", [128, 2, 64], BF16, 2)

    def dt_front(src3, Bpd, g, nblk, dtw, Bdtw):
        n = nblk * 8
        v3 = lambda r: dtw[:, r, 0:n].rearrange("p (b h) -> p b h", h=8)
        bias3 = dtb_bc[:, g * 8:(g + 1) * 8].unsqueeze(1).broadcast_to([128, nblk, 8])
        A3 = A_bc[:, g * 8:(g + 1) * 8].unsqueeze(1).broadcast_to([128, nblk, 8])
        op(DVE, lambda e: e.tensor_tensor(out=v3(0), in0=src3, in1=bias3, op=ALU.add),
           reads=[Bpd, B_ssmc], writes=[Bdtw])
        op(ACT, lambda e: e.activation(out=dtw[:, 0, 0:n], in_=dtw[:, 0, 0:n], func=AF.Exp), reads=[Bdtw], writes=[Bdtw])
        op(ACT, lambda e: e.activation(out=dtw[:, 0, 0:n], in_=dtw[:, 0, 0:n], func=AF.Ln, bias=1.0), reads=[Bdtw], writes=[Bdtw])
        op(DVE, lambda e: e.tensor_tensor(out=v3(1), in0=v3(0), in1=A3, op=ALU.mult), reads=[Bdtw, B_ssmc], writes=[Bdtw])
        ahl, Bahl = ahl_r.next()
        op(DVE, lambda e: e.tensor_copy(out=ahl[:, 0, 0:n], in_=dtw[:, 1, 0:n]), reads=[Bdtw], writes=[Bahl])
        op(DVE, lambda e: e.tensor_tensor(out=ahl[:, 1, 0:n], in0=dtw[:, 1, 0:n], in1=ahl[:, 0, 0:n], op=ALU.subtract),
           reads=[Bdtw, Bahl], writes=[Bahl])
        return ahl, Bahl

    def dt_back(nblk, dtw, Bdtw, ahl, Bahl):
        n = nblk * 8
        pc, Bpc = next_ps()
        pe_group([lambda e: e.matmul(pc[:, 0:n], tri_b[:], ahl[:, 0, 0:n], start=True, stop=False),
                  lambda e: e.matmul(pc[:, 0:n], tri_b[:], ahl[:, 1, 0:n], start=False, stop=True),
                  lambda e: e.matmul(pc[:, 64:64 + n], ones_b[:], ahl[:, 0, 0:n], start=True, stop=False),
                  lambda e: e.matmul(pc[:, 64:64 + n], ones_b[:], ahl[:, 1, 0:n], start=False, stop=True)],
                 reads=[Bahl, B_tri, B_ones_b], writes=[Bpc])
        op(ACT, lambda e: e.copy(out=dtw[:, 2, 0:n], in_=pc[:, 0:n]), reads=[Bpc], writes=[Bdtw])
        op(DVE, lambda e: e.tensor_tensor(out=dtw[:, 3, 0:n], in0=pc[:, 64:64 + n], in1=dtw[:, 2, 0:n], op=ALU.subtract),
           reads=[Bpc, Bdtw], writes=[Bdtw])
        op(ACT, lambda e: e.activation(out=dtw[:, 3, 0:n], in_=dtw[:, 3, 0:n], func=AF.Exp), reads=[Bdtw], writes=[Bdtw])
        op(ACT, lambda e: e.activation(out=dtw[:, 4, 0:n], in_=pc[:, 64:64 + n], func=AF.Exp), reads=[Bpc], writes=[Bdtw])
        op(DVE, lambda e: e.tensor_tensor(out=dtw[:, 5, 0:n], in0=dtw[:, 0, 0:n], in1=dtw[:, 3, 0:n], op=ALU.mult),
           reads=[Bdtw], writes=[Bdtw])
        op(ACT, lambda e: e.activation(out=dtw[:, 6, 0:n], in_=dtw[:, 2, 0:n], func=AF.Exp), reads=[Bdtw], writes=[Bdtw])


    def dt_math(src3, Bpd, g, nblk, dtw, Bdtw):
        ahl, Bahl = dt_front(src3, Bpd, g, nblk, dtw, Bdtw)
        dt_back(nblk, dtw, Bdtw, ahl, Bahl)

    if STAGE >= 2:
        for g in range(8):
            wx, Bwx = load_w(w_in, [(O_X + g * 512, 512)])
            wb, Bwb = load_w(w_in, [(O_B + g * 128, 128)])
            chunks = [4 * g + i for i in range(4)] + [32 + g]
            build_diag(chunks)
            op(DVE, lambda e: e.memset(Sst[:], 0.0), writes=[B_S])
            pre_prev_box = [None]
            dtctx = {}

            def stageA(T):
                    dtr, Bdtr = dtr_r.next()
                    with nc.allow_non_contiguous_dma(reason="small dt slices"):
                        dma(SP, [lambda e: e.dma_start(out=dtr[:], in_=dtraw_d[T * 4:(T + 1) * 4, :, g * 8:(g + 1) * 8].rearrange("b p h -> p b h"))],
                            reads=[B_dtraw], writes=[Bdtr])
                    dtw, Bdtw = dtw_r.next()
                    ahl, Bahl = dt_front(dtr[:], Bdtr, g, 4, dtw, Bdtw)
                    dtctx[T] = (dtw, Bdtw, ahl, Bahl)
                    pre, Bpre = pre_r.next()
                    if T == 0:
                        op(DVE, lambda e: e.memset(pre[:, :, :, 0:4], 0.0), writes=[Bpre])
                    else:
                        pp, Bpp = pre_prev_box[0]
                        op(DVE, lambda e: e.tensor_copy(out=pre[:, :, 0, 0:4], in_=pp[:, :, 0, 512:516]), reads=[Bpp], writes=[Bpre])
                        op(DVE, lambda e: e.tensor_copy(out=pre[:, :, 1, 0:4], in_=pp[:, :, 1, 512:516]), reads=[Bpp], writes=[Bpre])
                    for i in range(5):
                        ps, Bp = next_ps()
                        wt, Bwt = (wx, Bwx) if i < 4 else (wb, Bwb)
                        c0 = i * 128 if i < 4 else 0
                        pe_group([lambda e, kc=kc: e.matmul(ps[:], wt[:, kc, c0:c0 + 128], hT_all[:, kc, T * 512:(T + 1) * 512],
                                                            start=(kc == 0), stop=(kc == 15)) for kc in range(16)],
                                 reads=[Bwt, B_hT], writes=[Bp])
                        op(ACT, lambda e: e.copy(out=pre[:, i, 0, 4:516], in_=ps[:]), reads=[Bp], writes=[Bpre])
                        op(DVE, lambda e: e.tensor_copy(out=pre[:, i, 1, 3:515], in_=ps[:]), reads=[Bp], writes=[Bpre])
                    pre_prev_box[0] = (pre, Bpre)
                    return pre, Bpre

            def tr_step(T, r, xc, Bxc, dtw, Bdtw):
                pt, Bpt = next_pt()
                pe_group([lambda e, i=i: e.transpose(out=pt[:, i * 128:(i + 1) * 128], in_=xc[:, i, r * 128:(r + 1) * 128],
                                                     identity=ident_b[:]) for i in range(5)],
                         reads=[Bxc, B_ident_b], writes=[Bpt])
                xw, Bxw = xw_r.next()
                bt, Bbt = bt_r.next()
                sc3 = dtw[:, 5, r * 8:(r + 1) * 8].unsqueeze(2).broadcast_to([128, 8, 64])
                op(DVE, lambda e: e.tensor_tensor(out=xw[:].rearrange("p (h d) -> p h d", h=8),
                                                  in0=pt[:, 0:512].rearrange("p (h d) -> p h d", h=8), in1=sc3, op=ALU.mult),
                   reads=[Bpt, Bdtw], writes=[Bxw])
                op(ACT, lambda e: e.copy(out=bt[:], in_=pt[:, 512:640]), reads=[Bpt], writes=[Bbt])
                return xw, Bxw, bt, Bbt

            def st_step(T, r, xw, Bxw, bt, Bbt, dtw, Bdtw):
                if r == 0:
                    op(DVE, lambda e: e.tensor_scalar(out=Ssel[:], in0=Sst[:], scalar1=flags_t[:, 0:1], scalar2=None, op0=ALU.mult),
                       reads=[B_S, B_flags], writes=[B_Ssel])
                else:
                    op(DVE, lambda e: e.scalar_tensor_tensor(out=Ssel[:], in0=Sst[:], scalar=flags_t[:, r:r + 1], in1=Ssel[:],
                                                             op0=ALU.mult, op1=ALU.add), reads=[B_S, B_flags, B_Ssel], writes=[B_Ssel])
                ps, Bp = next_ps()
                pe_group([lambda e: e.matmul(ps[:], bt[:], xw[:], start=True, stop=True)], reads=[Bbt, Bxw], writes=[Bp])
                cd3 = dtw[:, 4, r * 8:(r + 1) * 8].unsqueeze(2).broadcast_to([128, 8, 64])
                op(DVE, lambda e: e.tensor_tensor(out=Sst[:].rearrange("p (h d) -> p h d", h=8),
                                                  in0=Sst[:].rearrange("p (h d) -> p h d", h=8), in1=cd3, op=ALU.mult),
                   reads=[B_S, Bdtw], writes=[B_S])
                op(DVE, lambda e: e.tensor_tensor(out=Sst[:], in0=Sst[:], in1=ps[:], op=ALU.add), reads=[B_S, Bp], writes=[B_S])

            def stageB1(T, pre, Bpre):
                xc, Bxc = xc_r.next()
                for i in range(5):
                    conv_chunk(pre, Bpre, i, chunks[i], 512, xc[:, i, :], Bxc)
                dtw, Bdtw, ahl, Bahl = dtctx.pop(T)
                dt_back(4, dtw, Bdtw, ahl, Bahl)
                t0 = tr_step(T, 0, xc, Bxc, dtw, Bdtw)
                t1 = tr_step(T, 1, xc, Bxc, dtw, Bdtw)
                return xc, Bxc, dtw, Bdtw, t0, t1

            def stageB2(T, xc, Bxc, dtw, Bdtw, t0, t1):
                st_step(T, 0, *t0, dtw, Bdtw)
                t2 = tr_step(T, 2, xc, Bxc, dtw, Bdtw)
                st_step(T, 1, *t1, dtw, Bdtw)
                t3 = tr_step(T, 3, xc, Bxc, dtw, Bdtw)
                st_step(T, 2, *t2, dtw, Bdtw)
                st_step(T, 3, *t3, dtw, Bdtw)
                stg, Bstg = stg_r.next()
                op(DVE, lambda e: e.tensor_copy(out=stg[:], in_=Ssel[:]), reads=[B_Ssel], writes=[Bstg])
                dma(SP, [lambda e: e.dma_start(out=ssave_d[T, :, g * 512:(g + 1) * 512], in_=stg[:])], reads=[Bstg])

            pendA = stageA(0)
            for T in range(8):
                ctxB = stageB1(T, *pendA)
                pendA = stageA(T + 1) if T + 1 < 8 else None
                stageB2(T, *ctxB)

    pop_scope()
    pop_scope()
    def build_hT_own():
        hT = sb("hT_own", [128, 16, OWN, 132], BF16); B = Buf()
        op(DVE, lambda e: e.memset(hT[:, :, :, 0:1], 0.0), writes=[B])
        push_scope()
        alloc_norm(g1)
        for m in range(OWN):
            hb, Bh, _, _ = norm_rows(x_own[m], 128)
            transpose_rows(hb, Bh, 128, lambda kc0, n, m=m: hT[:, kc0:kc0 + n, m, 4:132], B)
        hb, Bh, _, _ = norm_rows(x_halo, OWN * HALO)
        for half in range(2):
            pt, Bp = next_pt()
            pe_group([lambda e, q8=q8: e.transpose(out=pt[:, q8 * 128:q8 * 128 + 24], in_=hb[0:24, (half * 8 + q8) * 128:(half * 8 + q8 + 1) * 128],
                                                   identity=ident_b[0:24, 0:24]) for q8 in range(8)], reads=[Bh, B_ident_b], writes=[Bp])
            for m in range(OWN):
                src = pt[:].rearrange("p (a b) -> p a b", a=8)[:, :, m * 3:m * 3 + 3]
                op(DVE, lambda e: e.tensor_copy(out=hT[:, half * 8:half * 8 + 8, m, 1:4], in_=src), reads=[Bp], writes=[B])
        pop_scope()
        return hT, B

    push_scope()
    qT = sb("qT", [128, 16, OWN * 128], BF16); B_qT = Buf()
    qiT = sb("qiT", [128, 8, OWN * 128], BF16); B_qiT = Buf()
    wtok = sb("wtok", [128, OWN, 16]); B_wtok = Buf()
    push_scope()
    hT_own, B_hTo = build_hT_own()
    alloc_ws(2)
    sq_r = Rot("sq", [128, 512], BF16, 1)
    rs_r = Rot("rs", [128, 512], F32, 1)

    def own_rhs(kc, half):
        return hT_own[:, kc, 4 * half:4 * half + 4, 4:132]

    for hq in range(4):
        wq, Bwq = load_w(w_in, [(O_Q + hq * 512, 512)])
        for hh in range(4):
            h = hq * 4 + hh
            for half in range(2):
                ps, Bp = next_ps()
                pe_group([lambda e, kc=kc: e.matmul(ps[:], wq[:, kc, hh * 128:(hh + 1) * 128], own_rhs(kc, half),
                                                    start=(kc == 0), stop=(kc == 15)) for kc in range(16)],
                         reads=[Bwq, B_hTo], writes=[Bp])
                headnorm_T(ps, Bp, gq_t[:, 0:1], B_gqk, qT[:, h, half * 512:(half + 1) * 512], B_qT, 128.0 ** -0.5)
    for c2 in range(2):
        wqi, Bwqi = load_w(w_in, [(O_QI + c2 * 512, 512)])
        for cc in range(4):
            for half in range(2):
                ps, Bp = next_ps()
                pe_group([lambda e, kc=kc: e.matmul(ps[:], wqi[:, kc, cc * 128:(cc + 1) * 128], own_rhs(kc, half),
                                                    start=(kc == 0), stop=(kc == 15)) for kc in range(16)],
                         reads=[Bwqi, B_hTo], writes=[Bp])
                op(ACT, lambda e: e.activation(out=qiT[:, c2 * 4 + cc, half * 512:(half + 1) * 512], in_=ps[:], func=AF.Copy, scale=0.125),
                   reads=[Bp], writes=[B_qiT])
    ww, Bww = load_w(w_in, [(O_WI, 16)])
    for m in range(OWN):
        ps, Bp = next_ps()
        pe_group([lambda e, kc=kc: e.matmul(ps[:, 0:16], hT_own[:, kc, m, 4:132], ww[:, kc, 0:16],
                                            start=(kc == 0), stop=(kc == 15)) for kc in range(16)],
                 reads=[Bww, B_hTo], writes=[Bp])
        op(DVE, lambda e: e.tensor_scalar(out=wtok[:, m, :], in0=ps[:, 0:16], scalar1=0.25, scalar2=None, op0=ALU.mult),
           reads=[Bp], writes=[B_wtok])
    if DEBUG:
        dma(SP, [lambda e: e.dma_start(out=dbg["qT"], in_=qT[:])], reads=[B_qT])
        dma(SP, [lambda e: e.dma_start(out=dbg["qiT"], in_=qiT[:])], reads=[B_qiT])
        dma(SP, [lambda e: e.dma_start(out=dbg["wtok"], in_=wtok[:])], reads=[B_wtok])
    pop_scope()

    kT = sb("kT", [128, 4, SEQ], BF16); B_kT = Buf()
    Vs = sb("Vs", [128, NB, 512], BF16); B_V = Buf()
    kiT = sb("kiT", [128, SEQ], BF16); B_kiT = Buf()
    dma(SP, [lambda e: e.dma_start(out=kT[:], in_=kT_d)], writes=[B_kT])
    dma(SP, [lambda e: e.dma_start(out=Vs[:], in_=v_d)], writes=[B_V])
    dma(SP, [lambda e: e.dma_start(out=kiT[:], in_=kiT_d)], writes=[B_kiT])
    EB = sb("EB", [128, 5, 16, 128], BF16); B_EB = Buf()
    push_scope()
    relb_t = sb("relb_t", [32, 16]); ohb_t = sb("ohb_t", [32, 1280]); B_rb = Buf()
    rbh = sb("rbh", [32, 2, 16], BF16); ohb_b = sb("ohb_b", [32, 1280], BF16)
    Fv = sb("Fv", [16, 1280]); B_Fv = Buf()
    EBf = sb("EBf", [128, 16, 128]); B_EBf = Buf()
    dma(SP, [lambda e: e.dma_start(out=relb_t[:], in_=relb), lambda e: e.dma_start(out=ohb_t[:], in_=c_ohb)], writes=[B_rb])
    op(DVE, lambda e: e.tensor_copy(out=rbh[:, 0, :], in_=relb_t[:]), reads=[B_rb], writes=[B_rb])
    op(DVE, lambda e: e.tensor_tensor(out=rbh[:, 1, :], in0=relb_t[:], in1=rbh[:, 0, :], op=ALU.subtract), reads=[B_rb], writes=[B_rb])
    op(DVE, lambda e: e.tensor_copy(out=ohb_b[:], in_=ohb_t[:]), reads=[B_rb], writes=[B_rb])
    for c3 in range(3 if STAGE >= 4 else 0):
        n0, n1 = c3 * 512, min(1280, c3 * 512 + 512)
        ps, Bp = next_ps()
        pe_group([lambda e: e.matmul(ps[0:16, 0:n1 - n0], rbh[:, 0, :], ohb_b[:, n0:n1], start=True, stop=False),
                  lambda e: e.matmul(ps[0:16, 0:n1 - n0], rbh[:, 1, :], ohb_b[:, n0:n1], start=False, stop=True)],
                 reads=[B_rb], writes=[Bp])
        op(ACT, lambda e: e.activation(out=Fv[:, n0:n1], in_=ps[0:16, 0:n1 - n0], func=AF.Exp), reads=[Bp], writes=[B_Fv])
    for kb in range(5 if STAGE >= 4 else 0):
        for r0 in range(0, 128, 32):
            src = Fv[:, kb * 256:(kb + 1) * 256].unsqueeze(1).broadcast_to([16, 32, 256])
            dma(SP, [lambda e: e.dma_start(out=ebz_d[:, kb, r0:r0 + 32, :], in_=src)], reads=[B_Fv], writes=[B_EBf])
    barrier()
    for kb in range(5 if STAGE >= 4 else 0):
        srcs = []
        for h in range(16):
            base = ebz_d[h, kb]
            srcs.append(bass.AP(tensor=base.tensor, offset=base.offset + 127, ap=[[255, 128], [1, 128]]))
        dma(SP, [lambda e, h=h: e.dma_start(out=EBf[:, h, :], in_=srcs[h]) for h in range(16)], reads=[B_EBf], writes=[B_EBf])
        op(DVE, lambda e: e.tensor_copy(out=EB[:, kb, :, :], in_=EBf[:]), reads=[B_EBf], writes=[B_EB])
    pop_scope()

    score = sb("score", [128, SEQ]); B_score = Buf()
    sel01 = sb("sel01", [128, SEQ], BF16); B_sel = Buf()
    selT = sb("selT", [128, NB, 128], BF16); B_selT = Buf()
    Dg = sb("Dg", [128, 16, 128], BF16); B_Dg = Buf()
    amask_t = sb("amask_t", [128, 512]); B_am = Buf()
    bs = sb("bs", [128, 16]); B_bs = Buf()
    half_c = sb("half_c", [128, 1]); B_hc = Buf()
    R_r = Rot("Rr", [128, 512], BF16, 3)
    E_r = Rot("Er", [128, 512], BF16, 2)
    P_r = Rot("Pr", [128, 512], BF16, 3)
    rec_r = Rot("rec", [128, 512], F32, 1)
    ao_r = Rot("ao", [128, 4, 128], BF16, 2)
    dma(SP, [lambda e: e.dma_start(out=amask_t[:], in_=c_amask)], writes=[B_am])
    p2t = sb("p2t", [128, 32]); B_p2 = Buf()
    dk = sb("dk", [128, 32]); B_dk = Buf()
    cntt = sb("cntt", [128, 32]); B_cnt = Buf()
    for kk_ in range(32):
        op(DVE, lambda e, kk_=kk_: e.memset(p2t[:, kk_:kk_ + 1], 2.0 ** -(kk_ + 1)), writes=[B_p2])
    op(DVE, lambda e: e.memset(half_c[:], 0.5), writes=[B_hc])
    psn[0] = 4
    NIT = 24

    def sc_init(m):
        nkb = 4 * (m + 1)
        Lk = nkb * 128
        for h in range(16):
            op(DVE, lambda e, h=h: e.tensor_scalar(out=Dg[:, h, :], in0=ident_f[:], scalar1=wtok[:, m, h:h + 1], scalar2=None, op0=ALU.mult),
               reads=[B_ident_f, B_wtok], writes=[B_Dg])
        for kt in range(m + 1):
            scp, Bscp = PS[4 + kt % 2], BPS[4 + kt % 2]

            def sc_front(h):
                po = (h % 2) * 64
                ps, Bp = next_ps()
                pe_group([lambda e: e.matmul(ps[:], qiT[po:po + 64, h // 2, m * 128:(m + 1) * 128], kiT[po:po + 64, kt * 512:(kt + 1) * 512],
                                             start=True, stop=True)], reads=[B_qiT, B_kiT], writes=[Bp])
                R, BR = R_r.next()
                op(ACT, lambda e: e.activation(out=R[:], in_=ps[:], func=AF.Relu), reads=[Bp], writes=[BR])
                return R, BR

            pend = {h: sc_front(h) for h in range(2)}
            for h in range(16):
                R, BR = pend.pop(h)
                if h + 2 < 16:
                    pend[h + 2] = sc_front(h + 2)
                pe_group([lambda e: e.matmul(scp[:], Dg[:, h, :], R[:], start=(h == 0), stop=(h == 15))], reads=[B_Dg, BR], writes=[Bscp])
            if kt < m:
                op(ACT, lambda e: e.copy(out=score[:, kt * 512:(kt + 1) * 512], in_=scp[:]), reads=[Bscp], writes=[B_score])
            else:
                op(DVE, lambda e: e.tensor_tensor(out=score[:, kt * 512:(kt + 1) * 512], in0=scp[:], in1=amask_t[:], op=ALU.add),
                   reads=[Bscp, B_am], writes=[B_score])
                jk, Bjk = rec_r.next()
                op(DVE, lambda e: e.tensor_tensor(out=jk[:], in0=scp[:], in1=amask_t[:], op=ALU.subtract), reads=[Bscp, B_am], writes=[Bjk])
                op(DVE, lambda e: e.tensor_reduce(out=bs[:, 0:1], in_=jk[:], axis=AX.X, op=ALU.min), reads=[Bjk], writes=[B_bs])
        op(DVE, lambda e: e.tensor_reduce(out=bs[:, 1:2], in_=score[:, 0:Lk], axis=AX.X, op=ALU.max), reads=[B_score], writes=[B_bs])
        if m > 0:
            op(DVE, lambda e: e.tensor_reduce(out=bs[:, 2:3], in_=score[:, 0:Lk - 512], axis=AX.X, op=ALU.min), reads=[B_score], writes=[B_bs])
            op(DVE, lambda e: e.tensor_tensor(out=bs[:, 0:1], in0=bs[:, 0:1], in1=bs[:, 2:3], op=ALU.min), reads=[B_bs], writes=[B_bs])
        op(DVE, lambda e: e.tensor_scalar(out=bs[:, 3:4], in0=bs[:, 0:1], scalar1=-1.0, scalar2=None, op0=ALU.add), reads=[B_bs], writes=[B_bs])
        op(DVE, lambda e: e.scalar_tensor_tensor(out=bs[:, 4:5], in0=bs[:, 1:2], scalar=1.0, in1=bs[:, 3:4], op0=ALU.add, op1=ALU.subtract),
           reads=[B_bs], writes=[B_bs])
        op(DVE, lambda e: e.tensor_scalar(out=dk[:, 0:NIT + 1], in0=p2t[:, 0:NIT + 1], scalar1=bs[:, 4:5], scalar2=None, op0=ALU.mult),
           reads=[B_bs, B_p2], writes=[B_dk])
        op(DVE, lambda e: e.memset(cntt[:], 0.0), writes=[B_cnt])
        op(DVE, lambda e: e.tensor_tensor(out=bs[:, 6:7], in0=bs[:, 3:4], in1=dk[:, 0:1], op=ALU.add), reads=[B_bs, B_dk], writes=[B_bs])
        jkb, Bjkb = sel01, B_sel

    def bis_iter(m, it):
        Lk = 512 * (m + 1)
        jkb, Bjkb = sel01, B_sel
        op(DVE, lambda e: e.tensor_scalar(out=jkb[:, 0:Lk], in0=score[:, 0:Lk], scalar1=bs[:, 6:7], scalar2=0.0,
                                          op0=ALU.is_ge, op1=ALU.add, accum_out=cntt[:, it:it + 1]),
           reads=[B_score, B_bs], writes=[Bjkb, B_cnt])
        op(DVE, lambda e: e.scalar_tensor_tensor(out=bs[:, 7:8], in0=cntt[:, it:it + 1], scalar=255.5, in1=dk[:, it:it + 1],
                                                 op0=ALU.is_ge, op1=ALU.mult), reads=[B_cnt, B_dk], writes=[B_bs])
        if it < NIT - 1:
            op(DVE, lambda e: e.scalar_tensor_tensor(out=bs[:, 6:7], in0=bs[:, 3:4], scalar=bs[:, 7:8], in1=dk[:, it + 1:it + 2],
                                                     op0=ALU.add, op1=ALU.add), reads=[B_bs, B_dk], writes=[B_bs])
        op(DVE, lambda e: e.tensor_tensor(out=bs[:, 3:4], in0=bs[:, 3:4], in1=bs[:, 7:8], op=ALU.add), reads=[B_bs], writes=[B_bs])

    def bis_fin(m):
        nkb = 4 * (m + 1)
        Lk = nkb * 128
        op(DVE, lambda e: e.tensor_scalar(out=sel01[:, 0:Lk], in0=score[:, 0:Lk], scalar1=bs[:, 3:4], scalar2=None, op0=ALU.is_ge),
           reads=[B_score, B_bs], writes=[B_sel])
        if DEBUG:
            dma(SP, [lambda e: e.dma_start(out=dbg["score"][m, :, 0:Lk], in_=score[:, 0:Lk])], reads=[B_score])
            dma(SP, [lambda e: e.dma_start(out=dbg["thr"][:, m, :], in_=bs[:, 3:7])], reads=[B_bs])
        for k8 in range(0, nkb, 8):
            nn = min(8, nkb - k8)
            pt, Bp = next_pt()
            pe_group([lambda e, q=q: e.transpose(out=pt[:, q * 128:(q + 1) * 128], in_=sel01[:, (k8 + q) * 128:(k8 + q + 1) * 128],
                                                 identity=ident_b[:]) for q in range(nn)], reads=[B_sel, B_ident_b], writes=[Bp])
            op(ACT, lambda e: e.copy(out=selT[:, k8:k8 + nn, :], in_=pt[:, 0:nn * 128].rearrange("p (a b) -> p a b", b=128)),
               reads=[Bp], writes=[B_selT])

    def attention(m, hook):
        nkb = 4 * (m + 1)
        for g4 in range(4 if STAGE >= 6 else 0):
            op_ps, Bop = PS[4], BPS[4]
            sm_ps, Bsm = PS[5], BPS[5]
            def att_front(kb):
                ps, Bp = next_ps()
                pe_group([lambda e: e.matmul(ps[:], kT[:, g4, kb * 128:(kb + 1) * 128], qT[:, 4 * g4:4 * g4 + 4, m * 128:(m + 1) * 128],
                                             start=True, stop=True)], reads=[B_kT, B_qT], writes=[Bp])
                E, BE = E_r.next()
                op(ACT, lambda e: e.activation(out=E[:], in_=ps[:], func=AF.Exp), reads=[Bp], writes=[BE])
                P, BP = P_r.next()
                selb = selT[:, kb, :].unsqueeze(1).broadcast_to([128, 4, 128])
                op(DVE, lambda e: e.tensor_tensor(out=P[:].rearrange("p (r t) -> p r t", r=4), in0=E[:].rearrange("p (r t) -> p r t", r=4),
                                                  in1=selb, op=ALU.mult), reads=[BE, B_selT], writes=[BP])
                kbrel = kb - (nkb - 5)
                if kbrel >= 0:
                    op(DVE, lambda e: e.tensor_tensor(out=P[:].rearrange("p (r t) -> p r t", r=4), in0=P[:].rearrange("p (r t) -> p r t", r=4),
                                                      in1=EB[:, kbrel, 4 * g4:4 * g4 + 4, :], op=ALU.mult), reads=[BP, B_EB], writes=[BP])
                return P, BP

            pend = {kb: att_front(kb) for kb in range(min(2, nkb))}
            for kb in range(nkb):
                P, BP = pend.pop(kb)
                if kb + 2 < nkb:
                    pend[kb + 2] = att_front(kb + 2)
                pe_group([lambda e: e.matmul(op_ps[:], Vs[:, kb, g4 * 128:(g4 + 1) * 128], P[:], start=(kb == 0), stop=(kb == nkb - 1))],
                         reads=[B_V, BP], writes=[Bop])
                pe_group([lambda e: e.matmul(sm_ps[:], ones_b[:], P[:], start=(kb == 0), stop=(kb == nkb - 1))],
                         reads=[B_ones_b, BP], writes=[Bsm])
                hook()
            if STAGE < 8:
                continue
            rec, Brec = rec_r.next()
            op(ACT, lambda e: e.copy(out=rec[:], in_=sm_ps[:]), reads=[Bsm], writes=[Brec])
            op(DVE, lambda e: e.reciprocal(out=rec[:], in_=rec[:]), reads=[Brec], writes=[Brec])
            ao, Bao = ao_r.next()
            op(DVE, lambda e: e.tensor_tensor(out=ao[:].rearrange("p r t -> p (r t)"), in0=op_ps[:], in1=rec[:], op=ALU.mult),
               reads=[Bop, Brec], writes=[Bao])
            dma(SP, [lambda e: e.dma_start(out=attT_d[:, 4 * g4:4 * g4 + 4, m * 128:(m + 1) * 128], in_=ao[:])], reads=[Bao])

    if STAGE >= 5:
        sc_init(0)
        for it in range(NIT):
            bis_iter(0, it)
        bis_fin(0)
        for m in range(OWN):
            todo = []
            if m + 1 < OWN:
                sc_init(m + 1)
                todo = list(range(NIT))

            def hook():
                if todo:
                    bis_iter(m + 1, todo.pop(0))

            attention(m, hook)
            while todo:
                bis_iter(m + 1, todo.pop(0))
            if m + 1 < OWN:
                bis_fin(m + 1)
    psn[0] = 6
    pop_scope()

    push_scope()
    ynT_all = sb("ynT_all", [128, 32, OWN * 128], BF16); B_ynT = Buf()
    hT_own, B_hTo = build_hT_own()
    push_scope()
    alloc_ws(3)
    diag = sb("diag6", [128, 6, 4, 128], BF16); B_diag = Buf()
    negmT = sb("negmT", [128, 128]); B_negm = Buf()
    op(DVE, lambda e: e.tensor_scalar(out=negmT[:], in0=tri_f[:], scalar1=-1.0, scalar2=-NEG, op0=ALU.add, op1=ALU.mult),
       reads=[B_tri], writes=[B_negm])
    gsg = sb("gsg", [128, 512]); B_gsg = Buf()
    pre6 = sb("pre6", [128, 6, 2, 132], BF16); B_pre6 = Buf()
    xc6_r = Rot("xc6", [128, 6, 128], BF16, 2)
    zs_r = Rot("zs", [128, 512], F32, 2)
    dtw_r = Rot("dtwb", [128, 8, 32], F32, 2)
    ahl_r = Rot("ahlb", [128, 2, 64], BF16, 1)
    xd = sb("xd", [128, 512], BF16); B_xd = Buf()
    xdsk = sb("xdsk", [128, 512]); B_xdsk = Buf()
    Sg = sb("Sg", [128, 512], BF16); B_Sg = Buf()
    cbm = sb("cbm", [128, 128]); B_cbm = Buf()
    Rm = sb("Rm", [128, 8, 128]); B_Rm = Buf()
    Rhl = sb("Rhl", [128, 2, 8, 128], BF16); B_Rhl = Buf()
    seg = sb("seg", [128, 8, 128]); B_seg = Buf()
    eab = sb("eab", [128, 8, 128]); B_eab = Buf()
    Mt = sb("Mt", [128, 8, 128], BF16); B_Mt = Buf()
    CE = sb("CE", [128, 8, 128], BF16); B_CE = Buf()
    y3 = sb("y3", [128, 512]); B_y3 = Buf()
    ynb = sb("ynb", [128, 512], BF16); B_ynb = Buf()
    nst = sb("nst", [128, 16]); B_nst = Buf()
    dtown = sb("dtown", [128, OWN, 64]); B_dtown = Buf()
    wdt, Bwdt = load_w(w_in, [(O_DT, 64)])
    for m in range(OWN):
        ps, Bp = next_ps()
        pe_group([lambda e, kc=kc: e.matmul(ps[:, 0:64], hT_own[:, kc, m, 4:132], wdt[:, kc, 0:64],
                                            start=(kc == 0), stop=(kc == 15)) for kc in range(16)],
                 reads=[Bwdt, B_hTo], writes=[Bp])
        op(DVE, lambda e: e.tensor_copy(out=dtown[:, m, :], in_=ps[:, 0:64]), reads=[Bp], writes=[B_dtown])
    for g in range(8 if STAGE >= 9 else 0):
        wz, Bwz = load_w(w_in, [(O_Z + g * 512, 512)])
        wx, Bwx = load_w(w_in, [(O_X + g * 512, 512)])
        wbc, Bwbc = load_w(w_in, [(O_B + g * 128, 128), (O_C + g * 128, 128)])
        chunks = [4 * g + i for i in range(4)] + [32 + g, 40 + g]
        build_diag(chunks)
        dma(SP, [lambda e: e.dma_start(out=gsg[:], in_=gs[:, g * 512:(g + 1) * 512].partition_broadcast(128))], writes=[B_gsg])
        def stageA2(m):
                xc6, B_xc6 = xc6_r.next()
                zs, B_zs = zs_r.next()
                for i in range(6):
                    ps, Bp = next_ps()
                    wt, Bwt, c0 = (wx, Bwx, i * 128) if i < 4 else (wbc, Bwbc, (i - 4) * 128)
                    pe_group([lambda e, kc=kc: e.matmul(ps[:, 0:132], wt[:, kc, c0:c0 + 128], hT_own[:, kc, m, 0:132],
                                                        start=(kc == 0), stop=(kc == 15)) for kc in range(16)],
                             reads=[Bwt, B_hTo], writes=[Bp])
                    op(ACT, lambda e: e.copy(out=pre6[:, i, 0, 0:132], in_=ps[:, 0:132]), reads=[Bp], writes=[B_pre6])
                    op(DVE, lambda e: e.tensor_copy(out=pre6[:, i, 1, 0:131], in_=ps[:, 1:132]), reads=[Bp], writes=[B_pre6])
                for i in range(6):
                    conv_chunk(pre6, B_pre6, i, chunks[i], 128, xc6[:, i, :], B_xc6)
                if SUB < 11:
                    return
                ps, Bp = next_ps()
                pe_group([lambda e, kc=kc: e.matmul(ps[:], hT_own[:, kc, m, 4:132], wz[:, kc, 0:512],
                                                    start=(kc == 0), stop=(kc == 15)) for kc in range(16)],
                         reads=[Bwz, B_hTo], writes=[Bp])
                op(ACT, lambda e: e.activation(out=zs[:], in_=ps[:], func=AF.Silu), reads=[Bp], writes=[B_zs])
                dtw, Bdtw = dtw_r.next()
                dt_math(dtown[:, m:m + 1, g * 8:(g + 1) * 8], B_dtown, g, 1, dtw, Bdtw)
                return xc6, B_xc6, zs, B_zs, dtw, Bdtw

        def stageB2(m, xc6, B_xc6, zs, B_zs, dtw, Bdtw):
                if SUB < 12:
                    return
                pt, Bpt = next_pt()
                pe_group([lambda e, i=i: e.transpose(out=pt[:, i * 128:(i + 1) * 128], in_=xc6[:, i, :], identity=ident_b[:]) for i in range(4)],
                         reads=[B_xc6, B_ident_b], writes=[Bpt])
                dt3 = dtw[:, 0, 0:8].unsqueeze(2).broadcast_to([128, 8, 64])
                dk3 = dsk_bc[:, g * 8:(g + 1) * 8].unsqueeze(2).broadcast_to([128, 8, 64])
                pt3 = pt[:, 0:512].rearrange("p (h d) -> p h d", h=8)
                op(DVE, lambda e: e.tensor_tensor(out=xd[:].rearrange("p (h d) -> p h d", h=8), in0=pt3, in1=dt3, op=ALU.mult),
                   reads=[Bpt, Bdtw], writes=[B_xd])
                if SUB != 132:
                    op(DVE, lambda e: e.tensor_tensor(out=xdsk[:].rearrange("p (h d) -> p h d", h=8), in0=pt3, in1=dk3, op=ALU.mult),
                       reads=[Bpt, B_ssmc], writes=[B_xdsk])
                if SUB != 131:
                    dma(SP, [lambda e: e.dma_start(out=Sg[:], in_=ssave_d[m, :, g * 512:(g + 1) * 512])], writes=[B_Sg])
                if SUB < 13 or SUB in (131, 132):
                    return
                ps, Bp = next_ps()
                pe_group([lambda e: e.matmul(ps[:, 0:128], xc6[:, 4, :], xc6[:, 5, :], start=True, stop=True)], reads=[B_xc6], writes=[Bp])
                op(DVE, lambda e: e.tensor_tensor(out=cbm[:], in0=ps[:, 0:128], in1=tri_f[:], op=ALU.mult), reads=[Bp, B_tri], writes=[B_cbm])
                if SUB == 133:
                    return
                id3 = ident_f[:].unsqueeze(1).broadcast_to([128, 8, 128])
                ac3 = dtw[:, 2, 0:8].unsqueeze(2).broadcast_to([128, 8, 128])
                for h in range(8):
                    op(DVE, lambda e, h=h: e.tensor_scalar(out=Rm[:, h, :], in0=ident_f[:], scalar1=dtw[:, 2, h:h + 1], scalar2=None, op0=ALU.mult),
                       reads=[B_ident_f, Bdtw], writes=[B_Rm])
                op(DVE, lambda e: e.tensor_copy(out=Rhl[:, 0], in_=Rm[:]), reads=[B_Rm], writes=[B_Rhl])
                op(DVE, lambda e: e.tensor_tensor(out=Rhl[:, 1], in0=Rm[:], in1=Rhl[:, 0], op=ALU.subtract), reads=[B_Rm, B_Rhl], writes=[B_Rhl])
                if SUB == 134:
                    return
                abc = []
                for hb2 in range(2):
                    pa, Bpa = next_ps()
                    pe_group([lambda e: e.matmul(pa[:], ones_b[:], Rhl[:, 0, 4 * hb2:4 * hb2 + 4, :], start=True, stop=False),
                              lambda e: e.matmul(pa[:], ones_b[:], Rhl[:, 1, 4 * hb2:4 * hb2 + 4, :], start=False, stop=True)],
                             reads=[B_Rhl, B_ones_b], writes=[Bpa])
                    abc.append((pa, Bpa))
                    pa3 = pa[:].rearrange("p (h l) -> p h l", h=4)
                    nm3 = negmT[:].unsqueeze(1).broadcast_to([128, 4, 128])
                    op(DVE, lambda e: e.tensor_tensor(out=seg[:, 4 * hb2:4 * hb2 + 4, :], in0=pa3, in1=nm3, op=ALU.add),
                       reads=[Bpa, B_negm], writes=[B_seg])
                    if SUB != 135:
                        op(ACT, lambda e: e.activation(out=eab[:, 4 * hb2:4 * hb2 + 4, :], in_=pa3, func=AF.Exp), reads=[Bpa], writes=[B_eab])
                if SUB < 14 or SUB in (133, 134, 135):
                    return
                op(DVE, lambda e: e.tensor_scalar(out=nst[:, 0:8], in0=dtw[:, 2, 0:8], scalar1=-1.0, scalar2=None, op0=ALU.mult),
                   reads=[Bdtw], writes=[B_nst])
                for h in range(8):
                    op(ACT, lambda e, h=h: e.activation(out=seg[:, h, :], in_=seg[:, h, :], func=AF.Exp, bias=nst[:, h:h + 1]),
                       reads=[B_seg, B_nst], writes=[B_seg])
                cb3 = cbm[:].unsqueeze(1).broadcast_to([128, 8, 128])
                op(DVE, lambda e: e.tensor_tensor(out=Mt[:], in0=seg[:], in1=cb3, op=ALU.mult), reads=[B_seg, B_cbm], writes=[B_Mt])
                c3_ = xc6[:, 5, :].unsqueeze(1).broadcast_to([128, 8, 128])
                op(DVE, lambda e: e.tensor_tensor(out=CE[:], in0=eab[:], in1=c3_, op=ALU.mult), reads=[B_eab, B_xc6], writes=[B_CE])
                if SUB < 15:
                    return
                psy, Bpy = next_ps()
                fns = []
                for h in range(8):
                    fns.append(lambda e, h=h: e.matmul(psy[:, h * 64:(h + 1) * 64], Mt[:, h, :], xd[:, h * 64:(h + 1) * 64], start=True, stop=False))
                    fns.append(lambda e, h=h: e.matmul(psy[:, h * 64:(h + 1) * 64], CE[:, h, :], Sg[:, h * 64:(h + 1) * 64], start=False, stop=True))
                pe_group(fns, reads=[B_Mt, B_CE, B_xd, B_Sg], writes=[Bpy])
                op(DVE, lambda e: e.tensor_tensor(out=y3[:], in0=psy[:], in1=xdsk[:], op=ALU.add), reads=[Bpy, B_xdsk], writes=[B_y3])
                op(DVE, lambda e: e.tensor_tensor(out=y3[:], in0=y3[:], in1=zs[:], op=ALU.mult), reads=[B_y3, B_zs], writes=[B_y3])
                if SUB < 16:
                    return
                op(DVE, lambda e: e.memset(nst[:, 8:9], 0.0), writes=[B_nst])
                op(ACT, lambda e: e.activation(out=ynb[:], in_=y3[:], func=AF.Square, accum_out=nst[:, 8:9]), reads=[B_y3, B_nst], writes=[B_ynb, B_nst])
                op(DVE, lambda e: e.tensor_scalar(out=nst[:, 9:10], in0=nst[:, 8:9], scalar1=1.0 / 512, scalar2=EPS, op0=ALU.mult, op1=ALU.add),
                   reads=[B_nst], writes=[B_nst])
                op(ACT, lambda e: e.activation(out=nst[:, 10:11], in_=nst[:, 9:10], func=AF.Sqrt), reads=[B_nst], writes=[B_nst])
                op(DVE, lambda e: e.reciprocal(out=nst[:, 11:12], in_=nst[:, 10:11]), reads=[B_nst], writes=[B_nst])
                op(DVE, lambda e: e.scalar_tensor_tensor(out=ynb[:], in0=y3[:], scalar=nst[:, 11:12], in1=gsg[:], op0=ALU.mult, op1=ALU.mult),
                   reads=[B_y3, B_nst, B_gsg], writes=[B_ynb])
                if SUB < 17:
                    return
                pt, Bpt = next_pt()
                pe_group([lambda e, i=i: e.transpose(out=pt[:, i * 128:(i + 1) * 128], in_=ynb[:, i * 128:(i + 1) * 128], identity=ident_b[:])
                          for i in range(4)], reads=[B_ynb, B_ident_b], writes=[Bpt])
                op(ACT, lambda e: e.copy(out=ynT_all[:, 4 * g:4 * g + 4, m * 128:(m + 1) * 128],
                                         in_=pt[:, 0:512].rearrange("p (a b) -> p a b", a=4)), reads=[Bpt], writes=[B_ynT])


        pendA2 = stageA2(0)
        for m in range(OWN):
            nxtA2 = stageA2(m + 1) if m + 1 < OWN else None
            stageB2(m, *pendA2)
            pendA2 = nxtA2
    if DEBUG:
        dma(SP, [lambda e: e.dma_start(out=dbg["ynT"], in_=ynT_all[:])], reads=[B_ynT])
    pop_scope()
    mergedT = sb("mergedT", [128, 16, OWN * 128], BF16); B_mT = Buf()
    sg_r = Rot("sg", [128, 512], F32, 2)
    tmpm_r = Rot("tmpm", [128, 512], F32, 2)
    push_scope()
    attT = sb("attT", [128, 16, OWN * 128], BF16); B_attT = Buf()
    dma(SP, [lambda e: e.dma_start(out=attT[:], in_=attT_d)], writes=[B_attT])
    alloc_ws(2)
    for cg in range(4 if STAGE >= 10 else 0):
        wab, Bwab = load_w(w_ab, [(cg * 512, 512)])
        wga, Bwga = load_w(w_in, [(O_GA + cg * 512, 512)])
        for cc in range(4):
            ct = cg * 4 + cc
            for half in range(2):
                pg, Bpg = next_ps()
                pe_group([lambda e, kc=kc: e.matmul(pg[:], wga[:, kc, cc * 128:(cc + 1) * 128], own_rhs(kc, half),
                                                    start=(kc == 0), stop=(kc == 15)) for kc in range(16)],
                         reads=[Bwga, B_hTo], writes=[Bpg])
                sg, Bsg = sg_r.next()
                op(ACT, lambda e: e.activation(out=sg[:], in_=pg[:], func=AF.Sigmoid), reads=[Bpg], writes=[Bsg])
                pa, Bpa = next_ps()
                pe_group([lambda e, kc=kc: e.matmul(pa[:], wab[:, kc, cc * 128:(cc + 1) * 128], attT[:, kc, half * 512:(half + 1) * 512],
                                                    start=(kc == 0), stop=(kc == 15)) for kc in range(16)],
                         reads=[Bwab, B_attT], writes=[Bpa])
                op(DVE, lambda e: e.tensor_tensor(out=mergedT[:, ct, half * 512:(half + 1) * 512], in0=pa[:], in1=sg[:], op=ALU.mult),
                   reads=[Bpa, Bsg], writes=[B_mT])
    pop_scope()
    push_scope()
    alloc_ws(3)
    for cg in range(4 if STAGE >= 10 else 0):
        wsb0, Bwsb0 = load_w(w_sb, [(cg * 512, 512)], row0=0)
        wsb1, Bwsb1 = load_w(w_sb, [(cg * 512, 512)], row0=2048)
        wgs, Bwgs = load_w(w_in, [(O_GS + cg * 512, 512)])
        for cc in range(4):
            ct = cg * 4 + cc
            for half in range(2):
                pg, Bpg = next_ps()
                pe_group([lambda e, kc=kc: e.matmul(pg[:], wgs[:, kc, cc * 128:(cc + 1) * 128], own_rhs(kc, half),
                                                    start=(kc == 0), stop=(kc == 15)) for kc in range(16)],
                         reads=[Bwgs, B_hTo], writes=[Bpg])
                sg, Bsg = sg_r.next()
                op(ACT, lambda e: e.activation(out=sg[:], in_=pg[:], func=AF.Sigmoid), reads=[Bpg], writes=[Bsg])
                py, Bpy = next_ps()
                pe_group([lambda e, kc=kc: e.matmul(py[:], (wsb0 if kc < 16 else wsb1)[:, kc % 16, cc * 128:(cc + 1) * 128],
                                                    ynT_all[:, kc, half * 512:(half + 1) * 512],
                                                    start=(kc == 0), stop=(kc == 31)) for kc in range(32)],
                         reads=[Bwsb0, Bwsb1, B_ynT], writes=[Bpy])
                tm, Btm = tmpm_r.next()
                op(DVE, lambda e: e.tensor_tensor(out=tm[:], in0=py[:], in1=sg[:], op=ALU.mult), reads=[Bpy, Bsg], writes=[Btm])
                op(DVE, lambda e: e.tensor_tensor(out=mergedT[:, ct, half * 512:(half + 1) * 512], in0=tm[:],
                                                  in1=mergedT[:, ct, half * 512:(half + 1) * 512], op=ALU.add),
                   reads=[Btm, B_mT], writes=[B_mT])
    pop_scope()
    dma(SP, [lambda e: e.dma_start(out=mT_d, in_=mergedT[:])], reads=[B_mT])
    pop_scope()

    push_scope()
    x1acc = sb("x1acc", [128, OWN, D]); Bx1 = [Buf() for _ in range(OWN)]
    h2T = sb("h2T", [128, 16, OWN * 128], BF16); B_h2T = Buf()
    alloc_ws(3)
    for m in range(OWN):
        dma(SP, [lambda e: e.dma_start(out=x1acc[:, m, :], in_=x_own[m])], writes=[Bx1[m]])
    push_scope()
    mT = sb("mT", [128, 16, OWN * 128], BF16); B_mTl = Buf()
    dma(SP, [lambda e: e.dma_start(out=mT[:], in_=mT_d)], writes=[B_mTl])
    for ct in range(4 if STAGE >= 10 else 0):
        wo, Bwo = load_w(w_out, [(ct * 512, 512)])
        for blk in range(OWN):
            ps, Bp = next_ps()
            pe_group([lambda e, kc=kc: e.matmul(ps[:], mT[:, kc, blk * 128:(blk + 1) * 128], wo[:, kc, 0:512],
                                                start=(kc == 0), stop=(kc == 15)) for kc in range(16)],
                     reads=[Bwo, B_mTl], writes=[Bp])
            op(DVE, lambda e: e.tensor_tensor(out=x1acc[:, blk, ct * 512:(ct + 1) * 512], in0=x1acc[:, blk, ct * 512:(ct + 1) * 512],
                                              in1=ps[:], op=ALU.add), reads=[Bp, Bx1[blk]], writes=[Bx1[blk]])
    pop_scope()
    push_scope()
    alloc_norm(g2)
    for m in range(OWN):
        hb, Bh, _, _ = norm_rows(None, 128, src_sb=(x1acc[:, m, :], Bx1[m]))
        transpose_rows(hb, Bh, 128, lambda kc0, n, m=m: h2T[:, kc0:kc0 + n, m * 128:(m + 1) * 128], B_h2T)
    pop_scope()
    uT_r = Rot("uT", [128, 4, OWN * 128], BF16, 2)
    tmp_r = Rot("tmpf", [128, 512], F32, 2)
    for fg in range(16):
        wu, Bwu = load_w(w_up, [(fg * 512, 512)])
        uT, BuT = uT_r.next()
        for fc in range(4):
            for half in range(2):
                ps, Bp = next_ps()
                pe_group([lambda e, kc=kc: e.matmul(ps[:], wu[:, kc, fc * 128:(fc + 1) * 128], h2T[:, kc, half * 512:(half + 1) * 512],
                                                    start=(kc == 0), stop=(kc == 15)) for kc in range(16)],
                         reads=[Bwu, B_h2T], writes=[Bp])
                tmp, Btmp = tmp_r.next()
                op(ACT, lambda e: e.activation(out=tmp[:], in_=ps[:], func=AF.Relu), reads=[Bp], writes=[Btmp])
                op(DVE, lambda e: e.tensor_tensor(out=uT[:, fc, half * 512:(half + 1) * 512], in0=tmp[:], in1=tmp[:], op=ALU.mult),
                   reads=[Btmp], writes=[BuT])
        i = wsc[0] % len(WS)
        wsc[0] += 1
        wdt, Bwd = WS[i], BWS[i]
        wdv = wdt[:].rearrange("p a b -> p (a b)").rearrange("p (k c) -> p k c", k=4)
        svd = w_down[fg * 512:(fg + 1) * 512, :].rearrange("(kc p) c -> p kc c", p=128)
        dma(POOL, [lambda e: e.dma_start(out=wdv, in_=svd)], writes=[Bwd], nslots=4)
        for blk in range(OWN):
            for ct in range(4):
                ps, Bp = next_ps()
                pe_group([lambda e, fc=fc: e.matmul(ps[:], uT[:, fc, blk * 128:(blk + 1) * 128], wdv[:, fc, ct * 512:(ct + 1) * 512],
                                                    start=(fc == 0), stop=(fc == 3)) for fc in range(4)],
                         reads=[BuT, Bwd], writes=[Bp])
                op(DVE, lambda e: e.tensor_tensor(out=x1acc[:, blk, ct * 512:(ct + 1) * 512], in0=x1acc[:, blk, ct * 512:(ct + 1) * 512],
                                                  in1=ps[:], op=ALU.add), reads=[Bp, Bx1[blk]], writes=[Bx1[blk]])
    for m in range(OWN):
        dma(SP, [lambda e: e.dma_start(out=out_d[m], in_=x1acc[:, m, :])], reads=[Bx1[m]])
    pop_scope()
    for key, (slots, ctr) in dma_slots.items():
        for s in slots:
            if s.cnt > 0:
                SP.wait((s, s.cnt))
    for E in (PE, ACT, DVE):
        if E.p.cnt > 0:
            SP.wait((E.p, E.p.cnt))
    return nc


_CONST = {}


def _consts(j):
    if j in _CONST:
        return _CONST[j]
    ident = np.eye(128, dtype=np.float32)
    tri = np.triu(np.ones((128, 128), np.float32))
    flags = np.zeros((128, 4), np.float32); flags[:, j] = 1.0
    t = np.arange(128)
    am = np.zeros((128, 4, 128), np.float32)
    for r in range(4):
        if r == j:
            am[:, r, :] = np.where(t[None, :] <= t[:, None], 0.0, NEG)
        elif r > j:
            am[:, r, :] = NEG
    ohb = np.zeros((32, 5 * 256), np.float32)
    for kbrel in range(5):
        delta = j + 1 - kbrel
        for v in range(255):
            dd = 128 * delta + (v - 127)
            n = max(dd, 0)
            if n < 16:
                bkt = n
            else:
                bkt = min(31, 16 + int(np.float32(np.log(np.float32(max(n, 1)) / np.float32(16)) / np.float32(np.log(8.0)) * np.float32(16))))
            ohb[bkt, kbrel * 256 + v] += 1.0
            ohb[31, kbrel * 256 + v] -= 1.0
    _CONST[j] = dict(c_ident=ident, c_tri=tri, c_flags=flags, c_amask=am.reshape(128, 512), c_ohb=ohb)
    return _CONST[j]


def make_in_maps(inputs):
    x = np.ascontiguousarray(inputs["x"], dtype=np.float32)
    shared = dict(
        w_in=np.ascontiguousarray(inputs["w_in"][0]),
        w_ab=np.ascontiguousarray(inputs["w_att_branch"][0]),
        w_sb=np.ascontiguousarray(inputs["w_ssm_branch"][0]),
        w_out=np.ascontiguousarray(inputs["w_out"][0]),
        w_up=np.ascontiguousarray(inputs["w_up"][0]),
        w_down=np.ascontiguousarray(inputs["w_down"][0]),
        norm1_g=np.ascontiguousarray(inputs["norm1_g"].reshape(1, D)),
        norm2_g=np.ascontiguousarray(inputs["norm2_g"].reshape(1, D)),
        ssm_norm_g=np.ascontiguousarray(inputs["ssm_norm_g"].reshape(1, 4096)),
        q_norm_g=np.ascontiguousarray(inputs["q_norm_g"].reshape(128, 1)),
        k_norm_g=np.ascontiguousarray(inputs["k_norm_g"].reshape(128, 1)),
        conv_w=np.ascontiguousarray(inputs["conv_w"][0].T.reshape(48, 128, 4).transpose(1, 0, 2)),
        conv_b=np.ascontiguousarray(inputs["conv_b"][0].reshape(48, 128).T),
        dt_bias=np.ascontiguousarray(inputs["dt_bias"].reshape(1, 64)),
        a_log=np.ascontiguousarray(inputs["a_log"].reshape(1, 64)),
        d_skip=np.ascontiguousarray(inputs["d_skip"].reshape(1, 64)),
        rel_bias=np.ascontiguousarray(inputs["rel_bias"]),
    )
    maps = []
    for c in range(8):
        b, j = c // 4, c % 4
        xb = x[b].reshape(NB, 128, D)
        own = np.ascontiguousarray(xb[j::4])
        halo = np.zeros((OWN, HALO, D), np.float32)
        for m in range(OWN):
            t0 = (4 * m + j) * 128
            if t0 >= HALO:
                halo[m] = x[b, t0 - HALO:t0]
        m_ = dict(shared)
        m_.update(x_all=x[b], x_own=own, x_halo=halo.reshape(OWN * HALO, D))
        m_.update(_consts(j))
        maps.append(m_)
    return maps


_NC = None


def kernel(**inputs):
    global _NC
    if _NC is None:
        _NC = build_program()
    maps = make_in_maps(inputs)
    res = run_bass_kernel_spmd(_NC, maps, core_ids=list(range(8)))
    out = np.zeros((2, SEQ, D), np.float32)
    for c in range(8):
        b, j = c // 4, c % 4
        o = res.results[c]["out"]
        out[b].reshape(NB, 128, D)[j::4] = o
    return out
```

```python
from contextlib import ExitStack
import numpy as np
import concourse.bass as bass
import concourse.mybir as mybir
from concourse.bass_utils import run_bass_kernel_spmd

F32 = mybir.dt.float32
BF16 = mybir.dt.bfloat16
AF = mybir.ActivationFunctionType
ALU = mybir.AluOpType
AX = mybir.AxisListType

D = 2048
SEQ = 4096
NB = 32
OWN = 8
HALO = 3
BW = 128 + HALO
IN_DIM = 18576
O_GA, O_GS, O_Q, O_K, O_V, O_QI, O_KI, O_WI, O_Z, O_X, O_B, O_C, O_DT = (
    0, 2048, 4096, 6144, 6656, 7168, 8192, 8256, 8272, 12368, 16464, 17488, 18512)
EPS = 1e-6
NEG = -30000.0
DEBUG = False
STAGE = 99

SUB = 99


class Prod:
    def __init__(self, nc, name):
        self.sem = nc.alloc_semaphore(name)
        self.cnt = 0
        self.name = name


class Eng:
    def __init__(self, nc, e, name, selfsync=True):
        self.e = e
        self.p = Prod(nc, "s_" + name)
        self.seen = {}
        self.selfsync = selfsync
        self.name = name

    def wait(self, tok):
        if tok is None:
            return
        prod, val = tok
        if prod is self.p and not self.selfsync:
            return
        if self.seen.get(prod, 0) >= val:
            return
        self.seen[prod] = val
        self.e.wait_ge(prod.sem, val)


class Buf:
    def __init__(self, name="b"):
        self.name = name
        self.w = None
        self.rs = {}


def build_program():
    nc = bass.Bass("TRN2", target_bir_lowering=False)
    PE = Eng(nc, nc.tensor, "pe", selfsync=False)
    ACT = Eng(nc, nc.scalar, "act")
    DVE = Eng(nc, nc.vector, "dve")
    POOL = Eng(nc, nc.gpsimd, "pool")
    SP = Eng(nc, nc.sync, "sp")

    def deps(E, reads, writes):
        for b in reads:
            E.wait(b.w)
        for b in writes:
            E.wait(b.w)
            for t in list(b.rs.items()):
                E.wait(t)

    def commit(tok, reads, writes):
        prod, val = tok
        for b in reads:
            b.rs[prod] = val
        for b in writes:
            b.w = tok
            b.rs = {}

    def op(E, fn, reads=(), writes=()):
        extra = [b for b in reads if getattr(b, "psum", False) and b not in writes]
        if extra:
            writes = list(writes) + extra
        deps(E, reads, writes)
        inst = fn(E.e)
        E.p.cnt += 1
        inst.then_inc(E.p.sem, 1)
        commit((E.p, E.p.cnt), reads, writes)
        return inst

    def pe_group(fns, reads=(), writes=()):
        deps(PE, reads, writes)
        inst = None
        for fn in fns:
            inst = fn(nc.tensor)
        PE.p.cnt += 1
        inst.then_inc(PE.p.sem, 1)
        commit((PE.p, PE.p.cnt), reads, writes)

    dma_slots = {}

    def dma(E, fns, reads=(), writes=(), nslots=6):
        key = E.name
        if key not in dma_slots:
            dma_slots[key] = ([Prod(nc, f"d_{key}{i}") for i in range(nslots)], [0])
        slots, ctr = dma_slots[key]
        slot = slots[ctr[0] % len(slots)]
        ctr[0] += 1
        deps(E, reads, writes)
        if slot.cnt > 0:
            E.wait((slot, slot.cnt))
        for fn in fns:
            fn(E.e).then_inc(slot.sem, 16)
            slot.cnt += 16
        commit((slot, slot.cnt), reads, writes)

    def din(name, shape, dt=F32):
        return nc.dram_tensor(name, list(shape), dt, kind="ExternalInput").ap()

    x_all = din("x_all", [SEQ, D])
    x_own = din("x_own", [OWN, 128, D])
    x_halo = din("x_halo", [OWN * HALO, D])
    w_in = din("w_in", [D, IN_DIM])
    w_ab = din("w_ab", [2048, D])
    w_sb = din("w_sb", [4096, D])
    w_out = din("w_out", [D, D])
    w_up = din("w_up", [D, 8192])
    w_down = din("w_down", [8192, D])
    g1 = din("norm1_g", [1, D])
    g2 = din("norm2_g", [1, D])
    gs = din("ssm_norm_g", [1, 4096])
    gq = din("q_norm_g", [128, 1])
    gk = din("k_norm_g", [128, 1])
    cw = din("conv_w", [128, 48, 4])
    cb = din("conv_b", [128, 48])
    dtb = din("dt_bias", [1, 64])
    alog = din("a_log", [1, 64])
    dsk = din("d_skip", [1, 64])
    relb = din("rel_bias", [32, 16])
    c_ident = din("c_ident", [128, 128])
    c_tri = din("c_tri", [128, 128])
    c_flags = din("c_flags", [128, 4])
    c_amask = din("c_amask", [128, 512])
    c_ohb = din("c_ohb", [32, 5 * 256])

    out_d = nc.dram_tensor("out", [OWN, 128, D], F32, kind="ExternalOutput").ap()
    skind = "ExternalOutput" if DEBUG else "Internal"
    kT_d = nc.dram_tensor("kT_d", [128, 4, SEQ], BF16, kind=skind).ap()
    v_d = nc.dram_tensor("v_d", [128, NB, 512], BF16, kind=skind).ap()
    kiT_d = nc.dram_tensor("kiT_d", [128, SEQ], BF16, kind=skind).ap()
    ssave_d = nc.dram_tensor("ssave_d", [OWN, 128, 4096], BF16, kind=skind).ap()

    attT_d = nc.dram_tensor("attT_d", [128, 16, OWN * 128], BF16, kind=skind).ap()
    ebz_d = nc.dram_tensor("ebz_d", [16, 5, 128, 256], F32, kind="Internal").ap()
    mT_d = nc.dram_tensor("mT_d", [128, 16, OWN * 128], BF16, kind=skind).ap()
    dtraw_d = nc.dram_tensor("dtraw_d", [NB, 128, 64], F32, kind="Internal").ap()
    dbg = {}
    if DEBUG:
        dbg["qT"] = nc.dram_tensor("dbg_qT", [128, 16, OWN * 128], BF16, kind="ExternalOutput").ap()
        dbg["qiT"] = nc.dram_tensor("dbg_qiT", [128, 8, OWN * 128], BF16, kind="ExternalOutput").ap()
        dbg["wtok"] = nc.dram_tensor("dbg_wtok", [128, OWN, 16], F32, kind="ExternalOutput").ap()
        dbg["score"] = nc.dram_tensor("dbg_score", [OWN, 128, SEQ], F32, kind="ExternalOutput").ap()
        dbg["thr"] = nc.dram_tensor("dbg_thr", [128, OWN, 4], F32, kind="ExternalOutput").ap()
        dbg["ynT"] = nc.dram_tensor("dbg_ynT", [128, 32, OWN * 128], BF16, kind="ExternalOutput").ap()
    scopes = [ExitStack()]

    sbn = [0]

    def sb(name, shape, dt=F32):
        sbn[0] += 1
        return scopes[-1].enter_context(nc.sbuf_tensor(f"{name}_{sbn[0]}", list(shape), dt))

    def push_scope():
        scopes.append(ExitStack())

    def pop_scope():
        barrier()
        scopes.pop().close()

    def barrier():
        prods = [E.p for E in (PE, ACT, DVE, POOL, SP)]
        for key, (slots, ctr) in dma_slots.items():
            prods += slots
        for E in (PE, ACT, DVE, POOL, SP):
            for p in prods:
                if p.cnt > 0 and p is not E.p:
                    E.wait((p, p.cnt))

    ident_f = sb("ident_f", [128, 128]); B_ident_f = Buf()
    ident_b = sb("ident_b", [128, 128], BF16); B_ident_b = Buf()
    ones_b = sb("ones_b", [128, 128], BF16); B_ones_b = Buf()
    tri_f = sb("tri_f", [128, 128]); B_tri = Buf()
    gq_t = sb("gq_t", [128, 1]); gk_t = sb("gk_t", [128, 1]); B_gqk = Buf()
    flags_t = sb("flags_t", [128, 4]); B_flags = Buf()
    cw_t = sb("cw_t", [128, 48, 4]); cb_t = sb("cb_t", [128, 48]); B_cw = Buf()
    dtb_bc = sb("dtb_bc", [128, 64]); A_bc = sb("A_bc", [128, 64]); dsk_bc = sb("dsk_bc", [128, 64]); B_ssmc = Buf()

    dma(SP, [lambda e: e.dma_start(out=ident_f[:], in_=c_ident)], writes=[B_ident_f])
    dma(SP, [lambda e: e.dma_start(out=tri_f[:], in_=c_tri)], writes=[B_tri])
    dma(SP, [lambda e: e.dma_start(out=gq_t[:], in_=gq), lambda e: e.dma_start(out=gk_t[:], in_=gk)], writes=[B_gqk])
    dma(SP, [lambda e: e.dma_start(out=flags_t[:], in_=c_flags)], writes=[B_flags])
    dma(SP, [lambda e: e.dma_start(out=cw_t[:], in_=cw), lambda e: e.dma_start(out=cb_t[:], in_=cb)], writes=[B_cw])
    dma(SP, [lambda e: e.dma_start(out=dtb_bc[:], in_=dtb.partition_broadcast(128)),
             lambda e: e.dma_start(out=A_bc[:], in_=alog.partition_broadcast(128)),
             lambda e: e.dma_start(out=dsk_bc[:], in_=dsk.partition_broadcast(128))], writes=[B_ssmc])
    op(DVE, lambda e: e.tensor_copy(out=ident_b[:], in_=ident_f[:]), reads=[B_ident_f], writes=[B_ident_b])
    op(DVE, lambda e: e.memset(ones_b[:], 1.0), writes=[B_ones_b])
    tri_b = sb("tri_b", [128, 128], BF16)
    op(DVE, lambda e: e.tensor_copy(out=tri_b[:], in_=tri_f[:]), reads=[B_tri], writes=[B_tri])
    op(ACT, lambda e: e.activation(out=A_bc[:], in_=A_bc[:], func=AF.Exp), reads=[B_ssmc], writes=[B_ssmc])
    op(DVE, lambda e: e.tensor_scalar(out=A_bc[:], in0=A_bc[:], scalar1=-1.0, scalar2=None, op0=ALU.mult), reads=[B_ssmc], writes=[B_ssmc])

    PS = [nc.alloc_psum_tensor(f"ps{i}", [128, 512], F32) for i in range(6)]
    BPS = [Buf() for i in range(6)]
    for b_ in BPS:
        b_.psum = True
    PT = [nc.alloc_psum_tensor(f"pt{i}", [128, 1024], BF16) for i in range(2)]
    BPT = [Buf() for i in range(2)]
    for b_ in BPT:
        b_.psum = True
    psc = [0]

    psn = [6]

    def next_ps():
        i = psc[0] % psn[0]
        psc[0] += 1
        return PS[i], BPS[i]

    ptc = [0]

    def next_pt():
        i = ptc[0] % 2
        ptc[0] += 1
        return PT[i], BPT[i]

    WS = []
    BWS = []
    wsc = [0]

    def alloc_ws(n):
        WS.clear(); BWS.clear()
        for i in range(n):
            WS.append(sb(f"ws{i}_{wsc[0]}", [128, 16, 512], BF16)); BWS.append(Buf())

    def load_w(src, pieces, kchunks=16, row0=0):
        i = wsc[0] % len(WS)
        wsc[0] += 1
        t, b = WS[i], BWS[i]
        fns = []
        off = 0
        for (c0, ncol) in pieces:
            for k0 in range(0, kchunks, 4):
                k1 = min(kchunks, k0 + 4)
                sv = src[row0 + k0 * 128:row0 + k1 * 128, c0:c0 + ncol].rearrange("(kc p) c -> p kc c", p=128)
                fns.append(lambda e, sv=sv, off=off, ncol=ncol, k0=k0, k1=k1: e.dma_start(out=t[:, k0:k1, off:off + ncol], in_=sv))
            off += ncol
        dma(POOL, fns, writes=[b], nslots=4)
        return t, b

    class Rot:
        def __init__(self, name, shape, dt, n):
            self.t = [sb(f"{name}{i}", shape, dt) for i in range(n)]
            self.b = [Buf() for i in range(n)]
            self.c = 0

        def next(self):
            i = self.c % len(self.t)
            self.c += 1
            return self.t[i], self.b[i]

    NR = {}

    def alloc_norm(gsrc):
        NR["xb"] = Rot("xb", [128, D], F32, 2)
        NR["hb"] = Rot("hb", [128, D], BF16, 2)
        NR["st"] = Rot("st", [128, 4], F32, 2)
        NR["g"] = sb("gbc", [128, D]); NR["Bg"] = Buf()
        dma(SP, [lambda e: e.dma_start(out=NR["g"][:], in_=gsrc.partition_broadcast(128))], writes=[NR["Bg"]])

    def norm_rows(src, rows, src_sb=None):
        gbc, Bg = NR["g"], NR["Bg"]
        hb, Bh = NR["hb"].next()
        st, Bs = NR["st"].next()
        if src_sb is None:
            xt, Bx = NR["xb"].next()
            dma(SP, [lambda e: e.dma_start(out=xt[0:rows, :], in_=src)], writes=[Bx])
        else:
            xt, Bx = src_sb
        junk, Bjunk = hb, Bh
        op(DVE, lambda e: e.memset(st[0:rows, 0:1], 0.0), writes=[Bs])
        op(ACT, lambda e: e.activation(out=junk[0:rows, :], in_=xt[0:rows, :], func=AF.Square, accum_out=st[0:rows, 0:1]),
           reads=[Bx, Bs], writes=[Bjunk, Bs])
        op(DVE, lambda e: e.tensor_scalar(out=st[0:rows, 1:2], in0=st[0:rows, 0:1], scalar1=1.0 / D, scalar2=EPS,
                                          op0=ALU.mult, op1=ALU.add), reads=[Bs], writes=[Bs])
        op(ACT, lambda e: e.activation(out=st[0:rows, 2:3], in_=st[0:rows, 1:2], func=AF.Sqrt), reads=[Bs], writes=[Bs])
        op(DVE, lambda e: e.reciprocal(out=st[0:rows, 3:4], in_=st[0:rows, 2:3]), reads=[Bs], writes=[Bs])
        op(DVE, lambda e: e.scalar_tensor_tensor(out=hb[0:rows, :], in0=xt[0:rows, :], scalar=st[0:rows, 3:4],
                                                 in1=gbc[0:rows, :], op0=ALU.mult, op1=ALU.mult),
           reads=[Bx, Bs, Bg], writes=[Bh])
        return hb, Bh, xt, Bx

    def transpose_rows(hb, Bh, rows, dst_fn, Bdst):
        for half in range(2):
            pt, Bp = next_pt()
            fns = []
            for q8 in range(8):
                kc = half * 8 + q8
                fns.append(lambda e, kc=kc, q8=q8: e.transpose(out=pt[:, q8 * 128:q8 * 128 + rows],
                                                               in_=hb[0:rows, kc * 128:(kc + 1) * 128],
                                                               identity=ident_b[0:rows, 0:rows]))
            pe_group(fns, reads=[Bh, B_ident_b], writes=[Bp])
            src = pt[:].rearrange("p (a b) -> p a b", a=8)[:, :, 0:rows]
            op(ACT, lambda e: e.copy(out=dst_fn(half * 8, 8), in_=src), reads=[Bp], writes=[Bdst])

    push_scope()
    hT_all = sb("hT_all", [128, 16, SEQ], BF16); B_hT = Buf()
    alloc_ws(2)
    push_scope()
    alloc_norm(g1)
    for blk in range(NB):
        hb, Bh, _, _ = norm_rows(x_all[blk * 128:(blk + 1) * 128, :], 128)
        transpose_rows(hb, Bh, 128, lambda kc0, n, blk=blk: hT_all[:, kc0:kc0 + n, blk * 128:(blk + 1) * 128], B_hT)
    pop_scope()
    push_scope()

    stg_r = Rot("stg", [128, 512], BF16, 2)
    push_scope()
    sq_r = Rot("sq", [128, 512], BF16, 1)
    rs_r = Rot("rs", [128, 512], F32, 1)

    def headnorm_T(ps, Bp, gcol, Bgc, outt, Bout, post_scale):
        sq, Bsq = sq_r.next()
        op(ACT, lambda e: e.activation(out=sq[:], in_=ps[:], func=AF.Square), reads=[Bp], writes=[Bsq])
        ps2, Bp2 = next_ps()
        pe_group([lambda e: e.matmul(ps2[:], ones_b[:], sq[:], start=True, stop=True)], reads=[Bsq, B_ones_b], writes=[Bp2])
        rs, Brs = rs_r.next()
        op(DVE, lambda e: e.tensor_scalar(out=rs[:], in0=ps2[:], scalar1=1.0 / 128, scalar2=EPS, op0=ALU.mult, op1=ALU.add),
           reads=[Bp2], writes=[Brs])
        op(ACT, lambda e: e.activation(out=rs[:], in_=rs[:], func=AF.Sqrt), reads=[Brs], writes=[Brs])
        op(DVE, lambda e: e.reciprocal(out=rs[:], in_=rs[:]), reads=[Brs], writes=[Brs])
        if post_scale != 1.0:
            op(DVE, lambda e: e.tensor_scalar(out=rs[:], in0=rs[:], scalar1=post_scale, scalar2=None, op0=ALU.mult),
               reads=[Brs], writes=[Brs])
        op(DVE, lambda e: e.scalar_tensor_tensor(out=outt, in0=ps[:], scalar=gcol, in1=rs[:], op0=ALU.mult, op1=ALU.mult),
           reads=[Bp, Brs, Bgc], writes=[Bout])

    wk, Bwk = load_w(w_in, [(O_K, 512)])
    for T in range(8):
        for g in range(4):
            ps, Bp = next_ps()
            pe_group([lambda e, kc=kc: e.matmul(ps[:], wk[:, kc, g * 128:(g + 1) * 128], hT_all[:, kc, T * 512:(T + 1) * 512],
                                                start=(kc == 0), stop=(kc == 15)) for kc in range(16)],
                     reads=[Bwk, B_hT], writes=[Bp])
            stg, Bstg = stg_r.next()
            headnorm_T(ps, Bp, gk_t[:, 0:1], B_gqk, stg[:], Bstg, 1.0)
            dma(SP, [lambda e: e.dma_start(out=kT_d[:, g, T * 512:(T + 1) * 512], in_=stg[:])], reads=[Bstg])
    wv, Bwv = load_w(w_in, [(O_V, 512)])
    for blk in range(NB):
        ps, Bp = next_ps()
        pe_group([lambda e, kc=kc: e.matmul(ps[:], hT_all[:, kc, blk * 128:(blk + 1) * 128], wv[:, kc, 0:512],
                                            start=(kc == 0), stop=(kc == 15)) for kc in range(16)],
                 reads=[Bwv, B_hT], writes=[Bp])
        stg, Bstg = stg_r.next()
        op(ACT, lambda e: e.copy(out=stg[:], in_=ps[:]), reads=[Bp], writes=[Bstg])
        dma(SP, [lambda e: e.dma_start(out=v_d[:, blk, :], in_=stg[:])], reads=[Bstg])
    wki, Bwki = load_w(w_in, [(O_KI, 64), (O_KI, 64)])
    for T in range(8):
        ps, Bp = next_ps()
        pe_group([lambda e, kc=kc: e.matmul(ps[:], wki[:, kc, 0:128], hT_all[:, kc, T * 512:(T + 1) * 512],
                                            start=(kc == 0), stop=(kc == 15)) for kc in range(16)],
                 reads=[Bwki, B_hT], writes=[Bp])
        stg, Bstg = stg_r.next()
        op(ACT, lambda e: e.copy(out=stg[:], in_=ps[:]), reads=[Bp], writes=[Bstg])
        dma(SP, [lambda e: e.dma_start(out=kiT_d[:, T * 512:(T + 1) * 512], in_=stg[:])], reads=[Bstg])

    wdt, Bwdt = load_w(w_in, [(O_DT, 64)])
    dts_r = Rot("dts", [128, 64], F32, 2)
    B_dtraw = Buf()
    for blk in range(NB):
        ps, Bp = next_ps()
        pe_group([lambda e, kc=kc: e.matmul(ps[:, 0:64], hT_all[:, kc, blk * 128:(blk + 1) * 128], wdt[:, kc, 0:64],
                                            start=(kc == 0), stop=(kc == 15)) for kc in range(16)],
                 reads=[Bwdt, B_hT], writes=[Bp])
        dts, Bdts = dts_r.next()
        op(DVE, lambda e: e.tensor_copy(out=dts[:], in_=ps[:, 0:64]), reads=[Bp], writes=[Bdts])
        dma(SP, [lambda e: e.dma_start(out=dtraw_d[blk], in_=dts[:])], reads=[Bdts], writes=[B_dtraw])
    pop_scope()
    dtr_r = Rot("dtr", [128, 4, 8], F32, 2)

    pre_r = Rot("pre", [128, 5, 2, 516], BF16, 2)
    xc_r = Rot("xc", [128, 5, 512], BF16, 1)
    diag = sb("diag", [128, 5, 4, 128], BF16); B_diag = Buf()
    Sst = sb("Sst", [128, 512], F32); B_S = Buf()
    Ssel = sb("Ssel", [128, 512], F32); B_Ssel = Buf()
    dtw_r = Rot("dtw", [128, 8, 32], F32, 2)
    xw_r = Rot("xw", [128, 512], BF16, 2)
    bt_r = Rot("bt", [128, 128], BF16, 2)

    def conv_chunk(pre, Bpre, i, c, width, outt, Bout):
        ps, Bp = next_ps()
        def tap(kk):
            return pre[:, i, 0, 1 + kk:1 + kk + width] if kk % 2 == 1 else pre[:, i, 1, kk:kk + width]
        pe_group([lambda e, kk=kk: e.matmul(ps[:, 0:width], diag[:, i, kk, :], tap(kk),
                                            start=(kk == 0), stop=(kk == 3)) for kk in range(4)],
                 reads=[Bpre, B_diag], writes=[Bp])
        op(ACT, lambda e: e.activation(out=outt, in_=ps[:, 0:width], func=AF.Silu, bias=cb_t[:, c:c + 1]),
           reads=[Bp, B_cw], writes=[Bout])

    def build_diag(chunks):
        for i, c in enumerate(chunks):
            for kk in range(4):
                op(DVE, lambda e, i=i, c=c, kk=kk: e.tensor_scalar(out=diag[:, i, kk, :], in0=ident_f[:], scalar1=cw_t[:, c, kk:kk + 1],
                                                                 scalar2=None, op0=ALU.mult),
                   reads=[B_ident_f, B_cw], writes=[B_diag])

    ahl_r = Rot("ahl", [128, 2, 64], BF16, 2)

    def dt_front(src3, Bpd, g, nblk, dtw, Bdtw):
        n = nblk * 8
        v3 = lambda r: dtw[:, r, 0:n].rearrange("p (b h) -> p b h", h=8)
        bias3 = dtb_bc[:, g * 8:(g + 1) * 8].unsqueeze(1).broadcast_to([128, nblk, 8])
        A3 = A_bc[:, g * 8:(g + 1) * 8].unsqueeze(1).broadcast_to([128, nblk, 8])
        op(DVE, lambda e: e.tensor_tensor(out=v3(0), in0=src3, in1=bias3, op=ALU.add),
           reads=[Bpd, B_ssmc], writes=[Bdtw])
        op(ACT, lambda e: e.activation(out=dtw[:, 0, 0:n], in_=dtw[:, 0, 0:n], func=AF.Exp), reads=[Bdtw], writes=[Bdtw])
        op(ACT, lambda e: e.activation(out=dtw[:, 0, 0:n], in_=dtw[:, 0, 0:n], func=AF.Ln, bias=1.0), reads=[Bdtw], writes=[Bdtw])
        op(DVE, lambda e: e.tensor_tensor(out=v3(1), in0=v3(0), in1=A3, op=ALU.mult), reads=[Bdtw, B_ssmc], writes=[Bdtw])
        ahl, Bahl = ahl_r.next()
        op(DVE, lambda e: e.tensor_copy(out=ahl[:, 0, 0:n], in_=dtw[:, 1, 0:n]), reads=[Bdtw], writes=[Bahl])
        op(DVE, lambda e: e.tensor_tensor(out=ahl[:, 1, 0:n], in0=dtw[:, 1, 0:n], in1=ahl[:, 0, 0:n], op=ALU.subtract),
           reads=[Bdtw, Bahl], writes=[Bahl])
        return ahl, Bahl

    def dt_back(nblk, dtw, Bdtw, ahl, Bahl):
        n = nblk * 8
        pc, Bpc = next_ps()
        pe_group([lambda e: e.matmul(pc[:, 0:n], tri_b[:], ahl[:, 0, 0:n], start=True, stop=False),
                  lambda e: e.matmul(pc[:, 0:n], tri_b[:], ahl[:, 1, 0:n], start=False, stop=True),
                  lambda e: e.matmul(pc[:, 64:64 + n], ones_b[:], ahl[:, 0, 0:n], start=True, stop=False),
                  lambda e: e.matmul(pc[:, 64:64 + n], ones_b[:], ahl[:, 1, 0:n], start=False, stop=True)],
                 reads=[Bahl, B_tri, B_ones_b], writes=[Bpc])
        op(ACT, lambda e: e.copy(out=dtw[:, 2, 0:n], in_=pc[:, 0:n]), reads=[Bpc], writes=[Bdtw])
        op(DVE, lambda e: e.tensor_tensor(out=dtw[:, 3, 0:n], in0=pc[:, 64:64 + n], in1=dtw[:, 2, 0:n], op=ALU.subtract),
           reads=[Bpc, Bdtw], writes=[Bdtw])
        op(ACT, lambda e: e.activation(out=dtw[:, 3, 0:n], in_=dtw[:, 3, 0:n], func=AF.Exp), reads=[Bdtw], writes=[Bdtw])
        op(ACT, lambda e: e.activation(out=dtw[:, 4, 0:n], in_=pc[:, 64:64 + n], func=AF.Exp), reads=[Bpc], writes=[Bdtw])
        op(DVE, lambda e: e.tensor_tensor(out=dtw[:, 5, 0:n], in0=dtw[:, 0, 0:n], in1=dtw[:, 3, 0:n], op=ALU.mult),
           reads=[Bdtw], writes=[Bdtw])
        op(ACT, lambda e: e.activation(out=dtw[:, 6, 0:n], in_=dtw[:, 2, 0:n], func=AF.Exp), reads=[Bdtw], writes=[Bdtw])


    def dt_math(src3, Bpd, g, nblk, dtw, Bdtw):
        ahl, Bahl = dt_front(src3, Bpd, g, nblk, dtw, Bdtw)
        dt_back(nblk, dtw, Bdtw, ahl, Bahl)

    if STAGE >= 2:
        for g in range(8):
            wx, Bwx = load_w(w_in, [(O_X + g * 512, 512)])
            wb, Bwb = load_w(w_in, [(O_B + g * 128, 128)])
            chunks = [4 * g + i for i in range(4)] + [32 + g]
            build_diag(chunks)
            op(DVE, lambda e: e.memset(Sst[:], 0.0), writes=[B_S])
            pre_prev_box = [None]
            dtctx = {}

            def stageA(T):
                    dtr, Bdtr = dtr_r.next()
                    with nc.allow_non_contiguous_dma(reason="small dt slices"):
                        dma(SP, [lambda e: e.dma_start(out=dtr[:], in_=dtraw_d[T * 4:(T + 1) * 4, :, g * 8:(g + 1) * 8].rearrange("b p h -> p b h"))],
                            reads=[B_dtraw], writes=[Bdtr])
                    dtw, Bdtw = dtw_r.next()
                    ahl, Bahl = dt_front(dtr[:], Bdtr, g, 4, dtw, Bdtw)
                    dtctx[T] = (dtw, Bdtw, ahl, Bahl)
                    pre, Bpre = pre_r.next()
                    if T == 0:
                        op(DVE, lambda e: e.memset(pre[:, :, :, 0:4], 0.0), writes=[Bpre])
                    else:
                        pp, Bpp = pre_prev_box[0]
                        op(DVE, lambda e: e.tensor_copy(out=pre[:, :, 0, 0:4], in_=pp[:, :, 0, 512:516]), reads=[Bpp], writes=[Bpre])
                        op(DVE, lambda e: e.tensor_copy(out=pre[:, :, 1, 0:4], in_=pp[:, :, 1, 512:516]), reads=[Bpp], writes=[Bpre])
                    for i in range(5):
                        ps, Bp = next_ps()
                        wt, Bwt = (wx, Bwx) if i < 4 else (wb, Bwb)
                        c0 = i * 128 if i < 4 else 0
                        pe_group([lambda e, kc=kc: e.matmul(ps[:], wt[:, kc, c0:c0 + 128], hT_all[:, kc, T * 512:(T + 1) * 512],
                                                            start=(kc == 0), stop=(kc == 15)) for kc in range(16)],
                                 reads=[Bwt, B_hT], writes=[Bp])
                        op(ACT, lambda e: e.copy(out=pre[:, i, 0, 4:516], in_=ps[:]), reads=[Bp], writes=[Bpre])
                        op(DVE, lambda e: e.tensor_copy(out=pre[:, i, 1, 3:515], in_=ps[:]), reads=[Bp], writes=[Bpre])
                    pre_prev_box[0] = (pre, Bpre)
                    return pre, Bpre

            def tr_step(T, r, xc, Bxc, dtw, Bdtw):
                pt, Bpt = next_pt()
                pe_group([lambda e, i=i: e.transpose(out=pt[:, i * 128:(i + 1) * 128], in_=xc[:, i, r * 128:(r + 1) * 128],
                                                     identity=ident_b[:]) for i in range(5)],
                         reads=[Bxc, B_ident_b], writes=[Bpt])
                xw, Bxw = xw_r.next()
                bt, Bbt = bt_r.next()
                sc3 = dtw[:, 5, r * 8:(r + 1) * 8].unsqueeze(2).broadcast_to([128, 8, 64])
                op(DVE, lambda e: e.tensor_tensor(out=xw[:].rearrange("p (h d) -> p h d", h=8),
                                                  in0=pt[:, 0:512].rearrange("p (h d) -> p h d", h=8), in1=sc3, op=ALU.mult),
                   reads=[Bpt, Bdtw], writes=[Bxw])
                op(ACT, lambda e: e.copy(out=bt[:], in_=pt[:, 512:640]), reads=[Bpt], writes=[Bbt])
                return xw, Bxw, bt, Bbt

            def st_step(T, r, xw, Bxw, bt, Bbt, dtw, Bdtw):
                if r == 0:
                    op(DVE, lambda e: e.tensor_scalar(out=Ssel[:], in0=Sst[:], scalar1=flags_t[:, 0:1], scalar2=None, op0=ALU.mult),
                       reads=[B_S, B_flags], writes=[B_Ssel])
                else:
                    op(DVE, lambda e: e.scalar_tensor_tensor(out=Ssel[:], in0=Sst[:], scalar=flags_t[:, r:r + 1], in1=Ssel[:],
                                                             op0=ALU.mult, op1=ALU.add), reads=[B_S, B_flags, B_Ssel], writes=[B_Ssel])
                ps, Bp = next_ps()
                pe_group([lambda e: e.matmul(ps[:], bt[:], xw[:], start=True, stop=True)], reads=[Bbt, Bxw], writes=[Bp])
                cd3 = dtw[:, 4, r * 8:(r + 1) * 8].unsqueeze(2).broadcast_to([128, 8, 64])
                op(DVE, lambda e: e.tensor_tensor(out=Sst[:].rearrange("p (h d) -> p h d", h=8),
                                                  in0=Sst[:].rearrange("p (h d) -> p h d", h=8), in1=cd3, op=ALU.mult),
                   reads=[B_S, Bdtw], writes=[B_S])
                op(DVE, lambda e: e.tensor_tensor(out=Sst[:], in0=Sst[:], in1=ps[:], op=ALU.add), reads=[B_S, Bp], writes=[B_S])

            def stageB1(T, pre, Bpre):
                xc, Bxc = xc_r.next()
                for i in range(5):
                    conv_chunk(pre, Bpre, i, chunks[i], 512, xc[:, i, :], Bxc)
                dtw, Bdtw, ahl, Bahl = dtctx.pop(T)
                dt_back(4, dtw, Bdtw, ahl, Bahl)
                t0 = tr_step(T, 0, xc, Bxc, dtw, Bdtw)
                t1 = tr_step(T, 1, xc, Bxc, dtw, Bdtw)
                return xc, Bxc, dtw, Bdtw, t0, t1

            def stageB2(T, xc, Bxc, dtw, Bdtw, t0, t1):
                st_step(T, 0, *t0, dtw, Bdtw)
                t2 = tr_step(T, 2, xc, Bxc, dtw, Bdtw)
                st_step(T, 1, *t1, dtw, Bdtw)
                t3 = tr_step(T, 3, xc, Bxc, dtw, Bdtw)
                st_step(T, 2, *t2, dtw, Bdtw)
                st_step(T, 3, *t3, dtw, Bdtw)
                stg, Bstg = stg_r.next()
                op(DVE, lambda e: e.tensor_copy(out=stg[:], in_=Ssel[:]), reads=[B_Ssel], writes=[Bstg])
                dma(SP, [lambda e: e.dma_start(out=ssave_d[T, :, g * 512:(g + 1) * 512], in_=stg[:])], reads=[Bstg])

            pendA = stageA(0)
            for T in range(8):
                ctxB = stageB1(T, *pendA)
                pendA = stageA(T + 1) if T + 1 < 8 else None
                stageB2(T, *ctxB)

    pop_scope()
    pop_scope()
    def build_hT_own():
        hT = sb("hT_own", [128, 16, OWN, 132], BF16); B = Buf()
        op(DVE, lambda e: e.memset(hT[:, :, :, 0:1], 0.0), writes=[B])
        push_scope()
        alloc_norm(g1)
        for m in range(OWN):
            hb, Bh, _, _ = norm_rows(x_own[m], 128)
            transpose_rows(hb, Bh, 128, lambda kc0, n, m=m: hT[:, kc0:kc0 + n, m, 4:132], B)
        hb, Bh, _, _ = norm_rows(x_halo, OWN * HALO)
        for half in range(2):
            pt, Bp = next_pt()
            pe_group([lambda e, q8=q8: e.transpose(out=pt[:, q8 * 128:q8 * 128 + 24], in_=hb[0:24, (half * 8 + q8) * 128:(half * 8 + q8 + 1) * 128],
                                                   identity=ident_b[0:24, 0:24]) for q8 in range(8)], reads=[Bh, B_ident_b], writes=[Bp])
            for m in range(OWN):
                src = pt[:].rearrange("p (a b) -> p a b", a=8)[:, :, m * 3:m * 3 + 3]
                op(DVE, lambda e: e.tensor_copy(out=hT[:, half * 8:half * 8 + 8, m, 1:4], in_=src), reads=[Bp], writes=[B])
        pop_scope()
        return hT, B

    push_scope()
    qT = sb("qT", [128, 16, OWN * 128], BF16); B_qT = Buf()
    qiT = sb("qiT", [128, 8, OWN * 128], BF16); B_qiT = Buf()
    wtok = sb("wtok", [128, OWN, 16]); B_wtok = Buf()
    push_scope()
    hT_own, B_hTo = build_hT_own()
    alloc_ws(2)
    sq_r = Rot("sq", [128, 512], BF16, 1)
    rs_r = Rot("rs", [128, 512], F32, 1)

    def own_rhs(kc, half):
        return hT_own[:, kc, 4 * half:4 * half + 4, 4:132]

    for hq in range(4):
        wq, Bwq = load_w(w_in, [(O_Q + hq * 512, 512)])
        for hh in range(4):
            h = hq * 4 + hh
            for half in range(2):
                ps, Bp = next_ps()
                pe_group([lambda e, kc=kc: e.matmul(ps[:], wq[:, kc, hh * 128:(hh + 1) * 128], own_rhs(kc, half),
                                                    start=(kc == 0), stop=(kc == 15)) for kc in range(16)],
                         reads=[Bwq, B_hTo], writes=[Bp])
                headnorm_T(ps, Bp, gq_t[:, 0:1], B_gqk, qT[:, h, half * 512:(half + 1) * 512], B_qT, 128.0 ** -0.5)
    for c2 in range(2):
        wqi, Bwqi = load_w(w_in, [(O_QI + c2 * 512, 512)])
        for cc in range(4):
            for half in range(2):
                ps, Bp = next_ps()
                pe_group([lambda e, kc=kc: e.matmul(ps[:], wqi[:, kc, cc * 128:(cc + 1) * 128], own_rhs(kc, half),
                                                    start=(kc == 0), stop=(kc == 15)) for kc in range(16)],
                         reads=[Bwqi, B_hTo], writes=[Bp])
                op(ACT, lambda e: e.activation(out=qiT[:, c2 * 4 + cc, half * 512:(half + 1) * 512], in_=ps[:], func=AF.Copy, scale=0.125),
                   reads=[Bp], writes=[B_qiT])
    ww, Bww = load_w(w_in, [(O_WI, 16)])
    for m in range(OWN):
        ps, Bp = next_ps()
        pe_group([lambda e, kc=kc: e.matmul(ps[:, 0:16], hT_own[:, kc, m, 4:132], ww[:, kc, 0:16],
                                            start=(kc == 0), stop=(kc == 15)) for kc in range(16)],
                 reads=[Bww, B_hTo], writes=[Bp])
        op(DVE, lambda e: e.tensor_scalar(out=wtok[:, m, :], in0=ps[:, 0:16], scalar1=0.25, scalar2=None, op0=ALU.mult),
           reads=[Bp], writes=[B_wtok])
    if DEBUG:
        dma(SP, [lambda e: e.dma_start(out=dbg["qT"], in_=qT[:])], reads=[B_qT])
        dma(SP, [lambda e: e.dma_start(out=dbg["qiT"], in_=qiT[:])], reads=[B_qiT])
        dma(SP, [lambda e: e.dma_start(out=dbg["wtok"], in_=wtok[:])], reads=[B_wtok])
    pop_scope()

    kT = sb("kT", [128, 4, SEQ], BF16); B_kT = Buf()
    Vs = sb("Vs", [128, NB, 512], BF16); B_V = Buf()
    kiT = sb("kiT", [128, SEQ], BF16); B_kiT = Buf()
    dma(SP, [lambda e: e.dma_start(out=kT[:], in_=kT_d)], writes=[B_kT])
    dma(SP, [lambda e: e.dma_start(out=Vs[:], in_=v_d)], writes=[B_V])
    dma(SP, [lambda e: e.dma_start(out=kiT[:], in_=kiT_d)], writes=[B_kiT])
    EB = sb("EB", [128, 5, 16, 128], BF16); B_EB = Buf()
    push_scope()
    relb_t = sb("relb_t", [32, 16]); ohb_t = sb("ohb_t", [32, 1280]); B_rb = Buf()
    rbh = sb("rbh", [32, 2, 16], BF16); ohb_b = sb("ohb_b", [32, 1280], BF16)
    Fv = sb("Fv", [16, 1280]); B_Fv = Buf()
    EBf = sb("EBf", [128, 16, 128]); B_EBf = Buf()
    dma(SP, [lambda e: e.dma_start(out=relb_t[:], in_=relb), lambda e: e.dma_start(out=ohb_t[:], in_=c_ohb)], writes=[B_rb])
    op(DVE, lambda e: e.tensor_copy(out=rbh[:, 0, :], in_=relb_t[:]), reads=[B_rb], writes=[B_rb])
    op(DVE, lambda e: e.tensor_tensor(out=rbh[:, 1, :], in0=relb_t[:], in1=rbh[:, 0, :], op=ALU.subtract), reads=[B_rb], writes=[B_rb])
    op(DVE, lambda e: e.tensor_copy(out=ohb_b[:], in_=ohb_t[:]), reads=[B_rb], writes=[B_rb])
    for c3 in range(3 if STAGE >= 4 else 0):
        n0, n1 = c3 * 512, min(1280, c3 * 512 + 512)
        ps, Bp = next_ps()
        pe_group([lambda e: e.matmul(ps[0:16, 0:n1 - n0], rbh[:, 0, :], ohb_b[:, n0:n1], start=True, stop=False),
                  lambda e: e.matmul(ps[0:16, 0:n1 - n0], rbh[:, 1, :], ohb_b[:, n0:n1], start=False, stop=True)],
                 reads=[B_rb], writes=[Bp])
        op(ACT, lambda e: e.activation(out=Fv[:, n0:n1], in_=ps[0:16, 0:n1 - n0], func=AF.Exp), reads=[Bp], writes=[B_Fv])
    for kb in range(5 if STAGE >= 4 else 0):
        for r0 in range(0, 128, 32):
            src = Fv[:, kb * 256:(kb + 1) * 256].unsqueeze(1).broadcast_to([16, 32, 256])
            dma(SP, [lambda e: e.dma_start(out=ebz_d[:, kb, r0:r0 + 32, :], in_=src)], reads=[B_Fv], writes=[B_EBf])
    barrier()
    for kb in range(5 if STAGE >= 4 else 0):
        srcs = []
        for h in range(16):
            base = ebz_d[h, kb]
            srcs.append(bass.AP(tensor=base.tensor, offset=base.offset + 127, ap=[[255, 128], [1, 128]]))
        dma(SP, [lambda e, h=h: e.dma_start(out=EBf[:, h, :], in_=srcs[h]) for h in range(16)], reads=[B_EBf], writes=[B_EBf])
        op(DVE, lambda e: e.tensor_copy(out=EB[:, kb, :, :], in_=EBf[:]), reads=[B_EBf], writes=[B_EB])
    pop_scope()

    score = sb("score", [128, SEQ]); B_score = Buf()
    sel01 = sb("sel01", [128, SEQ], BF16); B_sel = Buf()
    selT = sb("selT", [128, NB, 128], BF16); B_selT = Buf()
    Dg = sb("Dg", [128, 16, 128], BF16); B_Dg = Buf()
    amask_t = sb("amask_t", [128, 512]); B_am = Buf()
    bs = sb("bs", [128, 16]); B_bs = Buf()
    half_c = sb("half_c", [128, 1]); B_hc = Buf()
    R_r = Rot("Rr", [128, 512], BF16, 3)
    E_r = Rot("Er", [128, 512], BF16, 2)
    P_r = Rot("Pr", [128, 512], BF16, 3)
    rec_r = Rot("rec", [128, 512], F32, 1)
    ao_r = Rot("ao", [128, 4, 128], BF16, 2)
    dma(SP, [lambda e: e.dma_start(out=amask_t[:], in_=c_amask)], writes=[B_am])
    p2t = sb("p2t", [128, 32]); B_p2 = Buf()
    dk = sb("dk", [128, 32]); B_dk = Buf()
    cntt = sb("cntt", [128, 32]); B_cnt = Buf()
    for kk_ in range(32):
        op(DVE, lambda e, kk_=kk_: e.memset(p2t[:, kk_:kk_ + 1], 2.0 ** -(kk_ + 1)), writes=[B_p2])
    op(DVE, lambda e: e.memset(half_c[:], 0.5), writes=[B_hc])
    psn[0] = 4
    NIT = 24

    def sc_init(m):
        nkb = 4 * (m + 1)
        Lk = nkb * 128
        for h in range(16):
            op(DVE, lambda e, h=h: e.tensor_scalar(out=Dg[:, h, :], in0=ident_f[:], scalar1=wtok[:, m, h:h + 1], scalar2=None, op0=ALU.mult),
               reads=[B_ident_f, B_wtok], writes=[B_Dg])
        for kt in range(m + 1):
            scp, Bscp = PS[4 + kt % 2], BPS[4 + kt % 2]

            def sc_front(h):
                po = (h % 2) * 64
                ps, Bp = next_ps()
                pe_group([lambda e: e.matmul(ps[:], qiT[po:po + 64, h // 2, m * 128:(m + 1) * 128], kiT[po:po + 64, kt * 512:(kt + 1) * 512],
                                             start=True, stop=True)], reads=[B_qiT, B_kiT], writes=[Bp])
                R, BR = R_r.next()
                op(ACT, lambda e: e.activation(out=R[:], in_=ps[:], func=AF.Relu), reads=[Bp], writes=[BR])
                return R, BR

            pend = {h: sc_front(h) for h in range(2)}
            for h in range(16):
                R, BR = pend.pop(h)
                if h + 2 < 16:
                    pend[h + 2] = sc_front(h + 2)
                pe_group([lambda e: e.matmul(scp[:], Dg[:, h, :], R[:], start=(h == 0), stop=(h == 15))], reads=[B_Dg, BR], writes=[Bscp])
            if kt < m:
                op(ACT, lambda e: e.copy(out=score[:, kt * 512:(kt + 1) * 512], in_=scp[:]), reads=[Bscp], writes=[B_score])
            else:
                op(DVE, lambda e: e.tensor_tensor(out=score[:, kt * 512:(kt + 1) * 512], in0=scp[:], in1=amask_t[:], op=ALU.add),
                   reads=[Bscp, B_am], writes=[B_score])
                jk, Bjk = rec_r.next()
                op(DVE, lambda e: e.tensor_tensor(out=jk[:], in0=scp[:], in1=amask_t[:], op=ALU.subtract), reads=[Bscp, B_am], writes=[Bjk])
                op(DVE, lambda e: e.tensor_reduce(out=bs[:, 0:1], in_=jk[:], axis=AX.X, op=ALU.min), reads=[Bjk], writes=[B_bs])
        op(DVE, lambda e: e.tensor_reduce(out=bs[:, 1:2], in_=score[:, 0:Lk], axis=AX.X, op=ALU.max), reads=[B_score], writes=[B_bs])
        if m > 0:
            op(DVE, lambda e: e.tensor_reduce(out=bs[:, 2:3], in_=score[:, 0:Lk - 512], axis=AX.X, op=ALU.min), reads=[B_score], writes=[B_bs])
            op(DVE, lambda e: e.tensor_tensor(out=bs[:, 0:1], in0=bs[:, 0:1], in1=bs[:, 2:3], op=ALU.min), reads=[B_bs], writes=[B_bs])
        op(DVE, lambda e: e.tensor_scalar(out=bs[:, 3:4], in0=bs[:, 0:1], scalar1=-1.0, scalar2=None, op0=ALU.add), reads=[B_bs], writes=[B_bs])
        op(DVE, lambda e: e.scalar_tensor_tensor(out=bs[:, 4:5], in0=bs[:, 1:2], scalar=1.0, in1=bs[:, 3:4], op0=ALU.add, op1=ALU.subtract),
           reads=[B_bs], writes=[B_bs])
        op(DVE, lambda e: e.tensor_scalar(out=dk[:, 0:NIT + 1], in0=p2t[:, 0:NIT + 1], scalar1=bs[:, 4:5], scalar2=None, op0=ALU.mult),
           reads=[B_bs, B_p2], writes=[B_dk])
        op(DVE, lambda e: e.memset(cntt[:], 0.0), writes=[B_cnt])
        op(DVE, lambda e: e.tensor_tensor(out=bs[:, 6:7], in0=bs[:, 3:4], in1=dk[:, 0:1], op=ALU.add), reads=[B_bs, B_dk], writes=[B_bs])
        jkb, Bjkb = sel01, B_sel

    def bis_iter(m, it):
        Lk = 512 * (m + 1)
        jkb, Bjkb = sel01, B_sel
        op(DVE, lambda e: e.tensor_scalar(out=jkb[:, 0:Lk], in0=score[:, 0:Lk], scalar1=bs[:, 6:7], scalar2=0.0,
                                          op0=ALU.is_ge, op1=ALU.add, accum_out=cntt[:, it:it + 1]),
           reads=[B_score, B_bs], writes=[Bjkb, B_cnt])
        op(DVE, lambda e: e.scalar_tensor_tensor(out=bs[:, 7:8], in0=cntt[:, it:it + 1], scalar=255.5, in1=dk[:, it:it + 1],
                                                 op0=ALU.is_ge, op1=ALU.mult), reads=[B_cnt, B_dk], writes=[B_bs])
        if it < NIT - 1:
            op(DVE, lambda e: e.scalar_tensor_tensor(out=bs[:, 6:7], in0=bs[:, 3:4], scalar=bs[:, 7:8], in1=dk[:, it + 1:it + 2],
                                                     op0=ALU.add, op1=ALU.add), reads=[B_bs, B_dk], writes=[B_bs])
        op(DVE, lambda e: e.tensor_tensor(out=bs[:, 3:4], in0=bs[:, 3:4], in1=bs[:, 7:8], op=ALU.add), reads=[B_bs], writes=[B_bs])

    def bis_fin(m):
        nkb = 4 * (m + 1)
        Lk = nkb * 128
        op(DVE, lambda e: e.tensor_scalar(out=sel01[:, 0:Lk], in0=score[:, 0:Lk], scalar1=bs[:, 3:4], scalar2=None, op0=ALU.is_ge),
           reads=[B_score, B_bs], writes=[B_sel])
        if DEBUG:
            dma(SP, [lambda e: e.dma_start(out=dbg["score"][m, :, 0:Lk], in_=score[:, 0:Lk])], reads=[B_score])
            dma(SP, [lambda e: e.dma_start(out=dbg["thr"][:, m, :], in_=bs[:, 3:7])], reads=[B_bs])
        for k8 in range(0, nkb, 8):
            nn = min(8, nkb - k8)
            pt, Bp = next_pt()
            pe_group([lambda e, q=q: e.transpose(out=pt[:, q * 128:(q + 1) * 128], in_=sel01[:, (k8 + q) * 128:(k8 + q + 1) * 128],
                                                 identity=ident_b[:]) for q in range(nn)], reads=[B_sel, B_ident_b], writes=[Bp])
            op(ACT, lambda e: e.copy(out=selT[:, k8:k8 + nn, :], in_=pt[:, 0:nn * 128].rearrange("p (a b) -> p a b", b=128)),
               reads=[Bp], writes=[B_selT])

    def attention(m, hook):
        nkb = 4 * (m + 1)
        for g4 in range(4 if STAGE >= 6 else 0):
            op_ps, Bop = PS[4], BPS[4]
            sm_ps, Bsm = PS[5], BPS[5]
            def att_front(kb):
                ps, Bp = next_ps()
                pe_group([lambda e: e.matmul(ps[:], kT[:, g4, kb * 128:(kb + 1) * 128], qT[:, 4 * g4:4 * g4 + 4, m * 128:(m + 1) * 128],
                                             start=True, stop=True)], reads=[B_kT, B_qT], writes=[Bp])
                E, BE = E_r.next()
                op(ACT, lambda e: e.activation(out=E[:], in_=ps[:], func=AF.Exp), reads=[Bp], writes=[BE])
                P, BP = P_r.next()
                selb = selT[:, kb, :].unsqueeze(1).broadcast_to([128, 4, 128])
                op(DVE, lambda e: e.tensor_tensor(out=P[:].rearrange("p (r t) -> p r t", r=4), in0=E[:].rearrange("p (r t) -> p r t", r=4),
                                                  in1=selb, op=ALU.mult), reads=[BE, B_selT], writes=[BP])
                kbrel = kb - (nkb - 5)
                if kbrel >= 0:
                    op(DVE, lambda e: e.tensor_tensor(out=P[:].rearrange("p (r t) -> p r t", r=4), in0=P[:].rearrange("p (r t) -> p r t", r=4),
                                                      in1=EB[:, kbrel, 4 * g4:4 * g4 + 4, :], op=ALU.mult), reads=[BP, B_EB], writes=[BP])
                return P, BP

            pend = {kb: att_front(kb) for kb in range(min(2, nkb))}
            for kb in range(nkb):
                P, BP = pend.pop(kb)
                if kb + 2 < nkb:
                    pend[kb + 2] = att_front(kb + 2)
                pe_group([lambda e: e.matmul(op_ps[:], Vs[:, kb, g4 * 128:(g4 + 1) * 128], P[:], start=(kb == 0), stop=(kb == nkb - 1))],
                         reads=[B_V, BP], writes=[Bop])
                pe_group([lambda e: e.matmul(sm_ps[:], ones_b[:], P[:], start=(kb == 0), stop=(kb == nkb - 1))],
                         reads=[B_ones_b, BP], writes=[Bsm])
                hook()
            if STAGE < 8:
                continue
            rec, Brec = rec_r.next()
            op(ACT, lambda e: e.copy(out=rec[:], in_=sm_ps[:]), reads=[Bsm], writes=[Brec])
            op(DVE, lambda e: e.reciprocal(out=rec[:], in_=rec[:]), reads=[Brec], writes=[Brec])
            ao, Bao = ao_r.next()
            op(DVE, lambda e: e.tensor_tensor(out=ao[:].rearrange("p r t -> p (r t)"), in0=op_ps[:], in1=rec[:], op=ALU.mult),
               reads=[Bop, Brec], writes=[Bao])
            dma(SP, [lambda e: e.dma_start(out=attT_d[:, 4 * g4:4 * g4 + 4, m * 128:(m + 1) * 128], in_=ao[:])], reads=[Bao])

    if STAGE >= 5:
        sc_init(0)
        for it in range(NIT):
            bis_iter(0, it)
        bis_fin(0)
        for m in range(OWN):
            todo = []
            if m + 1 < OWN:
                sc_init(m + 1)
                todo = list(range(NIT))

            def hook():
                if todo:
                    bis_iter(m + 1, todo.pop(0))

            attention(m, hook)
            while todo:
                bis_iter(m + 1, todo.pop(0))
            if m + 1 < OWN:
                bis_fin(m + 1)
    psn[0] = 6
    pop_scope()

    push_scope()
    ynT_all = sb("ynT_all", [128, 32, OWN * 128], BF16); B_ynT = Buf()
    hT_own, B_hTo = build_hT_own()
    push_scope()
    alloc_ws(3)
    diag = sb("diag6", [128, 6, 4, 128], BF16); B_diag = Buf()
    negmT = sb("negmT", [128, 128]); B_negm = Buf()
    op(DVE, lambda e: e.tensor_scalar(out=negmT[:], in0=tri_f[:], scalar1=-1.0, scalar2=-NEG, op0=ALU.add, op1=ALU.mult),
       reads=[B_tri], writes=[B_negm])
    gsg = sb("gsg", [128, 512]); B_gsg = Buf()
    pre6 = sb("pre6", [128, 6, 2, 132], BF16); B_pre6 = Buf()
    xc6_r = Rot("xc6", [128, 6, 128], BF16, 2)
    zs_r = Rot("zs", [128, 512], F32, 2)
    dtw_r = Rot("dtwb", [128, 8, 32], F32, 2)
    ahl_r = Rot("ahlb", [128, 2, 64], BF16, 1)
    xd = sb("xd", [128, 512], BF16); B_xd = Buf()
    xdsk = sb("xdsk", [128, 512]); B_xdsk = Buf()
    Sg = sb("Sg", [128, 512], BF16); B_Sg = Buf()
    cbm = sb("cbm", [128, 128]); B_cbm = Buf()
    Rm = sb("Rm", [128, 8, 128]); B_Rm = Buf()
    Rhl = sb("Rhl", [128, 2, 8, 128], BF16); B_Rhl = Buf()
    seg = sb("seg", [128, 8, 128]); B_seg = Buf()
    eab = sb("eab", [128, 8, 128]); B_eab = Buf()
    Mt = sb("Mt", [128, 8, 128], BF16); B_Mt = Buf()
    CE = sb("CE", [128, 8, 128], BF16); B_CE = Buf()
    y3 = sb("y3", [128, 512]); B_y3 = Buf()
    ynb = sb("ynb", [128, 512], BF16); B_ynb = Buf()
    nst = sb("nst", [128, 16]); B_nst = Buf()
    dtown = sb("dtown", [128, OWN, 64]); B_dtown = Buf()
    wdt, Bwdt = load_w(w_in, [(O_DT, 64)])
    for m in range(OWN):
        ps, Bp = next_ps()
        pe_group([lambda e, kc=kc: e.matmul(ps[:, 0:64], hT_own[:, kc, m, 4:132], wdt[:, kc, 0:64],
                                            start=(kc == 0), stop=(kc == 15)) for kc in range(16)],
                 reads=[Bwdt, B_hTo], writes=[Bp])
        op(DVE, lambda e: e.tensor_copy(out=dtown[:, m, :], in_=ps[:, 0:64]), reads=[Bp], writes=[B_dtown])
    for g in range(8 if STAGE >= 9 else 0):
        wz, Bwz = load_w(w_in, [(O_Z + g * 512, 512)])
        wx, Bwx = load_w(w_in, [(O_X + g * 512, 512)])
        wbc, Bwbc = load_w(w_in, [(O_B + g * 128, 128), (O_C + g * 128, 128)])
        chunks = [4 * g + i for i in range(4)] + [32 + g, 40 + g]
        build_diag(chunks)
        dma(SP, [lambda e: e.dma_start(out=gsg[:], in_=gs[:, g * 512:(g + 1) * 512].partition_broadcast(128))], writes=[B_gsg])
        def stageA2(m):
                xc6, B_xc6 = xc6_r.next()
                zs, B_zs = zs_r.next()
                for i in range(6):
                    ps, Bp = next_ps()
                    wt, Bwt, c0 = (wx, Bwx, i * 128) if i < 4 else (wbc, Bwbc, (i - 4) * 128)
                    pe_group([lambda e, kc=kc: e.matmul(ps[:, 0:132], wt[:, kc, c0:c0 + 128], hT_own[:, kc, m, 0:132],
                                                        start=(kc == 0), stop=(kc == 15)) for kc in range(16)],
                             reads=[Bwt, B_hTo], writes=[Bp])
                    op(ACT, lambda e: e.copy(out=pre6[:, i, 0, 0:132], in_=ps[:, 0:132]), reads=[Bp], writes=[B_pre6])
                    op(DVE, lambda e: e.tensor_copy(out=pre6[:, i, 1, 0:131], in_=ps[:, 1:132]), reads=[Bp], writes=[B_pre6])
                for i in range(6):
                    conv_chunk(pre6, B_pre6, i, chunks[i], 128, xc6[:, i, :], B_xc6)
                if SUB < 11:
                    return
                ps, Bp = next_ps()
                pe_group([lambda e, kc=kc: e.matmul(ps[:], hT_own[:, kc, m, 4:132], wz[:, kc, 0:512],
                                                    start=(kc == 0), stop=(kc == 15)) for kc in range(16)],
                         reads=[Bwz, B_hTo], writes=[Bp])
                op(ACT, lambda e: e.activation(out=zs[:], in_=ps[:], func=AF.Silu), reads=[Bp], writes=[B_zs])
                dtw, Bdtw = dtw_r.next()
                dt_math(dtown[:, m:m + 1, g * 8:(g + 1) * 8], B_dtown, g, 1, dtw, Bdtw)
                return xc6, B_xc6, zs, B_zs, dtw, Bdtw

        def stageB2(m, xc6, B_xc6, zs, B_zs, dtw, Bdtw):
                if SUB < 12:
                    return
                pt, Bpt = next_pt()
                pe_group([lambda e, i=i: e.transpose(out=pt[:, i * 128:(i + 1) * 128], in_=xc6[:, i, :], identity=ident_b[:]) for i in range(4)],
                         reads=[B_xc6, B_ident_b], writes=[Bpt])
                dt3 = dtw[:, 0, 0:8].unsqueeze(2).broadcast_to([128, 8, 64])
                dk3 = dsk_bc[:, g * 8:(g + 1) * 8].unsqueeze(2).broadcast_to([128, 8, 64])
                pt3 = pt[:, 0:512].rearrange("p (h d) -> p h d", h=8)
                op(DVE, lambda e: e.tensor_tensor(out=xd[:].rearrange("p (h d) -> p h d", h=8), in0=pt3, in1=dt3, op=ALU.mult),
                   reads=[Bpt, Bdtw], writes=[B_xd])
                if SUB != 132:
                    op(DVE, lambda e: e.tensor_tensor(out=xdsk[:].rearrange("p (h d) -> p h d", h=8), in0=pt3, in1=dk3, op=ALU.mult),
                       reads=[Bpt, B_ssmc], writes=[B_xdsk])
                if SUB != 131:
                    dma(SP, [lambda e: e.dma_start(out=Sg[:], in_=ssave_d[m, :, g * 512:(g + 1) * 512])], writes=[B_Sg])
                if SUB < 13 or SUB in (131, 132):
                    return
                ps, Bp = next_ps()
                pe_group([lambda e: e.matmul(ps[:, 0:128], xc6[:, 4, :], xc6[:, 5, :], start=True, stop=True)], reads=[B_xc6], writes=[Bp])
                op(DVE, lambda e: e.tensor_tensor(out=cbm[:], in0=ps[:, 0:128], in1=tri_f[:], op=ALU.mult), reads=[Bp, B_tri], writes=[B_cbm])
                if SUB == 133:
                    return
                id3 = ident_f[:].unsqueeze(1).broadcast_to([128, 8, 128])
                ac3 = dtw[:, 2, 0:8].unsqueeze(2).broadcast_to([128, 8, 128])
                op(DVE, lambda e: e.tensor_tensor(out=Rm[:], in0=id3, in1=ac3, op=ALU.mult), reads=[B_ident_f, Bdtw], writes=[B_Rm])
                op(DVE, lambda e: e.tensor_copy(out=Rhl[:, 0], in_=Rm[:]), reads=[B_Rm], writes=[B_Rhl])
                op(DVE, lambda e: e.tensor_tensor(out=Rhl[:, 1], in0=Rm[:], in1=Rhl[:, 0], op=ALU.subtract), reads=[B_Rm, B_Rhl], writes=[B_Rhl])
                if SUB == 134:
                    return
                abc = []
                for hb2 in range(2):
                    pa, Bpa = next_ps()
                    pe_group([lambda e: e.matmul(pa[:], ones_b[:], Rhl[:, 0, 4 * hb2:4 * hb2 + 4, :], start=True, stop=False),
                              lambda e: e.matmul(pa[:], ones_b[:], Rhl[:, 1, 4 * hb2:4 * hb2 + 4, :], start=False, stop=True)],
                             reads=[B_Rhl, B_ones_b], writes=[Bpa])
                    abc.append((pa, Bpa))
                    pa3 = pa[:].rearrange("p (h l) -> p h l", h=4)
                    nm3 = negmT[:].unsqueeze(1).broadcast_to([128, 4, 128])
                    op(DVE, lambda e: e.tensor_tensor(out=seg[:, 4 * hb2:4 * hb2 + 4, :], in0=pa3, in1=nm3, op=ALU.add),
                       reads=[Bpa, B_negm], writes=[B_seg])
                    if SUB != 135:
                        op(ACT, lambda e: e.activation(out=eab[:, 4 * hb2:4 * hb2 + 4, :], in_=pa3, func=AF.Exp), reads=[Bpa], writes=[B_eab])
                if SUB < 14 or SUB in (133, 134, 135):
                    return
                op(DVE, lambda e: e.tensor_scalar(out=nst[:, 0:8], in0=dtw[:, 2, 0:8], scalar1=-1.0, scalar2=None, op0=ALU.mult),
                   reads=[Bdtw], writes=[B_nst])
                na3 = nst[:, 0:8].unsqueeze(2).broadcast_to([128, 8, 128])
                op(DVE, lambda e: e.tensor_tensor(out=seg[:], in0=seg[:], in1=na3, op=ALU.add), reads=[B_seg, B_nst], writes=[B_seg])
                op(ACT, lambda e: e.activation(out=seg[:], in_=seg[:], func=AF.Exp), reads=[B_seg], writes=[B_seg])
                cb3 = cbm[:].unsqueeze(1).broadcast_to([128, 8, 128])
                op(DVE, lambda e: e.tensor_tensor(out=Mt[:], in0=seg[:], in1=cb3, op=ALU.mult), reads=[B_seg, B_cbm], writes=[B_Mt])
                c3_ = xc6[:, 5, :].unsqueeze(1).broadcast_to([128, 8, 128])
                op(DVE, lambda e: e.tensor_tensor(out=CE[:], in0=eab[:], in1=c3_, op=ALU.mult), reads=[B_eab, B_xc6], writes=[B_CE])
                if SUB < 15:
                    return
                psy, Bpy = next_ps()
                fns = []
                for h in range(8):
                    fns.append(lambda e, h=h: e.matmul(psy[:, h * 64:(h + 1) * 64], Mt[:, h, :], xd[:, h * 64:(h + 1) * 64], start=True, stop=False))
                    fns.append(lambda e, h=h: e.matmul(psy[:, h * 64:(h + 1) * 64], CE[:, h, :], Sg[:, h * 64:(h + 1) * 64], start=False, stop=True))
                pe_group(fns, reads=[B_Mt, B_CE, B_xd, B_Sg], writes=[Bpy])
                op(DVE, lambda e: e.tensor_tensor(out=y3[:], in0=psy[:], in1=xdsk[:], op=ALU.add), reads=[Bpy, B_xdsk], writes=[B_y3])
                op(DVE, lambda e: e.tensor_tensor(out=y3[:], in0=y3[:], in1=zs[:], op=ALU.mult), reads=[B_y3, B_zs], writes=[B_y3])
                if SUB < 16:
                    return
                op(DVE, lambda e: e.memset(nst[:, 8:9], 0.0), writes=[B_nst])
                op(ACT, lambda e: e.activation(out=ynb[:], in_=y3[:], func=AF.Square, accum_out=nst[:, 8:9]), reads=[B_y3, B_nst], writes=[B_ynb, B_nst])
                op(DVE, lambda e: e.tensor_scalar(out=nst[:, 9:10], in0=nst[:, 8:9], scalar1=1.0 / 512, scalar2=EPS, op0=ALU.mult, op1=ALU.add),
                   reads=[B_nst], writes=[B_nst])
                op(ACT, lambda e: e.activation(out=nst[:, 10:11], in_=nst[:, 9:10], func=AF.Sqrt), reads=[B_nst], writes=[B_nst])
                op(DVE, lambda e: e.reciprocal(out=nst[:, 11:12], in_=nst[:, 10:11]), reads=[B_nst], writes=[B_nst])
                op(DVE, lambda e: e.scalar_tensor_tensor(out=ynb[:], in0=y3[:], scalar=nst[:, 11:12], in1=gsg[:], op0=ALU.mult, op1=ALU.mult),
                   reads=[B_y3, B_nst, B_gsg], writes=[B_ynb])
                if SUB < 17:
                    return
                pt, Bpt = next_pt()
                pe_group([lambda e, i=i: e.transpose(out=pt[:, i * 128:(i + 1) * 128], in_=ynb[:, i * 128:(i + 1) * 128], identity=ident_b[:])
                          for i in range(4)], reads=[B_ynb, B_ident_b], writes=[Bpt])
                op(ACT, lambda e: e.copy(out=ynT_all[:, 4 * g:4 * g + 4, m * 128:(m + 1) * 128],
                                         in_=pt[:, 0:512].rearrange("p (a b) -> p a b", a=4)), reads=[Bpt], writes=[B_ynT])


        pendA2 = stageA2(0)
        for m in range(OWN):
            nxtA2 = stageA2(m + 1) if m + 1 < OWN else None
            stageB2(m, *pendA2)
            pendA2 = nxtA2
    if DEBUG:
        dma(SP, [lambda e: e.dma_start(out=dbg["ynT"], in_=ynT_all[:])], reads=[B_ynT])
    pop_scope()
    mergedT = sb("mergedT", [128, 16, OWN * 128], BF16); B_mT = Buf()
    sg_r = Rot("sg", [128, 512], F32, 2)
    tmpm_r = Rot("tmpm", [128, 512], F32, 2)
    push_scope()
    attT = sb("attT", [128, 16, OWN * 128], BF16); B_attT = Buf()
    dma(SP, [lambda e: e.dma_start(out=attT[:], in_=attT_d)], writes=[B_attT])
    alloc_ws(2)
    for cg in range(4 if STAGE >= 10 else 0):
        wab, Bwab = load_w(w_ab, [(cg * 512, 512)])
        wga, Bwga = load_w(w_in, [(O_GA + cg * 512, 512)])
        for cc in range(4):
            ct = cg * 4 + cc
            for half in range(2):
                pg, Bpg = next_ps()
                pe_group([lambda e, kc=kc: e.matmul(pg[:], wga[:, kc, cc * 128:(cc + 1) * 128], own_rhs(kc, half),
                                                    start=(kc == 0), stop=(kc == 15)) for kc in range(16)],
                         reads=[Bwga, B_hTo], writes=[Bpg])
                sg, Bsg = sg_r.next()
                op(ACT, lambda e: e.activation(out=sg[:], in_=pg[:], func=AF.Sigmoid), reads=[Bpg], writes=[Bsg])
                pa, Bpa = next_ps()
                pe_group([lambda e, kc=kc: e.matmul(pa[:], wab[:, kc, cc * 128:(cc + 1) * 128], attT[:, kc, half * 512:(half + 1) * 512],
                                                    start=(kc == 0), stop=(kc == 15)) for kc in range(16)],
                         reads=[Bwab, B_attT], writes=[Bpa])
                op(DVE, lambda e: e.tensor_tensor(out=mergedT[:, ct, half * 512:(half + 1) * 512], in0=pa[:], in1=sg[:], op=ALU.mult),
                   reads=[Bpa, Bsg], writes=[B_mT])
    pop_scope()
    push_scope()
    alloc_ws(3)
    for cg in range(4 if STAGE >= 10 else 0):
        wsb0, Bwsb0 = load_w(w_sb, [(cg * 512, 512)], row0=0)
        wsb1, Bwsb1 = load_w(w_sb, [(cg * 512, 512)], row0=2048)
        wgs, Bwgs = load_w(w_in, [(O_GS + cg * 512, 512)])
        for cc in range(4):
            ct = cg * 4 + cc
            for half in range(2):
                pg, Bpg = next_ps()
                pe_group([lambda e, kc=kc: e.matmul(pg[:], wgs[:, kc, cc * 128:(cc + 1) * 128], own_rhs(kc, half),
                                                    start=(kc == 0), stop=(kc == 15)) for kc in range(16)],
                         reads=[Bwgs, B_hTo], writes=[Bpg])
                sg, Bsg = sg_r.next()
                op(ACT, lambda e: e.activation(out=sg[:], in_=pg[:], func=AF.Sigmoid), reads=[Bpg], writes=[Bsg])
                py, Bpy = next_ps()
                pe_group([lambda e, kc=kc: e.matmul(py[:], (wsb0 if kc < 16 else wsb1)[:, kc % 16, cc * 128:(cc + 1) * 128],
                                                    ynT_all[:, kc, half * 512:(half + 1) * 512],
                                                    start=(kc == 0), stop=(kc == 31)) for kc in range(32)],
                         reads=[Bwsb0, Bwsb1, B_ynT], writes=[Bpy])
                tm, Btm = tmpm_r.next()
                op(DVE, lambda e: e.tensor_tensor(out=tm[:], in0=py[:], in1=sg[:], op=ALU.mult), reads=[Bpy, Bsg], writes=[Btm])
                op(DVE, lambda e: e.tensor_tensor(out=mergedT[:, ct, half * 512:(half + 1) * 512], in0=tm[:],
                                                  in1=mergedT[:, ct, half * 512:(half + 1) * 512], op=ALU.add),
                   reads=[Btm, B_mT], writes=[B_mT])
    pop_scope()
    dma(SP, [lambda e: e.dma_start(out=mT_d, in_=mergedT[:])], reads=[B_mT])
    pop_scope()

    push_scope()
    x1acc = sb("x1acc", [128, OWN, D]); Bx1 = [Buf() for _ in range(OWN)]
    h2T = sb("h2T", [128, 16, OWN * 128], BF16); B_h2T = Buf()
    alloc_ws(3)
    for m in range(OWN):
        dma(SP, [lambda e: e.dma_start(out=x1acc[:, m, :], in_=x_own[m])], writes=[Bx1[m]])
    push_scope()
    mT = sb("mT", [128, 16, OWN * 128], BF16); B_mTl = Buf()
    dma(SP, [lambda e: e.dma_start(out=mT[:], in_=mT_d)], writes=[B_mTl])
    for ct in range(4 if STAGE >= 10 else 0):
        wo, Bwo = load_w(w_out, [(ct * 512, 512)])
        for blk in range(OWN):
            ps, Bp = next_ps()
            pe_group([lambda e, kc=kc: e.matmul(ps[:], mT[:, kc, blk * 128:(blk + 1) * 128], wo[:, kc, 0:512],
                                                start=(kc == 0), stop=(kc == 15)) for kc in range(16)],
                     reads=[Bwo, B_mTl], writes=[Bp])
            op(DVE, lambda e: e.tensor_tensor(out=x1acc[:, blk, ct * 512:(ct + 1) * 512], in0=x1acc[:, blk, ct * 512:(ct + 1) * 512],
                                              in1=ps[:], op=ALU.add), reads=[Bp, Bx1[blk]], writes=[Bx1[blk]])
    pop_scope()
    push_scope()
    alloc_norm(g2)
    for m in range(OWN):
        hb, Bh, _, _ = norm_rows(None, 128, src_sb=(x1acc[:, m, :], Bx1[m]))
        transpose_rows(hb, Bh, 128, lambda kc0, n, m=m: h2T[:, kc0:kc0 + n, m * 128:(m + 1) * 128], B_h2T)
    pop_scope()
    uT_r = Rot("uT", [128, 4, OWN * 128], BF16, 2)
    tmp_r = Rot("tmpf", [128, 512], F32, 2)
    for fg in range(16):
        wu, Bwu = load_w(w_up, [(fg * 512, 512)])
        uT, BuT = uT_r.next()
        for fc in range(4):
            for half in range(2):
                ps, Bp = next_ps()
                pe_group([lambda e, kc=kc: e.matmul(ps[:], wu[:, kc, fc * 128:(fc + 1) * 128], h2T[:, kc, half * 512:(half + 1) * 512],
                                                    start=(kc == 0), stop=(kc == 15)) for kc in range(16)],
                         reads=[Bwu, B_h2T], writes=[Bp])
                tmp, Btmp = tmp_r.next()
                op(ACT, lambda e: e.activation(out=tmp[:], in_=ps[:], func=AF.Relu), reads=[Bp], writes=[Btmp])
                op(DVE, lambda e: e.tensor_tensor(out=uT[:, fc, half * 512:(half + 1) * 512], in0=tmp[:], in1=tmp[:], op=ALU.mult),
                   reads=[Btmp], writes=[BuT])
        i = wsc[0] % len(WS)
        wsc[0] += 1
        wdt, Bwd = WS[i], BWS[i]
        wdv = wdt[:].rearrange("p a b -> p (a b)").rearrange("p (k c) -> p k c", k=4)
        svd = w_down[fg * 512:(fg + 1) * 512, :].rearrange("(kc p) c -> p kc c", p=128)
        dma(POOL, [lambda e: e.dma_start(out=wdv, in_=svd)], writes=[Bwd], nslots=4)
        for blk in range(OWN):
            for ct in range(4):
                ps, Bp = next_ps()
                pe_group([lambda e, fc=fc: e.matmul(ps[:], uT[:, fc, blk * 128:(blk + 1) * 128], wdv[:, fc, ct * 512:(ct + 1) * 512],
                                                    start=(fc == 0), stop=(fc == 3)) for fc in range(4)],
                         reads=[BuT, Bwd], writes=[Bp])
                op(DVE, lambda e: e.tensor_tensor(out=x1acc[:, blk, ct * 512:(ct + 1) * 512], in0=x1acc[:, blk, ct * 512:(ct + 1) * 512],
                                                  in1=ps[:], op=ALU.add), reads=[Bp, Bx1[blk]], writes=[Bx1[blk]])
    for m in range(OWN):
        dma(SP, [lambda e: e.dma_start(out=out_d[m], in_=x1acc[:, m, :])], reads=[Bx1[m]])
    pop_scope()
    for key, (slots, ctr) in dma_slots.items():
        for s in slots:
            if s.cnt > 0:
                SP.wait((s, s.cnt))
    for E in (PE, ACT, DVE):
        if E.p.cnt > 0:
            SP.wait((E.p, E.p.cnt))
    return nc


_CONST = {}


def _consts(j):
    if j in _CONST:
        return _CONST[j]
    ident = np.eye(128, dtype=np.float32)
    tri = np.triu(np.ones((128, 128), np.float32))
    flags = np.zeros((128, 4), np.float32); flags[:, j] = 1.0
    t = np.arange(128)
    am = np.zeros((128, 4, 128), np.float32)
    for r in range(4):
        if r == j:
            am[:, r, :] = np.where(t[None, :] <= t[:, None], 0.0, NEG)
        elif r > j:
            am[:, r, :] = NEG
    ohb = np.zeros((32, 5 * 256), np.float32)
    for kbrel in range(5):
        delta = j + 1 - kbrel
        for v in range(255):
            dd = 128 * delta + (v - 127)
            n = max(dd, 0)
            if n < 16:
                bkt = n
            else:
                bkt = min(31, 16 + int(np.float32(np.log(np.float32(max(n, 1)) / np.float32(16)) / np.float32(np.log(8.0)) * np.float32(16))))
            ohb[bkt, kbrel * 256 + v] += 1.0
            ohb[31, kbrel * 256 + v] -= 1.0
    _CONST[j] = dict(c_ident=ident, c_tri=tri, c_flags=flags, c_amask=am.reshape(128, 512), c_ohb=ohb)
    return _CONST[j]


def make_in_maps(inputs):
    x = np.ascontiguousarray(inputs["x"], dtype=np.float32)
    shared = dict(
        w_in=np.ascontiguousarray(inputs["w_in"][0]),
        w_ab=np.ascontiguousarray(inputs["w_att_branch"][0]),
        w_sb=np.ascontiguousarray(inputs["w_ssm_branch"][0]),
        w_out=np.ascontiguousarray(inputs["w_out"][0]),
        w_up=np.ascontiguousarray(inputs["w_up"][0]),
        w_down=np.ascontiguousarray(inputs["w_down"][0]),
        norm1_g=np.ascontiguousarray(inputs["norm1_g"].reshape(1, D)),
        norm2_g=np.ascontiguousarray(inputs["norm2_g"].reshape(1, D)),
        ssm_norm_g=np.ascontiguousarray(inputs["ssm_norm_g"].reshape(1, 4096)),
        q_norm_g=np.ascontiguousarray(inputs["q_norm_g"].reshape(128, 1)),
        k_norm_g=np.ascontiguousarray(inputs["k_norm_g"].reshape(128, 1)),
        conv_w=np.ascontiguousarray(inputs["conv_w"][0].T.reshape(48, 128, 4).transpose(1, 0, 2)),
        conv_b=np.ascontiguousarray(inputs["conv_b"][0].reshape(48, 128).T),
        dt_bias=np.ascontiguousarray(inputs["dt_bias"].reshape(1, 64)),
        a_log=np.ascontiguousarray(inputs["a_log"].reshape(1, 64)),
        d_skip=np.ascontiguousarray(inputs["d_skip"].reshape(1, 64)),
        rel_bias=np.ascontiguousarray(inputs["rel_bias"]),
    )
    maps = []
    for c in range(8):
        b, j = c // 4, c % 4
        xb = x[b].reshape(NB, 128, D)
        own = np.ascontiguousarray(xb[j::4])
        halo = np.zeros((OWN, HALO, D), np.float32)
        for m in range(OWN):
            t0 = (4 * m + j) * 128
            if t0 >= HALO:
                halo[m] = x[b, t0 - HALO:t0]
        m_ = dict(shared)
        m_.update(x_all=x[b], x_own=own, x_halo=halo.reshape(OWN * HALO, D))
        m_.update(_consts(j))
        maps.append(m_)
    return maps


_NC = None


def kernel(**inputs):
    global _NC
    if _NC is None:
        _NC = build_program()
    maps = make_in_maps(inputs)
    res = run_bass_kernel_spmd(_NC, maps, core_ids=list(range(8)))
    out = np.zeros((2, SEQ, D), np.float32)
    for c in range(8):
        b, j = c // 4, c % 4
        o = res.results[c]["out"]
        out[b].reshape(NB, 128, D)[j::4] = o
    return out
```

```python
from contextlib import ExitStack
import numpy as np
import concourse.bass as bass
import concourse.mybir as mybir
from concourse.bass_utils import run_bass_kernel_spmd

F32 = mybir.dt.float32
BF16 = mybir.dt.bfloat16
AF = mybir.ActivationFunctionType
ALU = mybir.AluOpType
AX = mybir.AxisListType

D = 2048
SEQ = 4096
NB = 32
OWN = 8
HALO = 3
BW = 128 + HALO
IN_DIM = 18576
O_GA, O_GS, O_Q, O_K, O_V, O_QI, O_KI, O_WI, O_Z, O_X, O_B, O_C, O_DT = (
    0, 2048, 4096, 6144, 6656, 7168, 8192, 8256, 8272, 12368, 16464, 17488, 18512)
EPS = 1e-6
NEG = -30000.0
DEBUG = False
STAGE = 99

SUB = 99


class Prod:
    def __init__(self, nc, name):
        self.sem = nc.alloc_semaphore(name)
        self.cnt = 0
        self.name = name


class Eng:
    def __init__(self, nc, e, name, selfsync=True):
        self.e = e
        self.p = Prod(nc, "s_" + name)
        self.seen = {}
        self.selfsync = selfsync
        self.name = name

    def wait(self, tok):
        if tok is None:
            return
        prod, val = tok
        if prod is self.p and not self.selfsync:
            return
        if self.seen.get(prod, 0) >= val:
            return
        self.seen[prod] = val
        self.e.wait_ge(prod.sem, val)


class Buf:
    def __init__(self, name="b"):
        self.name = name
        self.w = None
        self.rs = {}


def build_program():
    nc = bass.Bass("TRN2", target_bir_lowering=False)
    PE = Eng(nc, nc.tensor, "pe", selfsync=False)
    ACT = Eng(nc, nc.scalar, "act")
    DVE = Eng(nc, nc.vector, "dve")
    POOL = Eng(nc, nc.gpsimd, "pool")
    SP = Eng(nc, nc.sync, "sp")

    def deps(E, reads, writes):
        for b in reads:
            E.wait(b.w)
        for b in writes:
            E.wait(b.w)
            for t in list(b.rs.items()):
                E.wait(t)

    def commit(tok, reads, writes):
        prod, val = tok
        for b in reads:
            b.rs[prod] = val
        for b in writes:
            b.w = tok
            b.rs = {}

    def op(E, fn, reads=(), writes=()):
        extra = [b for b in reads if getattr(b, "psum", False) and b not in writes]
        if extra:
            writes = list(writes) + extra
        deps(E, reads, writes)
        inst = fn(E.e)
        E.p.cnt += 1
        inst.then_inc(E.p.sem, 1)
        commit((E.p, E.p.cnt), reads, writes)
        return inst

    def pe_group(fns, reads=(), writes=()):
        deps(PE, reads, writes)
        inst = None
        for fn in fns:
            inst = fn(nc.tensor)
        PE.p.cnt += 1
        inst.then_inc(PE.p.sem, 1)
        commit((PE.p, PE.p.cnt), reads, writes)

    dma_slots = {}

    def dma(E, fns, reads=(), writes=(), nslots=6):
        key = E.name
        if key not in dma_slots:
            dma_slots[key] = ([Prod(nc, f"d_{key}{i}") for i in range(nslots)], [0])
        slots, ctr = dma_slots[key]
        slot = slots[ctr[0] % len(slots)]
        ctr[0] += 1
        deps(E, reads, writes)
        if slot.cnt > 0:
            E.wait((slot, slot.cnt))
        for fn in fns:
            fn(E.e).then_inc(slot.sem, 16)
            slot.cnt += 16
        commit((slot, slot.cnt), reads, writes)

    def din(name, shape, dt=F32):
        return nc.dram_tensor(name, list(shape), dt, kind="ExternalInput").ap()

    x_all = din("x_all", [SEQ, D])
    x_own = din("x_own", [OWN, 128, D])
    x_halo = din("x_halo", [OWN * HALO, D])
    w_in = din("w_in", [D, IN_DIM])
    w_ab = din("w_ab", [2048, D])
    w_sb = din("w_sb", [4096, D])
    w_out = din("w_out", [D, D])
    w_up = din("w_up", [D, 8192])
    w_down = din("w_down", [8192, D])
    g1 = din("norm1_g", [1, D])
    g2 = din("norm2_g", [1, D])
    gs = din("ssm_norm_g", [1, 4096])
    gq = din("q_norm_g", [128, 1])
    gk = din("k_norm_g", [128, 1])
    cw = din("conv_w", [128, 48, 4])
    cb = din("conv_b", [128, 48])
    dtb = din("dt_bias", [1, 64])
    alog = din("a_log", [1, 64])
    dsk = din("d_skip", [1, 64])
    relb = din("rel_bias", [32, 16])
    c_ident = din("c_ident", [128, 128])
    c_tri = din("c_tri", [128, 128])
    c_flags = din("c_flags", [128, 4])
    c_amask = din("c_amask", [128, 512])
    c_ohb = din("c_ohb", [32, 5 * 256])

    out_d = nc.dram_tensor("out", [OWN, 128, D], F32, kind="ExternalOutput").ap()
    skind = "ExternalOutput" if DEBUG else "Internal"
    kT_d = nc.dram_tensor("kT_d", [128, 4, SEQ], BF16, kind=skind).ap()
    v_d = nc.dram_tensor("v_d", [128, NB, 512], BF16, kind=skind).ap()
    kiT_d = nc.dram_tensor("kiT_d", [128, SEQ], BF16, kind=skind).ap()
    ssave_d = nc.dram_tensor("ssave_d", [OWN, 128, 4096], BF16, kind=skind).ap()

    attT_d = nc.dram_tensor("attT_d", [128, 16, OWN * 128], BF16, kind=skind).ap()
    ebz_d = nc.dram_tensor("ebz_d", [16, 5, 128, 256], F32, kind="Internal").ap()
    mT_d = nc.dram_tensor("mT_d", [128, 16, OWN * 128], BF16, kind=skind).ap()
    dtraw_d = nc.dram_tensor("dtraw_d", [NB, 128, 64], F32, kind="Internal").ap()
    dbg = {}
    if DEBUG:
        dbg["qT"] = nc.dram_tensor("dbg_qT", [128, 16, OWN * 128], BF16, kind="ExternalOutput").ap()
        dbg["qiT"] = nc.dram_tensor("dbg_qiT", [128, 8, OWN * 128], BF16, kind="ExternalOutput").ap()
        dbg["wtok"] = nc.dram_tensor("dbg_wtok", [128, OWN, 16], F32, kind="ExternalOutput").ap()
        dbg["score"] = nc.dram_tensor("dbg_score", [OWN, 128, SEQ], F32, kind="ExternalOutput").ap()
        dbg["thr"] = nc.dram_tensor("dbg_thr", [128, OWN, 4], F32, kind="ExternalOutput").ap()
        dbg["ynT"] = nc.dram_tensor("dbg_ynT", [128, 32, OWN * 128], BF16, kind="ExternalOutput").ap()
    scopes = [ExitStack()]

    sbn = [0]

    def sb(name, shape, dt=F32):
        sbn[0] += 1
        return scopes[-1].enter_context(nc.sbuf_tensor(f"{name}_{sbn[0]}", list(shape), dt))

    def push_scope():
        scopes.append(ExitStack())

    def pop_scope():
        barrier()
        scopes.pop().close()

    def barrier():
        prods = [E.p for E in (PE, ACT, DVE, POOL, SP)]
        for key, (slots, ctr) in dma_slots.items():
            prods += slots
        for E in (PE, ACT, DVE, POOL, SP):
            for p in prods:
                if p.cnt > 0 and p is not E.p:
                    E.wait((p, p.cnt))

    ident_f = sb("ident_f", [128, 128]); B_ident_f = Buf()
    ident_b = sb("ident_b", [128, 128], BF16); B_ident_b = Buf()
    ones_b = sb("ones_b", [128, 128], BF16); B_ones_b = Buf()
    tri_f = sb("tri_f", [128, 128]); B_tri = Buf()
    gq_t = sb("gq_t", [128, 1]); gk_t = sb("gk_t", [128, 1]); B_gqk = Buf()
    flags_t = sb("flags_t", [128, 4]); B_flags = Buf()
    cw_t = sb("cw_t", [128, 48, 4]); cb_t = sb("cb_t", [128, 48]); B_cw = Buf()
    dtb_bc = sb("dtb_bc", [128, 64]); A_bc = sb("A_bc", [128, 64]); dsk_bc = sb("dsk_bc", [128, 64]); B_ssmc = Buf()

    dma(SP, [lambda e: e.dma_start(out=ident_f[:], in_=c_ident)], writes=[B_ident_f])
    dma(SP, [lambda e: e.dma_start(out=tri_f[:], in_=c_tri)], writes=[B_tri])
    dma(SP, [lambda e: e.dma_start(out=gq_t[:], in_=gq), lambda e: e.dma_start(out=gk_t[:], in_=gk)], writes=[B_gqk])
    dma(SP, [lambda e: e.dma_start(out=flags_t[:], in_=c_flags)], writes=[B_flags])
    dma(SP, [lambda e: e.dma_start(out=cw_t[:], in_=cw), lambda e: e.dma_start(out=cb_t[:], in_=cb)], writes=[B_cw])
    dma(SP, [lambda e: e.dma_start(out=dtb_bc[:], in_=dtb.partition_broadcast(128)),
             lambda e: e.dma_start(out=A_bc[:], in_=alog.partition_broadcast(128)),
             lambda e: e.dma_start(out=dsk_bc[:], in_=dsk.partition_broadcast(128))], writes=[B_ssmc])
    op(DVE, lambda e: e.tensor_copy(out=ident_b[:], in_=ident_f[:]), reads=[B_ident_f], writes=[B_ident_b])
    op(DVE, lambda e: e.memset(ones_b[:], 1.0), writes=[B_ones_b])
    tri_b = sb("tri_b", [128, 128], BF16)
    op(DVE, lambda e: e.tensor_copy(out=tri_b[:], in_=tri_f[:]), reads=[B_tri], writes=[B_tri])
    op(ACT, lambda e: e.activation(out=A_bc[:], in_=A_bc[:], func=AF.Exp), reads=[B_ssmc], writes=[B_ssmc])
    op(DVE, lambda e: e.tensor_scalar(out=A_bc[:], in0=A_bc[:], scalar1=-1.0, scalar2=None, op0=ALU.mult), reads=[B_ssmc], writes=[B_ssmc])

    PS = [nc.alloc_psum_tensor(f"ps{i}", [128, 512], F32) for i in range(6)]
    BPS = [Buf() for i in range(6)]
    for b_ in BPS:
        b_.psum = True
    PT = [nc.alloc_psum_tensor(f"pt{i}", [128, 1024], BF16) for i in range(2)]
    BPT = [Buf() for i in range(2)]
    for b_ in BPT:
        b_.psum = True
    psc = [0]

    psn = [6]

    def next_ps():
        i = psc[0] % psn[0]
        psc[0] += 1
        return PS[i], BPS[i]

    ptc = [0]

    def next_pt():
        i = ptc[0] % 2
        ptc[0] += 1
        return PT[i], BPT[i]

    WS = []
    BWS = []
    wsc = [0]

    def alloc_ws(n):
        WS.clear(); BWS.clear()
        for i in range(n):
            WS.append(sb(f"ws{i}_{wsc[0]}", [128, 16, 512], BF16)); BWS.append(Buf())

    def load_w(src, pieces, kchunks=16, row0=0):
        i = wsc[0] % len(WS)
        wsc[0] += 1
        t, b = WS[i], BWS[i]
        fns = []
        off = 0
        for (c0, ncol) in pieces:
            for k0 in range(0, kchunks, 4):
                k1 = min(kchunks, k0 + 4)
                sv = src[row0 + k0 * 128:row0 + k1 * 128, c0:c0 + ncol].rearrange("(kc p) c -> p kc c", p=128)
                fns.append(lambda e, sv=sv, off=off, ncol=ncol, k0=k0, k1=k1: e.dma_start(out=t[:, k0:k1, off:off + ncol], in_=sv))
            off += ncol
        dma(POOL, fns, writes=[b], nslots=4)
        return t, b

    class Rot:
        def __init__(self, name, shape, dt, n):
            self.t = [sb(f"{name}{i}", shape, dt) for i in range(n)]
            self.b = [Buf() for i in range(n)]
            self.c = 0

        def next(self):
            i = self.c % len(self.t)
            self.c += 1
            return self.t[i], self.b[i]

    NR = {}

    def alloc_norm(gsrc):
        NR["xb"] = Rot("xb", [128, D], F32, 2)
        NR["hb"] = Rot("hb", [128, D], BF16, 2)
        NR["st"] = Rot("st", [128, 4], F32, 2)
        NR["g"] = sb("gbc", [128, D]); NR["Bg"] = Buf()
        dma(SP, [lambda e: e.dma_start(out=NR["g"][:], in_=gsrc.partition_broadcast(128))], writes=[NR["Bg"]])

    def norm_rows(src, rows, src_sb=None):
        gbc, Bg = NR["g"], NR["Bg"]
        hb, Bh = NR["hb"].next()
        st, Bs = NR["st"].next()
        if src_sb is None:
            xt, Bx = NR["xb"].next()
            dma(SP, [lambda e: e.dma_start(out=xt[0:rows, :], in_=src)], writes=[Bx])
        else:
            xt, Bx = src_sb
        junk, Bjunk = hb, Bh
        op(DVE, lambda e: e.memset(st[0:rows, 0:1], 0.0), writes=[Bs])
        op(ACT, lambda e: e.activation(out=junk[0:rows, :], in_=xt[0:rows, :], func=AF.Square, accum_out=st[0:rows, 0:1]),
           reads=[Bx, Bs], writes=[Bjunk, Bs])
        op(DVE, lambda e: e.tensor_scalar(out=st[0:rows, 1:2], in0=st[0:rows, 0:1], scalar1=1.0 / D, scalar2=EPS,
                                          op0=ALU.mult, op1=ALU.add), reads=[Bs], writes=[Bs])
        op(ACT, lambda e: e.activation(out=st[0:rows, 2:3], in_=st[0:rows, 1:2], func=AF.Sqrt), reads=[Bs], writes=[Bs])
        op(DVE, lambda e: e.reciprocal(out=st[0:rows, 3:4], in_=st[0:rows, 2:3]), reads=[Bs], writes=[Bs])
        op(DVE, lambda e: e.scalar_tensor_tensor(out=hb[0:rows, :], in0=xt[0:rows, :], scalar=st[0:rows, 3:4],
                                                 in1=gbc[0:rows, :], op0=ALU.mult, op1=ALU.mult),
           reads=[Bx, Bs, Bg], writes=[Bh])
        return hb, Bh, xt, Bx

    def transpose_rows(hb, Bh, rows, dst_fn, Bdst):
        for half in range(2):
            pt, Bp = next_pt()
            fns = []
            for q8 in range(8):
                kc = half * 8 + q8
                fns.append(lambda e, kc=kc, q8=q8: e.transpose(out=pt[:, q8 * 128:q8 * 128 + rows],
                                                               in_=hb[0:rows, kc * 128:(kc + 1) * 128],
                                                               identity=ident_b[0:rows, 0:rows]))
            pe_group(fns, reads=[Bh, B_ident_b], writes=[Bp])
            src = pt[:].rearrange("p (a b) -> p a b", a=8)[:, :, 0:rows]
            op(ACT, lambda e: e.copy(out=dst_fn(half * 8, 8), in_=src), reads=[Bp], writes=[Bdst])

    push_scope()
    hT_all = sb("hT_all", [128, 16, SEQ], BF16); B_hT = Buf()
    alloc_ws(2)
    push_scope()
    alloc_norm(g1)
    for blk in range(NB):
        hb, Bh, _, _ = norm_rows(x_all[blk * 128:(blk + 1) * 128, :], 128)
        transpose_rows(hb, Bh, 128, lambda kc0, n, blk=blk: hT_all[:, kc0:kc0 + n, blk * 128:(blk + 1) * 128], B_hT)
    pop_scope()
    push_scope()

    stg_r = Rot("stg", [128, 512], BF16, 2)
    push_scope()
    sq_r = Rot("sq", [128, 512], BF16, 1)
    rs_r = Rot("rs", [128, 512], F32, 1)

    def headnorm_T(ps, Bp, gcol, Bgc, outt, Bout, post_scale):
        sq, Bsq = sq_r.next()
        op(ACT, lambda e: e.activation(out=sq[:], in_=ps[:], func=AF.Square), reads=[Bp], writes=[Bsq])
        ps2, Bp2 = next_ps()
        pe_group([lambda e: e.matmul(ps2[:], ones_b[:], sq[:], start=True, stop=True)], reads=[Bsq, B_ones_b], writes=[Bp2])
        rs, Brs = rs_r.next()
        op(DVE, lambda e: e.tensor_scalar(out=rs[:], in0=ps2[:], scalar1=1.0 / 128, scalar2=EPS, op0=ALU.mult, op1=ALU.add),
           reads=[Bp2], writes=[Brs])
        op(ACT, lambda e: e.activation(out=rs[:], in_=rs[:], func=AF.Sqrt), reads=[Brs], writes=[Brs])
        op(DVE, lambda e: e.reciprocal(out=rs[:], in_=rs[:]), reads=[Brs], writes=[Brs])
        if post_scale != 1.0:
            op(DVE, lambda e: e.tensor_scalar(out=rs[:], in0=rs[:], scalar1=post_scale, scalar2=None, op0=ALU.mult),
               reads=[Brs], writes=[Brs])
        op(DVE, lambda e: e.scalar_tensor_tensor(out=outt, in0=ps[:], scalar=gcol, in1=rs[:], op0=ALU.mult, op1=ALU.mult),
           reads=[Bp, Brs, Bgc], writes=[Bout])

    wk, Bwk = load_w(w_in, [(O_K, 512)])
    for T in range(8):
        for g in range(4):
            ps, Bp = next_ps()
            pe_group([lambda e, kc=kc: e.matmul(ps[:], wk[:, kc, g * 128:(g + 1) * 128], hT_all[:, kc, T * 512:(T + 1) * 512],
                                                start=(kc == 0), stop=(kc == 15)) for kc in range(16)],
                     reads=[Bwk, B_hT], writes=[Bp])
            stg, Bstg = stg_r.next()
            headnorm_T(ps, Bp, gk_t[:, 0:1], B_gqk, stg[:], Bstg, 1.0)
            dma(SP, [lambda e: e.dma_start(out=kT_d[:, g, T * 512:(T + 1) * 512], in_=stg[:])], reads=[Bstg])
    wv, Bwv = load_w(w_in, [(O_V, 512)])
    for blk in range(NB):
        ps, Bp = next_ps()
        pe_group([lambda e, kc=kc: e.matmul(ps[:], hT_all[:, kc, blk * 128:(blk + 1) * 128], wv[:, kc, 0:512],
                                            start=(kc == 0), stop=(kc == 15)) for kc in range(16)],
                 reads=[Bwv, B_hT], writes=[Bp])
        stg, Bstg = stg_r.next()
        op(ACT, lambda e: e.copy(out=stg[:], in_=ps[:]), reads=[Bp], writes=[Bstg])
        dma(SP, [lambda e: e.dma_start(out=v_d[:, blk, :], in_=stg[:])], reads=[Bstg])
    wki, Bwki = load_w(w_in, [(O_KI, 64), (O_KI, 64)])
    for T in range(8):
        ps, Bp = next_ps()
        pe_group([lambda e, kc=kc: e.matmul(ps[:], wki[:, kc, 0:128], hT_all[:, kc, T * 512:(T + 1) * 512],
                                            start=(kc == 0), stop=(kc == 15)) for kc in range(16)],
                 reads=[Bwki, B_hT], writes=[Bp])
        stg, Bstg = stg_r.next()
        op(ACT, lambda e: e.copy(out=stg[:], in_=ps[:]), reads=[Bp], writes=[Bstg])
        dma(SP, [lambda e: e.dma_start(out=kiT_d[:, T * 512:(T + 1) * 512], in_=stg[:])], reads=[Bstg])

    wdt, Bwdt = load_w(w_in, [(O_DT, 64)])
    dts_r = Rot("dts", [128, 64], F32, 2)
    B_dtraw = Buf()
    for blk in range(NB):
        ps, Bp = next_ps()
        pe_group([lambda e, kc=kc: e.matmul(ps[:, 0:64], hT_all[:, kc, blk * 128:(blk + 1) * 128], wdt[:, kc, 0:64],
                                            start=(kc == 0), stop=(kc == 15)) for kc in range(16)],
                 reads=[Bwdt, B_hT], writes=[Bp])
        dts, Bdts = dts_r.next()
        op(DVE, lambda e: e.tensor_copy(out=dts[:], in_=ps[:, 0:64]), reads=[Bp], writes=[Bdts])
        dma(SP, [lambda e: e.dma_start(out=dtraw_d[blk], in_=dts[:])], reads=[Bdts], writes=[B_dtraw])
    pop_scope()
    dtr_r = Rot("dtr", [128, 4, 8], F32, 2)

    pre_r = Rot("pre", [128, 5, 2, 516], BF16, 2)
    xc_r = Rot("xc", [128, 5, 512], BF16, 1)
    diag = sb("diag", [128, 5, 4, 128], BF16); B_diag = Buf()
    Sst = sb("Sst", [128, 512], F32); B_S = Buf()
    Ssel = sb("Ssel", [128, 512], F32); B_Ssel = Buf()
    dtw_r = Rot("dtw", [128, 8, 32], F32, 2)
    xw_r = Rot("xw", [128, 512], BF16, 2)
    bt_r = Rot("bt", [128, 128], BF16, 2)

    def conv_chunk(pre, Bpre, i, c, width, outt, Bout):
        ps, Bp = next_ps()
        def tap(kk):
            return pre[:, i, 0, 1 + kk:1 + kk + width] if kk % 2 == 1 else pre[:, i, 1, kk:kk + width]
        pe_group([lambda e, kk=kk: e.matmul(ps[:, 0:width], diag[:, i, kk, :], tap(kk),
                                            start=(kk == 0), stop=(kk == 3)) for kk in range(4)],
                 reads=[Bpre, B_diag], writes=[Bp])
        op(ACT, lambda e: e.activation(out=outt, in_=ps[:, 0:width], func=AF.Silu, bias=cb_t[:, c:c + 1]),
           reads=[Bp, B_cw], writes=[Bout])

    def build_diag(chunks):
        for i, c in enumerate(chunks):
            for kk in range(4):
                op(DVE, lambda e, i=i, c=c, kk=kk: e.tensor_scalar(out=diag[:, i, kk, :], in0=ident_f[:], scalar1=cw_t[:, c, kk:kk + 1],
                                                                 scalar2=None, op0=ALU.mult),
                   reads=[B_ident_f, B_cw], writes=[B_diag])

    ahl_r = Rot("ahl", [128, 2, 64], BF16, 2)

    def dt_front(src3, Bpd, g, nblk, dtw, Bdtw):
        n = nblk * 8
        v3 = lambda r: dtw[:, r, 0:n].rearrange("p (b h) -> p b h", h=8)
        bias3 = dtb_bc[:, g * 8:(g + 1) * 8].unsqueeze(1).broadcast_to([128, nblk, 8])
        A3 = A_bc[:, g * 8:(g + 1) * 8].unsqueeze(1).broadcast_to([128, nblk, 8])
        op(DVE, lambda e: e.tensor_tensor(out=v3(0), in0=src3, in1=bias3, op=ALU.add),
           reads=[Bpd, B_ssmc], writes=[Bdtw])
        op(ACT, lambda e: e.activation(out=dtw[:, 0, 0:n], in_=dtw[:, 0, 0:n], func=AF.Exp), reads=[Bdtw], writes=[Bdtw])
        op(ACT, lambda e: e.activation(out=dtw[:, 0, 0:n], in_=dtw[:, 0, 0:n], func=AF.Ln, bias=1.0), reads=[Bdtw], writes=[Bdtw])
        op(DVE, lambda e: e.tensor_tensor(out=v3(1), in0=v3(0), in1=A3, op=ALU.mult), reads=[Bdtw, B_ssmc], writes=[Bdtw])
        ahl, Bahl = ahl_r.next()
        op(DVE, lambda e: e.tensor_copy(out=ahl[:, 0, 0:n], in_=dtw[:, 1, 0:n]), reads=[Bdtw], writes=[Bahl])
        op(DVE, lambda e: e.tensor_tensor(out=ahl[:, 1, 0:n], in0=dtw[:, 1, 0:n], in1=ahl[:, 0, 0:n], op=ALU.subtract),
           reads=[Bdtw, Bahl], writes=[Bahl])
        return ahl, Bahl

    def dt_back(nblk, dtw, Bdtw, ahl, Bahl):
        n = nblk * 8
        pc, Bpc = next_ps()
        pe_group([lambda e: e.matmul(pc[:, 0:n], tri_b[:], ahl[:, 0, 0:n], start=True, stop=False),
                  lambda e: e.matmul(pc[:, 0:n], tri_b[:], ahl[:, 1, 0:n], start=False, stop=True),
                  lambda e: e.matmul(pc[:, 64:64 + n], ones_b[:], ahl[:, 0, 0:n], start=True, stop=False),
                  lambda e: e.matmul(pc[:, 64:64 + n], ones_b[:], ahl[:, 1, 0:n], start=False, stop=True)],
                 reads=[Bahl, B_tri, B_ones_b], writes=[Bpc])
        op(ACT, lambda e: e.copy(out=dtw[:, 2, 0:n], in_=pc[:, 0:n]), reads=[Bpc], writes=[Bdtw])
        op(DVE, lambda e: e.tensor_tensor(out=dtw[:, 3, 0:n], in0=pc[:, 64:64 + n], in1=dtw[:, 2, 0:n], op=ALU.subtract),
           reads=[Bpc, Bdtw], writes=[Bdtw])
        op(ACT, lambda e: e.activation(out=dtw[:, 3, 0:n], in_=dtw[:, 3, 0:n], func=AF.Exp), reads=[Bdtw], writes=[Bdtw])
        op(ACT, lambda e: e.activation(out=dtw[:, 4, 0:n], in_=pc[:, 64:64 + n], func=AF.Exp), reads=[Bpc], writes=[Bdtw])
        op(DVE, lambda e: e.tensor_tensor(out=dtw[:, 5, 0:n], in0=dtw[:, 0, 0:n], in1=dtw[:, 3, 0:n], op=ALU.mult),
           reads=[Bdtw], writes=[Bdtw])
        op(ACT, lambda e: e.activation(out=dtw[:, 6, 0:n], in_=dtw[:, 2, 0:n], func=AF.Exp), reads=[Bdtw], writes=[Bdtw])


    def dt_math(src3, Bpd, g, nblk, dtw, Bdtw):
        ahl, Bahl = dt_front(src3, Bpd, g, nblk, dtw, Bdtw)
        dt_back(nblk, dtw, Bdtw, ahl, Bahl)

    if STAGE >= 2:
        for g in range(8):
            wx, Bwx = load_w(w_in, [(O_X + g * 512, 512)])
            wb, Bwb = load_w(w_in, [(O_B + g * 128, 128)])
            chunks = [4 * g + i for i in range(4)] + [32 + g]
            build_diag(chunks)
            op(DVE, lambda e: e.memset(Sst[:], 0.0), writes=[B_S])
            pre_prev_box = [None]
            dtctx = {}

            def stageA(T):
                    dtr, Bdtr = dtr_r.next()
                    with nc.allow_non_contiguous_dma(reason="small dt slices"):
                        dma(SP, [lambda e: e.dma_start(out=dtr[:], in_=dtraw_d[T * 4:(T + 1) * 4, :, g * 8:(g + 1) * 8].rearrange("b p h -> p b h"))],
                            reads=[B_dtraw], writes=[Bdtr])
                    dtw, Bdtw = dtw_r.next()
                    ahl, Bahl = dt_front(dtr[:], Bdtr, g, 4, dtw, Bdtw)
                    dtctx[T] = (dtw, Bdtw, ahl, Bahl)
                    pre, Bpre = pre_r.next()
                    if T == 0:
                        op(DVE, lambda e: e.memset(pre[:, :, :, 0:4], 0.0), writes=[Bpre])
                    else:
                        pp, Bpp = pre_prev_box[0]
                        op(DVE, lambda e: e.tensor_copy(out=pre[:, :, 0, 0:4], in_=pp[:, :, 0, 512:516]), reads=[Bpp], writes=[Bpre])
                        op(DVE, lambda e: e.tensor_copy(out=pre[:, :, 1, 0:4], in_=pp[:, :, 1, 512:516]), reads=[Bpp], writes=[Bpre])
                    for i in range(5):
                        ps, Bp = next_ps()
                        wt, Bwt = (wx, Bwx) if i < 4 else (wb, Bwb)
                        c0 = i * 128 if i < 4 else 0
                        pe_group([lambda e, kc=kc: e.matmul(ps[:], wt[:, kc, c0:c0 + 128], hT_all[:, kc, T * 512:(T + 1) * 512],
                                                            start=(kc == 0), stop=(kc == 15)) for kc in range(16)],
                                 reads=[Bwt, B_hT], writes=[Bp])
                        op(ACT, lambda e: e.copy(out=pre[:, i, 0, 4:516], in_=ps[:]), reads=[Bp], writes=[Bpre])
                        op(DVE, lambda e: e.tensor_copy(out=pre[:, i, 1, 3:515], in_=ps[:]), reads=[Bp], writes=[Bpre])
                    pre_prev_box[0] = (pre, Bpre)
                    return pre, Bpre

            def tr_step(T, r, xc, Bxc, dtw, Bdtw):
                pt, Bpt = next_pt()
                pe_group([lambda e, i=i: e.transpose(out=pt[:, i * 128:(i + 1) * 128], in_=xc[:, i, r * 128:(r + 1) * 128],
                                                     identity=ident_b[:]) for i in range(5)],
                         reads=[Bxc, B_ident_b], writes=[Bpt])
                xw, Bxw = xw_r.next()
                bt, Bbt = bt_r.next()
                sc3 = dtw[:, 5, r * 8:(r + 1) * 8].unsqueeze(2).broadcast_to([128, 8, 64])
                op(DVE, lambda e: e.tensor_tensor(out=xw[:].rearrange("p (h d) -> p h d", h=8),
                                                  in0=pt[:, 0:512].rearrange("p (h d) -> p h d", h=8), in1=sc3, op=ALU.mult),
                   reads=[Bpt, Bdtw], writes=[Bxw])
                op(ACT, lambda e: e.copy(out=bt[:], in_=pt[:, 512:640]), reads=[Bpt], writes=[Bbt])
                return xw, Bxw, bt, Bbt

            def st_step(T, r, xw, Bxw, bt, Bbt, dtw, Bdtw):
                if r == 0:
                    op(DVE, lambda e: e.tensor_scalar(out=Ssel[:], in0=Sst[:], scalar1=flags_t[:, 0:1], scalar2=None, op0=ALU.mult),
                       reads=[B_S, B_flags], writes=[B_Ssel])
                else:
                    op(DVE, lambda e: e.scalar_tensor_tensor(out=Ssel[:], in0=Sst[:], scalar=flags_t[:, r:r + 1], in1=Ssel[:],
                                                             op0=ALU.mult, op1=ALU.add), reads=[B_S, B_flags, B_Ssel], writes=[B_Ssel])
                ps, Bp = next_ps()
                pe_group([lambda e: e.matmul(ps[:], bt[:], xw[:], start=True, stop=True)], reads=[Bbt, Bxw], writes=[Bp])
                cd3 = dtw[:, 4, r * 8:(r + 1) * 8].unsqueeze(2).broadcast_to([128, 8, 64])
                op(DVE, lambda e: e.tensor_tensor(out=Sst[:].rearrange("p (h d) -> p h d", h=8),
                                                  in0=Sst[:].rearrange("p (h d) -> p h d", h=8), in1=cd3, op=ALU.mult),
                   reads=[B_S, Bdtw], writes=[B_S])
                op(DVE, lambda e: e.tensor_tensor(out=Sst[:], in0=Sst[:], in1=ps[:], op=ALU.add), reads=[B_S, Bp], writes=[B_S])

            def stageB1(T, pre, Bpre):
                xc, Bxc = xc_r.next()
                for i in range(5):
                    conv_chunk(pre, Bpre, i, chunks[i], 512, xc[:, i, :], Bxc)
                dtw, Bdtw, ahl, Bahl = dtctx.pop(T)
                dt_back(4, dtw, Bdtw, ahl, Bahl)
                t0 = tr_step(T, 0, xc, Bxc, dtw, Bdtw)
                t1 = tr_step(T, 1, xc, Bxc, dtw, Bdtw)
                return xc, Bxc, dtw, Bdtw, t0, t1

            def stageB2(T, xc, Bxc, dtw, Bdtw, t0, t1):
                st_step(T, 0, *t0, dtw, Bdtw)
                t2 = tr_step(T, 2, xc, Bxc, dtw, Bdtw)
                st_step(T, 1, *t1, dtw, Bdtw)
                t3 = tr_step(T, 3, xc, Bxc, dtw, Bdtw)
                st_step(T, 2, *t2, dtw, Bdtw)
                st_step(T, 3, *t3, dtw, Bdtw)
                stg, Bstg = stg_r.next()
                op(DVE, lambda e: e.tensor_copy(out=stg[:], in_=Ssel[:]), reads=[B_Ssel], writes=[Bstg])
                dma(SP, [lambda e: e.dma_start(out=ssave_d[T, :, g * 512:(g + 1) * 512], in_=stg[:])], reads=[Bstg])

            pendA = stageA(0)
            for T in range(8):
                ctxB = stageB1(T, *pendA)
                pendA = stageA(T + 1) if T + 1 < 8 else None
                stageB2(T, *ctxB)

    pop_scope()
    pop_scope()
    def build_hT_own():
        hT = sb("hT_own", [128, 16, OWN, 132], BF16); B = Buf()
        op(DVE, lambda e: e.memset(hT[:, :, :, 0:1], 0.0), writes=[B])
        push_scope()
        alloc_norm(g1)
        for m in range(OWN):
            hb, Bh, _, _ = norm_rows(x_own[m], 128)
            transpose_rows(hb, Bh, 128, lambda kc0, n, m=m: hT[:, kc0:kc0 + n, m, 4:132], B)
        hb, Bh, _, _ = norm_rows(x_halo, OWN * HALO)
        for half in range(2):
            pt, Bp = next_pt()
            pe_group([lambda e, q8=q8: e.transpose(out=pt[:, q8 * 128:q8 * 128 + 24], in_=hb[0:24, (half * 8 + q8) * 128:(half * 8 + q8 + 1) * 128],
                                                   identity=ident_b[0:24, 0:24]) for q8 in range(8)], reads=[Bh, B_ident_b], writes=[Bp])
            for m in range(OWN):
                src = pt[:].rearrange("p (a b) -> p a b", a=8)[:, :, m * 3:m * 3 + 3]
                op(DVE, lambda e: e.tensor_copy(out=hT[:, half * 8:half * 8 + 8, m, 1:4], in_=src), reads=[Bp], writes=[B])
        pop_scope()
        return hT, B

    push_scope()
    qT = sb("qT", [128, 16, OWN * 128], BF16); B_qT = Buf()
    qiT = sb("qiT", [128, 8, OWN * 128], BF16); B_qiT = Buf()
    wtok = sb("wtok", [128, OWN, 16]); B_wtok = Buf()
    push_scope()
    hT_own, B_hTo = build_hT_own()
    alloc_ws(2)
    sq_r = Rot("sq", [128, 512], BF16, 1)
    rs_r = Rot("rs", [128, 512], F32, 1)

    def own_rhs(kc, half):
        return hT_own[:, kc, 4 * half:4 * half + 4, 4:132]

    for hq in range(4):
        wq, Bwq = load_w(w_in, [(O_Q + hq * 512, 512)])
        for hh in range(4):
            h = hq * 4 + hh
            for half in range(2):
                ps, Bp = next_ps()
                pe_group([lambda e, kc=kc: e.matmul(ps[:], wq[:, kc, hh * 128:(hh + 1) * 128], own_rhs(kc, half),
                                                    start=(kc == 0), stop=(kc == 15)) for kc in range(16)],
                         reads=[Bwq, B_hTo], writes=[Bp])
                headnorm_T(ps, Bp, gq_t[:, 0:1], B_gqk, qT[:, h, half * 512:(half + 1) * 512], B_qT, 128.0 ** -0.5)
    for c2 in range(2):
        wqi, Bwqi = load_w(w_in, [(O_QI + c2 * 512, 512)])
        for cc in range(4):
            for half in range(2):
                ps, Bp = next_ps()
                pe_group([lambda e, kc=kc: e.matmul(ps[:], wqi[:, kc, cc * 128:(cc + 1) * 128], own_rhs(kc, half),
                                                    start=(kc == 0), stop=(kc == 15)) for kc in range(16)],
                         reads=[Bwqi, B_hTo], writes=[Bp])
                op(ACT, lambda e: e.activation(out=qiT[:, c2 * 4 + cc, half * 512:(half + 1) * 512], in_=ps[:], func=AF.Copy, scale=0.125),
                   reads=[Bp], writes=[B_qiT])
    ww, Bww = load_w(w_in, [(O_WI, 16)])
    for m in range(OWN):
        ps, Bp = next_ps()
        pe_group([lambda e, kc=kc: e.matmul(ps[:, 0:16], hT_own[:, kc, m, 4:132], ww[:, kc, 0:16],
                                            start=(kc == 0), stop=(kc == 15)) for kc in range(16)],
                 reads=[Bww, B_hTo], writes=[Bp])
        op(DVE, lambda e: e.tensor_scalar(out=wtok[:, m, :], in0=ps[:, 0:16], scalar1=0.25, scalar2=None, op0=ALU.mult),
           reads=[Bp], writes=[B_wtok])
    if DEBUG:
        dma(SP, [lambda e: e.dma_start(out=dbg["qT"], in_=qT[:])], reads=[B_qT])
        dma(SP, [lambda e: e.dma_start(out=dbg["qiT"], in_=qiT[:])], reads=[B_qiT])
        dma(SP, [lambda e: e.dma_start(out=dbg["wtok"], in_=wtok[:])], reads=[B_wtok])
    pop_scope()

    kT = sb("kT", [128, 4, SEQ], BF16); B_kT = Buf()
    Vs = sb("Vs", [128, NB, 512], BF16); B_V = Buf()
    kiT = sb("kiT", [128, SEQ], BF16); B_kiT = Buf()
    dma(SP, [lambda e: e.dma_start(out=kT[:], in_=kT_d)], writes=[B_kT])
    dma(SP, [lambda e: e.dma_start(out=Vs[:], in_=v_d)], writes=[B_V])
    dma(SP, [lambda e: e.dma_start(out=kiT[:], in_=kiT_d)], writes=[B_kiT])
    EB = sb("EB", [128, 5, 16, 128], BF16); B_EB = Buf()
    push_scope()
    relb_t = sb("relb_t", [32, 16]); ohb_t = sb("ohb_t", [32, 1280]); B_rb = Buf()
    rbh = sb("rbh", [32, 2, 16], BF16); ohb_b = sb("ohb_b", [32, 1280], BF16)
    Fv = sb("Fv", [16, 1280]); B_Fv = Buf()
    EBf = sb("EBf", [128, 16, 128]); B_EBf = Buf()
    dma(SP, [lambda e: e.dma_start(out=relb_t[:], in_=relb), lambda e: e.dma_start(out=ohb_t[:], in_=c_ohb)], writes=[B_rb])
    op(DVE, lambda e: e.tensor_copy(out=rbh[:, 0, :], in_=relb_t[:]), reads=[B_rb], writes=[B_rb])
    op(DVE, lambda e: e.tensor_tensor(out=rbh[:, 1, :], in0=relb_t[:], in1=rbh[:, 0, :], op=ALU.subtract), reads=[B_rb], writes=[B_rb])
    op(DVE, lambda e: e.tensor_copy(out=ohb_b[:], in_=ohb_t[:]), reads=[B_rb], writes=[B_rb])
    for c3 in range(3 if STAGE >= 4 else 0):
        n0, n1 = c3 * 512, min(1280, c3 * 512 + 512)
        ps, Bp = next_ps()
        pe_group([lambda e: e.matmul(ps[0:16, 0:n1 - n0], rbh[:, 0, :], ohb_b[:, n0:n1], start=True, stop=False),
                  lambda e: e.matmul(ps[0:16, 0:n1 - n0], rbh[:, 1, :], ohb_b[:, n0:n1], start=False, stop=True)],
                 reads=[B_rb], writes=[Bp])
        op(ACT, lambda e: e.activation(out=Fv[:, n0:n1], in_=ps[0:16, 0:n1 - n0], func=AF.Exp), reads=[Bp], writes=[B_Fv])
    for kb in range(5 if STAGE >= 4 else 0):
        for r0 in range(0, 128, 32):
            src = Fv[:, kb * 256:(kb + 1) * 256].unsqueeze(1).broadcast_to([16, 32, 256])
            dma(SP, [lambda e: e.dma_start(out=ebz_d[:, kb, r0:r0 + 32, :], in_=src)], reads=[B_Fv], writes=[B_EBf])
    barrier()
    for kb in range(5 if STAGE >= 4 else 0):
        srcs = []
        for h in range(16):
            base = ebz_d[h, kb]
            srcs.append(bass.AP(tensor=base.tensor, offset=base.offset + 127, ap=[[255, 128], [1, 128]]))
        dma(SP, [lambda e, h=h: e.dma_start(out=EBf[:, h, :], in_=srcs[h]) for h in range(16)], reads=[B_EBf], writes=[B_EBf])
        op(DVE, lambda e: e.tensor_copy(out=EB[:, kb, :, :], in_=EBf[:]), reads=[B_EBf], writes=[B_EB])
    pop_scope()

    score = sb("score", [128, SEQ]); B_score = Buf()
    sel01 = sb("sel01", [128, SEQ], BF16); B_sel = Buf()
    selT = sb("selT", [128, NB, 128], BF16); B_selT = Buf()
    Dg = sb("Dg", [128, 16, 128], BF16); B_Dg = Buf()
    amask_t = sb("amask_t", [128, 512]); B_am = Buf()
    bs = sb("bs", [128, 16]); B_bs = Buf()
    half_c = sb("half_c", [128, 1]); B_hc = Buf()
    R_r = Rot("Rr", [128, 512], BF16, 3)
    E_r = Rot("Er", [128, 512], BF16, 2)
    P_r = Rot("Pr", [128, 512], BF16, 3)
    rec_r = Rot("rec", [128, 512], F32, 1)
    ao_r = Rot("ao", [128, 4, 128], BF16, 2)
    dma(SP, [lambda e: e.dma_start(out=amask_t[:], in_=c_amask)], writes=[B_am])
    p2t = sb("p2t", [128, 32]); B_p2 = Buf()
    dk = sb("dk", [128, 32]); B_dk = Buf()
    cntt = sb("cntt", [128, 32]); B_cnt = Buf()
    for kk_ in range(32):
        op(DVE, lambda e, kk_=kk_: e.memset(p2t[:, kk_:kk_ + 1], 2.0 ** -(kk_ + 1)), writes=[B_p2])
    op(DVE, lambda e: e.memset(half_c[:], 0.5), writes=[B_hc])
    psn[0] = 4
    NIT = 24

    def sc_init(m):
        nkb = 4 * (m + 1)
        Lk = nkb * 128
        for h in range(16):
            op(DVE, lambda e, h=h: e.tensor_scalar(out=Dg[:, h, :], in0=ident_f[:], scalar1=wtok[:, m, h:h + 1], scalar2=None, op0=ALU.mult),
               reads=[B_ident_f, B_wtok], writes=[B_Dg])
        for kt in range(m + 1):
            scp, Bscp = PS[4 + kt % 2], BPS[4 + kt % 2]

            def sc_front(h):
                po = (h % 2) * 64
                ps, Bp = next_ps()
                pe_group([lambda e: e.matmul(ps[:], qiT[po:po + 64, h // 2, m * 128:(m + 1) * 128], kiT[po:po + 64, kt * 512:(kt + 1) * 512],
                                             start=True, stop=True)], reads=[B_qiT, B_kiT], writes=[Bp])
                R, BR = R_r.next()
                op(ACT, lambda e: e.activation(out=R[:], in_=ps[:], func=AF.Relu), reads=[Bp], writes=[BR])
                return R, BR

            pend = {h: sc_front(h) for h in range(2)}
            for h in range(16):
                R, BR = pend.pop(h)
                if h + 2 < 16:
                    pend[h + 2] = sc_front(h + 2)
                pe_group([lambda e: e.matmul(scp[:], Dg[:, h, :], R[:], start=(h == 0), stop=(h == 15))], reads=[B_Dg, BR], writes=[Bscp])
            if kt < m:
                op(ACT, lambda e: e.copy(out=score[:, kt * 512:(kt + 1) * 512], in_=scp[:]), reads=[Bscp], writes=[B_score])
            else:
                op(DVE, lambda e: e.tensor_tensor(out=score[:, kt * 512:(kt + 1) * 512], in0=scp[:], in1=amask_t[:], op=ALU.add),
                   reads=[Bscp, B_am], writes=[B_score])
                jk, Bjk = rec_r.next()
                op(DVE, lambda e: e.tensor_tensor(out=jk[:], in0=scp[:], in1=amask_t[:], op=ALU.subtract), reads=[Bscp, B_am], writes=[Bjk])
                op(DVE, lambda e: e.tensor_reduce(out=bs[:, 0:1], in_=jk[:], axis=AX.X, op=ALU.min), reads=[Bjk], writes=[B_bs])
        op(DVE, lambda e: e.tensor_reduce(out=bs[:, 1:2], in_=score[:, 0:Lk], axis=AX.X, op=ALU.max), reads=[B_score], writes=[B_bs])
        if m > 0:
            op(DVE, lambda e: e.tensor_reduce(out=bs[:, 2:3], in_=score[:, 0:Lk - 512], axis=AX.X, op=ALU.min), reads=[B_score], writes=[B_bs])
            op(DVE, lambda e: e.tensor_tensor(out=bs[:, 0:1], in0=bs[:, 0:1], in1=bs[:, 2:3], op=ALU.min), reads=[B_bs], writes=[B_bs])
        op(DVE, lambda e: e.tensor_scalar(out=bs[:, 3:4], in0=bs[:, 0:1], scalar1=-1.0, scalar2=None, op0=ALU.add), reads=[B_bs], writes=[B_bs])
        op(DVE, lambda e: e.scalar_tensor_tensor(out=bs[:, 4:5], in0=bs[:, 1:2], scalar=1.0, in1=bs[:, 3:4], op0=ALU.add, op1=ALU.subtract),
           reads=[B_bs], writes=[B_bs])
        op(DVE, lambda e: e.tensor_scalar(out=dk[:, 0:NIT + 1], in0=p2t[:, 0:NIT + 1], scalar1=bs[:, 4:5], scalar2=None, op0=ALU.mult),
           reads=[B_bs, B_p2], writes=[B_dk])
        op(DVE, lambda e: e.memset(cntt[:], 0.0), writes=[B_cnt])
        op(DVE, lambda e: e.tensor_tensor(out=bs[:, 6:7], in0=bs[:, 3:4], in1=dk[:, 0:1], op=ALU.add), reads=[B_bs, B_dk], writes=[B_bs])
        jkb, Bjkb = sel01, B_sel

    def bis_iter(m, it):
        Lk = 512 * (m + 1)
        jkb, Bjkb = sel01, B_sel
        op(DVE, lambda e: e.tensor_scalar(out=jkb[:, 0:Lk], in0=score[:, 0:Lk], scalar1=bs[:, 6:7], scalar2=0.0,
                                          op0=ALU.is_ge, op1=ALU.add, accum_out=cntt[:, it:it + 1]),
           reads=[B_score, B_bs], writes=[Bjkb, B_cnt])
        op(DVE, lambda e: e.scalar_tensor_tensor(out=bs[:, 7:8], in0=cntt[:, it:it + 1], scalar=255.5, in1=dk[:, it:it + 1],
                                                 op0=ALU.is_ge, op1=ALU.mult), reads=[B_cnt, B_dk], writes=[B_bs])
        if it < NIT - 1:
            op(DVE, lambda e: e.scalar_tensor_tensor(out=bs[:, 6:7], in0=bs[:, 3:4], scalar=bs[:, 7:8], in1=dk[:, it + 1:it + 2],
                                                     op0=ALU.add, op1=ALU.add), reads=[B_bs, B_dk], writes=[B_bs])
        op(DVE, lambda e: e.tensor_tensor(out=bs[:, 3:4], in0=bs[:, 3:4], in1=bs[:, 7:8], op=ALU.add), reads=[B_bs], writes=[B_bs])

    def bis_fin(m):
        nkb = 4 * (m + 1)
        Lk = nkb * 128
        op(DVE, lambda e: e.tensor_scalar(out=sel01[:, 0:Lk], in0=score[:, 0:Lk], scalar1=bs[:, 3:4], scalar2=None, op0=ALU.is_ge),
           reads=[B_score, B_bs], writes=[B_sel])
        if DEBUG:
            dma(SP, [lambda e: e.dma_start(out=dbg["score"][m, :, 0:Lk], in_=score[:, 0:Lk])], reads=[B_score])
            dma(SP, [lambda e: e.dma_start(out=dbg["thr"][:, m, :], in_=bs[:, 3:7])], reads=[B_bs])
        for k8 in range(0, nkb, 8):
            nn = min(8, nkb - k8)
            pt, Bp = next_pt()
            pe_group([lambda e, q=q: e.transpose(out=pt[:, q * 128:(q + 1) * 128], in_=sel01[:, (k8 + q) * 128:(k8 + q + 1) * 128],
                                                 identity=ident_b[:]) for q in range(nn)], reads=[B_sel, B_ident_b], writes=[Bp])
            op(ACT, lambda e: e.copy(out=selT[:, k8:k8 + nn, :], in_=pt[:, 0:nn * 128].rearrange("p (a b) -> p a b", b=128)),
               reads=[Bp], writes=[B_selT])

    def attention(m, hook):
        nkb = 4 * (m + 1)
        for g4 in range(4 if STAGE >= 6 else 0):
            op_ps, Bop = PS[4], BPS[4]
            sm_ps, Bsm = PS[5], BPS[5]
            def att_front(kb):
                ps, Bp = next_ps()
                pe_group([lambda e: e.matmul(ps[:], kT[:, g4, kb * 128:(kb + 1) * 128], qT[:, 4 * g4:4 * g4 + 4, m * 128:(m + 1) * 128],
                                             start=True, stop=True)], reads=[B_kT, B_qT], writes=[Bp])
                E, BE = E_r.next()
                op(ACT, lambda e: e.activation(out=E[:], in_=ps[:], func=AF.Exp), reads=[Bp], writes=[BE])
                P, BP = P_r.next()
                selb = selT[:, kb, :].unsqueeze(1).broadcast_to([128, 4, 128])
                op(DVE, lambda e: e.tensor_tensor(out=P[:].rearrange("p (r t) -> p r t", r=4), in0=E[:].rearrange("p (r t) -> p r t", r=4),
                                                  in1=selb, op=ALU.mult), reads=[BE, B_selT], writes=[BP])
                kbrel = kb - (nkb - 5)
                if kbrel >= 0:
                    op(DVE, lambda e: e.tensor_tensor(out=P[:].rearrange("p (r t) -> p r t", r=4), in0=P[:].rearrange("p (r t) -> p r t", r=4),
                                                      in1=EB[:, kbrel, 4 * g4:4 * g4 + 4, :], op=ALU.mult), reads=[BP, B_EB], writes=[BP])
                return P, BP

            pend = {kb: att_front(kb) for kb in range(min(2, nkb))}
            for kb in range(nkb):
                P, BP = pend.pop(kb)
                if kb + 2 < nkb:
                    pend[kb + 2] = att_front(kb + 2)
                pe_group([lambda e: e.matmul(op_ps[:], Vs[:, kb, g4 * 128:(g4 + 1) * 128], P[:], start=(kb == 0), stop=(kb == nkb - 1))],
                         reads=[B_V, BP], writes=[Bop])
                pe_group([lambda e: e.matmul(sm_ps[:], ones_b[:], P[:], start=(kb == 0), stop=(kb == nkb - 1))],
                         reads=[B_ones_b, BP], writes=[Bsm])
                hook()
            if STAGE < 8:
                continue
            rec, Brec = rec_r.next()
            op(ACT, lambda e: e.copy(out=rec[:], in_=sm_ps[:]), reads=[Bsm], writes=[Brec])
            op(DVE, lambda e: e.reciprocal(out=rec[:], in_=rec[:]), reads=[Brec], writes=[Brec])
            ao, Bao = ao_r.next()
            op(DVE, lambda e: e.tensor_tensor(out=ao[:].rearrange("p r t -> p (r t)"), in0=op_ps[:], in1=rec[:], op=ALU.mult),
               reads=[Bop, Brec], writes=[Bao])
            dma(SP, [lambda e: e.dma_start(out=attT_d[:, 4 * g4:4 * g4 + 4, m * 128:(m + 1) * 128], in_=ao[:])], reads=[Bao])

    if STAGE >= 5:
        sc_init(0)
        for it in range(NIT):
            bis_iter(0, it)
        bis_fin(0)
        for m in range(OWN):
            todo = []
            if m + 1 < OWN:
                sc_init(m + 1)
                todo = list(range(NIT))

            def hook():
                if todo:
                    bis_iter(m + 1, todo.pop(0))

            attention(m, hook)
            while todo:
                bis_iter(m + 1, todo.pop(0))
            if m + 1 < OWN:
                bis_fin(m + 1)
    psn[0] = 6
    pop_scope()

    push_scope()
    ynT_all = sb("ynT_all", [128, 32, OWN * 128], BF16); B_ynT = Buf()
    hT_own, B_hTo = build_hT_own()
    push_scope()
    alloc_ws(3)
    diag = sb("diag6", [128, 6, 4, 128], BF16); B_diag = Buf()
    negmT = sb("negmT", [128, 128]); B_negm = Buf()
    op(DVE, lambda e: e.tensor_scalar(out=negmT[:], in0=tri_f[:], scalar1=-1.0, scalar2=-NEG, op0=ALU.add, op1=ALU.mult),
       reads=[B_tri], writes=[B_negm])
    gsg = sb("gsg", [128, 512]); B_gsg = Buf()
    pre6 = sb("pre6", [128, 6, 2, 132], BF16); B_pre6 = Buf()
    xc6_r = Rot("xc6", [128, 6, 128], BF16, 2)
    zs_r = Rot("zs", [128, 512], F32, 2)
    dtw_r = Rot("dtwb", [128, 8, 32], F32, 2)
    ahl_r = Rot("ahlb", [128, 2, 64], BF16, 1)
    xd = sb("xd", [128, 512], BF16); B_xd = Buf()
    xdsk = sb("xdsk", [128, 512]); B_xdsk = Buf()
    Sg = sb("Sg", [128, 512], BF16); B_Sg = Buf()
    cbm = sb("cbm", [128, 128]); B_cbm = Buf()
    Rm = sb("Rm", [128, 8, 128]); B_Rm = Buf()
    Rhl = sb("Rhl", [128, 2, 8, 128], BF16); B_Rhl = Buf()
    seg = sb("seg", [128, 8, 128]); B_seg = Buf()
    eab = sb("eab", [128, 8, 128]); B_eab = Buf()
    Mt = sb("Mt", [128, 8, 128], BF16); B_Mt = Buf()
    CE = sb("CE", [128, 8, 128], BF16); B_CE = Buf()
    y3 = sb("y3", [128, 512]); B_y3 = Buf()
    ynb = sb("ynb", [128, 512], BF16); B_ynb = Buf()
    nst = sb("nst", [128, 16]); B_nst = Buf()
    dtown = sb("dtown", [128, OWN, 64]); B_dtown = Buf()
    wdt, Bwdt = load_w(w_in, [(O_DT, 64)])
    for m in range(OWN):
        ps, Bp = next_ps()
        pe_group([lambda e, kc=kc: e.matmul(ps[:, 0:64], hT_own[:, kc, m, 4:132], wdt[:, kc, 0:64],
                                            start=(kc == 0), stop=(kc == 15)) for kc in range(16)],
                 reads=[Bwdt, B_hTo], writes=[Bp])
        op(DVE, lambda e: e.tensor_copy(out=dtown[:, m, :], in_=ps[:, 0:64]), reads=[Bp], writes=[B_dtown])
    for g in range(8 if STAGE >= 9 else 0):
        wz, Bwz = load_w(w_in, [(O_Z + g * 512, 512)])
        wx, Bwx = load_w(w_in, [(O_X + g * 512, 512)])
        wbc, Bwbc = load_w(w_in, [(O_B + g * 128, 128), (O_C + g * 128, 128)])
        chunks = [4 * g + i for i in range(4)] + [32 + g, 40 + g]
        build_diag(chunks)
        dma(SP, [lambda e: e.dma_start(out=gsg[:], in_=gs[:, g * 512:(g + 1) * 512].partition_broadcast(128))], writes=[B_gsg])
        def stageA2(m):
                xc6, B_xc6 = xc6_r.next()
                zs, B_zs = zs_r.next()
                for i in range(6):
                    ps, Bp = next_ps()
                    wt, Bwt, c0 = (wx, Bwx, i * 128) if i < 4 else (wbc, Bwbc, (i - 4) * 128)
                    pe_group([lambda e, kc=kc: e.matmul(ps[:, 0:132], wt[:, kc, c0:c0 + 128], hT_own[:, kc, m, 0:132],
                                                        start=(kc == 0), stop=(kc == 15)) for kc in range(16)],
                             reads=[Bwt, B_hTo], writes=[Bp])
                    op(ACT, lambda e: e.copy(out=pre6[:, i, 0, 0:132], in_=ps[:, 0:132]), reads=[Bp], writes=[B_pre6])
                    op(DVE, lambda e: e.tensor_copy(out=pre6[:, i, 1, 0:131], in_=ps[:, 1:132]), reads=[Bp], writes=[B_pre6])
                for i in range(6):
                    conv_chunk(pre6, B_pre6, i, chunks[i], 128, xc6[:, i, :], B_xc6)
                if SUB < 11:
                    return
                ps, Bp = next_ps()
                pe_group([lambda e, kc=kc: e.matmul(ps[:], hT_own[:, kc, m, 4:132], wz[:, kc, 0:512],
                                                    start=(kc == 0), stop=(kc == 15)) for kc in range(16)],
                         reads=[Bwz, B_hTo], writes=[Bp])
                op(ACT, lambda e: e.activation(out=zs[:], in_=ps[:], func=AF.Silu), reads=[Bp], writes=[B_zs])
                dtw, Bdtw = dtw_r.next()
                dt_math(dtown[:, m:m + 1, g * 8:(g + 1) * 8], B_dtown, g, 1, dtw, Bdtw)
                return xc6, B_xc6, zs, B_zs, dtw, Bdtw

        def stageB2(m, xc6, B_xc6, zs, B_zs, dtw, Bdtw):
                if SUB < 12:
                    return
                id3 = ident_f[:].unsqueeze(1).broadcast_to([128, 8, 128])
                ac3 = dtw[:, 2, 0:8].unsqueeze(2).broadcast_to([128, 8, 128])
                op(DVE, lambda e: e.tensor_tensor(out=Rm[:], in0=id3, in1=ac3, op=ALU.mult), reads=[B_ident_f, Bdtw], writes=[B_Rm])
                op(DVE, lambda e: e.tensor_copy(out=Rhl[:, 0], in_=Rm[:]), reads=[B_Rm], writes=[B_Rhl])
                op(DVE, lambda e: e.tensor_tensor(out=Rhl[:, 1], in0=Rm[:], in1=Rhl[:, 0], op=ALU.subtract), reads=[B_Rm, B_Rhl], writes=[B_Rhl])
                pt, Bpt = next_pt()
                pe_group([lambda e, i=i: e.transpose(out=pt[:, i * 128:(i + 1) * 128], in_=xc6[:, i, :], identity=ident_b[:]) for i in range(4)],
                         reads=[B_xc6, B_ident_b], writes=[Bpt])
                dt3 = dtw[:, 0, 0:8].unsqueeze(2).broadcast_to([128, 8, 64])
                dk3 = dsk_bc[:, g * 8:(g + 1) * 8].unsqueeze(2).broadcast_to([128, 8, 64])
                pt3 = pt[:, 0:512].rearrange("p (h d) -> p h d", h=8)
                op(DVE, lambda e: e.tensor_tensor(out=xd[:].rearrange("p (h d) -> p h d", h=8), in0=pt3, in1=dt3, op=ALU.mult),
                   reads=[Bpt, Bdtw], writes=[B_xd])
                if SUB != 132:
                    op(DVE, lambda e: e.tensor_tensor(out=xdsk[:].rearrange("p (h d) -> p h d", h=8), in0=pt3, in1=dk3, op=ALU.mult),
                       reads=[Bpt, B_ssmc], writes=[B_xdsk])
                if SUB != 131:
                    dma(SP, [lambda e: e.dma_start(out=Sg[:], in_=ssave_d[m, :, g * 512:(g + 1) * 512])], writes=[B_Sg])
                if SUB < 13 or SUB in (131, 132):
                    return
                ps, Bp = next_ps()
                pe_group([lambda e: e.matmul(ps[:, 0:128], xc6[:, 4, :], xc6[:, 5, :], start=True, stop=True)], reads=[B_xc6], writes=[Bp])
                op(DVE, lambda e: e.tensor_tensor(out=cbm[:], in0=ps[:, 0:128], in1=tri_f[:], op=ALU.mult), reads=[Bp, B_tri], writes=[B_cbm])
                if SUB == 133:
                    return
                if SUB == 134:
                    return
                abc = []
                for hb2 in range(2):
                    pa, Bpa = next_ps()
                    pe_group([lambda e: e.matmul(pa[:], ones_b[:], Rhl[:, 0, 4 * hb2:4 * hb2 + 4, :], start=True, stop=False),
                              lambda e: e.matmul(pa[:], ones_b[:], Rhl[:, 1, 4 * hb2:4 * hb2 + 4, :], start=False, stop=True)],
                             reads=[B_Rhl, B_ones_b], writes=[Bpa])
                    abc.append((pa, Bpa))
                    pa3 = pa[:].rearrange("p (h l) -> p h l", h=4)
                    nm3 = negmT[:].unsqueeze(1).broadcast_to([128, 4, 128])
                    op(DVE, lambda e: e.tensor_tensor(out=seg[:, 4 * hb2:4 * hb2 + 4, :], in0=pa3, in1=nm3, op=ALU.add),
                       reads=[Bpa, B_negm], writes=[B_seg])
                    if SUB != 135:
                        op(ACT, lambda e: e.activation(out=eab[:, 4 * hb2:4 * hb2 + 4, :], in_=pa3, func=AF.Exp), reads=[Bpa], writes=[B_eab])
                if SUB < 14 or SUB in (133, 134, 135):
                    return
                op(DVE, lambda e: e.tensor_scalar(out=nst[:, 0:8], in0=dtw[:, 2, 0:8], scalar1=-1.0, scalar2=None, op0=ALU.mult),
                   reads=[Bdtw], writes=[B_nst])
                na3 = nst[:, 0:8].unsqueeze(2).broadcast_to([128, 8, 128])
                op(DVE, lambda e: e.tensor_tensor(out=seg[:], in0=seg[:], in1=na3, op=ALU.add), reads=[B_seg, B_nst], writes=[B_seg])
                op(ACT, lambda e: e.activation(out=seg[:], in_=seg[:], func=AF.Exp), reads=[B_seg], writes=[B_seg])
                cb3 = cbm[:].unsqueeze(1).broadcast_to([128, 8, 128])
                op(DVE, lambda e: e.tensor_tensor(out=Mt[:], in0=seg[:], in1=cb3, op=ALU.mult), reads=[B_seg, B_cbm], writes=[B_Mt])
                c3_ = xc6[:, 5, :].unsqueeze(1).broadcast_to([128, 8, 128])
                op(DVE, lambda e: e.tensor_tensor(out=CE[:], in0=eab[:], in1=c3_, op=ALU.mult), reads=[B_eab, B_xc6], writes=[B_CE])
                if SUB < 15:
                    return
                psy, Bpy = next_ps()
                fns = []
                for h in range(8):
                    fns.append(lambda e, h=h: e.matmul(psy[:, h * 64:(h + 1) * 64], Mt[:, h, :], xd[:, h * 64:(h + 1) * 64], start=True, stop=False))
                    fns.append(lambda e, h=h: e.matmul(psy[:, h * 64:(h + 1) * 64], CE[:, h, :], Sg[:, h * 64:(h + 1) * 64], start=False, stop=True))
                pe_group(fns, reads=[B_Mt, B_CE, B_xd, B_Sg], writes=[Bpy])
                op(DVE, lambda e: e.tensor_tensor(out=y3[:], in0=psy[:], in1=xdsk[:], op=ALU.add), reads=[Bpy, B_xdsk], writes=[B_y3])
                op(DVE, lambda e: e.tensor_tensor(out=y3[:], in0=y3[:], in1=zs[:], op=ALU.mult), reads=[B_y3, B_zs], writes=[B_y3])
                if SUB < 16:
                    return
                op(DVE, lambda e: e.memset(nst[:, 8:9], 0.0), writes=[B_nst])
                op(ACT, lambda e: e.activation(out=ynb[:], in_=y3[:], func=AF.Square, accum_out=nst[:, 8:9]), reads=[B_y3, B_nst], writes=[B_ynb, B_nst])
                op(DVE, lambda e: e.tensor_scalar(out=nst[:, 9:10], in0=nst[:, 8:9], scalar1=1.0 / 512, scalar2=EPS, op0=ALU.mult, op1=ALU.add),
                   reads=[B_nst], writes=[B_nst])
                op(ACT, lambda e: e.activation(out=nst[:, 10:11], in_=nst[:, 9:10], func=AF.Sqrt), reads=[B_nst], writes=[B_nst])
                op(DVE, lambda e: e.reciprocal(out=nst[:, 11:12], in_=nst[:, 10:11]), reads=[B_nst], writes=[B_nst])
                op(DVE, lambda e: e.scalar_tensor_tensor(out=ynb[:], in0=y3[:], scalar=nst[:, 11:12], in1=gsg[:], op0=ALU.mult, op1=ALU.mult),
                   reads=[B_y3, B_nst, B_gsg], writes=[B_ynb])
                if SUB < 17:
                    return
                pt, Bpt = next_pt()
                pe_group([lambda e, i=i: e.transpose(out=pt[:, i * 128:(i + 1) * 128], in_=ynb[:, i * 128:(i + 1) * 128], identity=ident_b[:])
                          for i in range(4)], reads=[B_ynb, B_ident_b], writes=[Bpt])
                op(ACT, lambda e: e.copy(out=ynT_all[:, 4 * g:4 * g + 4, m * 128:(m + 1) * 128],
                                         in_=pt[:, 0:512].rearrange("p (a b) -> p a b", a=4)), reads=[Bpt], writes=[B_ynT])


        pendA2 = stageA2(0)
        for m in range(OWN):
            nxtA2 = stageA2(m + 1) if m + 1 < OWN else None
            stageB2(m, *pendA2)
            pendA2 = nxtA2
    if DEBUG:
        dma(SP, [lambda e: e.dma_start(out=dbg["ynT"], in_=ynT_all[:])], reads=[B_ynT])
    pop_scope()
    mergedT = sb("mergedT", [128, 16, OWN * 128], BF16); B_mT = Buf()
    sg_r = Rot("sg", [128, 512], F32, 2)
    tmpm_r = Rot("tmpm", [128, 512], F32, 2)
    push_scope()
    attT = sb("attT", [128, 16, OWN * 128], BF16); B_attT = Buf()
    dma(SP, [lambda e: e.dma_start(out=attT[:], in_=attT_d)], writes=[B_attT])
    alloc_ws(2)
    for cg in range(4 if STAGE >= 10 else 0):
        wab, Bwab = load_w(w_ab, [(cg * 512, 512)])
        wga, Bwga = load_w(w_in, [(O_GA + cg * 512, 512)])
        for cc in range(4):
            ct = cg * 4 + cc
            for half in range(2):
                pg, Bpg = next_ps()
                pe_group([lambda e, kc=kc: e.matmul(pg[:], wga[:, kc, cc * 128:(cc + 1) * 128], own_rhs(kc, half),
                                                    start=(kc == 0), stop=(kc == 15)) for kc in range(16)],
                         reads=[Bwga, B_hTo], writes=[Bpg])
                sg, Bsg = sg_r.next()
                op(ACT, lambda e: e.activation(out=sg[:], in_=pg[:], func=AF.Sigmoid), reads=[Bpg], writes=[Bsg])
                pa, Bpa = next_ps()
                pe_group([lambda e, kc=kc: e.matmul(pa[:], wab[:, kc, cc * 128:(cc + 1) * 128], attT[:, kc, half * 512:(half + 1) * 512],
                                                    start=(kc == 0), stop=(kc == 15)) for kc in range(16)],
                         reads=[Bwab, B_attT], writes=[Bpa])
                op(DVE, lambda e: e.tensor_tensor(out=mergedT[:, ct, half * 512:(half + 1) * 512], in0=pa[:], in1=sg[:], op=ALU.mult),
                   reads=[Bpa, Bsg], writes=[B_mT])
    pop_scope()
    push_scope()
    alloc_ws(3)
    for cg in range(4 if STAGE >= 10 else 0):
        wsb0, Bwsb0 = load_w(w_sb, [(cg * 512, 512)], row0=0)
        wsb1, Bwsb1 = load_w(w_sb, [(cg * 512, 512)], row0=2048)
        wgs, Bwgs = load_w(w_in, [(O_GS + cg * 512, 512)])
        for cc in range(4):
            ct = cg * 4 + cc
            for half in range(2):
                pg, Bpg = next_ps()
                pe_group([lambda e, kc=kc: e.matmul(pg[:], wgs[:, kc, cc * 128:(cc + 1) * 128], own_rhs(kc, half),
                                                    start=(kc == 0), stop=(kc == 15)) for kc in range(16)],
                         reads=[Bwgs, B_hTo], writes=[Bpg])
                sg, Bsg = sg_r.next()
                op(ACT, lambda e: e.activation(out=sg[:], in_=pg[:], func=AF.Sigmoid), reads=[Bpg], writes=[Bsg])
                py, Bpy = next_ps()
                pe_group([lambda e, kc=kc: e.matmul(py[:], (wsb0 if kc < 16 else wsb1)[:, kc % 16, cc * 128:(cc + 1) * 128],
                                                    ynT_all[:, kc, half * 512:(half + 1) * 512],
                                                    start=(kc == 0), stop=(kc == 31)) for kc in range(32)],
                         reads=[Bwsb0, Bwsb1, B_ynT], writes=[Bpy])
                tm, Btm = tmpm_r.next()
                op(DVE, lambda e: e.tensor_tensor(out=tm[:], in0=py[:], in1=sg[:], op=ALU.mult), reads=[Bpy, Bsg], writes=[Btm])
                op(DVE, lambda e: e.tensor_tensor(out=mergedT[:, ct, half * 512:(half + 1) * 512], in0=tm[:],
                                                  in1=mergedT[:, ct, half * 512:(half + 1) * 512], op=ALU.add),
                   reads=[Btm, B_mT], writes=[B_mT])
    pop_scope()
    dma(SP, [lambda e: e.dma_start(out=mT_d, in_=mergedT[:])], reads=[B_mT])
    pop_scope()

    push_scope()
    x1acc = sb("x1acc", [128, OWN, D]); Bx1 = [Buf() for _ in range(OWN)]
    h2T = sb("h2T", [128, 16, OWN * 128], BF16); B_h2T = Buf()
    alloc_ws(3)
    for m in range(OWN):
        dma(SP, [lambda e: e.dma_start(out=x1acc[:, m, :], in_=x_own[m])], writes=[Bx1[m]])
    push_scope()
    mT = sb("mT", [128, 16, OWN * 128], BF16); B_mTl = Buf()
    dma(SP, [lambda e: e.dma_start(out=mT[:], in_=mT_d)], writes=[B_mTl])
    for ct in range(4 if STAGE >= 10 else 0):
        wo, Bwo = load_w(w_out, [(ct * 512, 512)])
        for blk in range(OWN):
            ps, Bp = next_ps()
            pe_group([lambda e, kc=kc: e.matmul(ps[:], mT[:, kc, blk * 128:(blk + 1) * 128], wo[:, kc, 0:512],
                                                start=(kc == 0), stop=(kc == 15)) for kc in range(16)],
                     reads=[Bwo, B_mTl], writes=[Bp])
            op(DVE, lambda e: e.tensor_tensor(out=x1acc[:, blk, ct * 512:(ct + 1) * 512], in0=x1acc[:, blk, ct * 512:(ct + 1) * 512],
                                              in1=ps[:], op=ALU.add), reads=[Bp, Bx1[blk]], writes=[Bx1[blk]])
    pop_scope()
    push_scope()
    alloc_norm(g2)
    for m in range(OWN):
        hb, Bh, _, _ = norm_rows(None, 128, src_sb=(x1acc[:, m, :], Bx1[m]))
        transpose_rows(hb, Bh, 128, lambda kc0, n, m=m: h2T[:, kc0:kc0 + n, m * 128:(m + 1) * 128], B_h2T)
    pop_scope()
    uT_r = Rot("uT", [128, 4, OWN * 128], BF16, 2)
    tmp_r = Rot("tmpf", [128, 512], F32, 2)
    for fg in range(16):
        wu, Bwu = load_w(w_up, [(fg * 512, 512)])
        uT, BuT = uT_r.next()
        for fc in range(4):
            for half in range(2):
                ps, Bp = next_ps()
                pe_group([lambda e, kc=kc: e.matmul(ps[:], wu[:, kc, fc * 128:(fc + 1) * 128], h2T[:, kc, half * 512:(half + 1) * 512],
                                                    start=(kc == 0), stop=(kc == 15)) for kc in range(16)],
                         reads=[Bwu, B_h2T], writes=[Bp])
                tmp, Btmp = tmp_r.next()
                op(ACT, lambda e: e.activation(out=tmp[:], in_=ps[:], func=AF.Relu), reads=[Bp], writes=[Btmp])
                op(DVE, lambda e: e.tensor_tensor(out=uT[:, fc, half * 512:(half + 1) * 512], in0=tmp[:], in1=tmp[:], op=ALU.mult),
                   reads=[Btmp], writes=[BuT])
        i = wsc[0] % len(WS)
        wsc[0] += 1
        wdt, Bwd = WS[i], BWS[i]
        wdv = wdt[:].rearrange("p a b -> p (a b)").rearrange("p (k c) -> p k c", k=4)
        svd = w_down[fg * 512:(fg + 1) * 512, :].rearrange("(kc p) c -> p kc c", p=128)
        dma(POOL, [lambda e: e.dma_start(out=wdv, in_=svd)], writes=[Bwd], nslots=4)
        for blk in range(OWN):
            for ct in range(4):
                ps, Bp = next_ps()
                pe_group([lambda e, fc=fc: e.matmul(ps[:], uT[:, fc, blk * 128:(blk + 1) * 128], wdv[:, fc, ct * 512:(ct + 1) * 512],
                                                    start=(fc == 0), stop=(fc == 3)) for fc in range(4)],
                         reads=[BuT, Bwd], writes=[Bp])
                op(DVE, lambda e: e.tensor_tensor(out=x1acc[:, blk, ct * 512:(ct + 1) * 512], in0=x1acc[:, blk, ct * 512:(ct + 1) * 512],
                                                  in1=ps[:], op=ALU.add), reads=[Bp, Bx1[blk]], writes=[Bx1[blk]])
    for m in range(OWN):
        dma(SP, [lambda e: e.dma_start(out=out_d[m], in_=x1acc[:, m, :])], reads=[Bx1[m]])
    pop_scope()
    for key, (slots, ctr) in dma_slots.items():
        for s in slots:
            if s.cnt > 0:
                SP.wait((s, s.cnt))
    for E in (PE, ACT, DVE):
        if E.p.cnt > 0:
            SP.wait((E.p, E.p.cnt))
    return nc


_CONST = {}


def _consts(j):
    if j in _CONST:
        return _CONST[j]
    ident = np.eye(128, dtype=np.float32)
    tri = np.triu(np.ones((128, 128), np.float32))
    flags = np.zeros((128, 4), np.float32); flags[:, j] = 1.0
    t = np.arange(128)
    am = np.zeros((128, 4, 128), np.float32)
    for r in range(4):
        if r == j:
            am[:, r, :] = np.where(t[None, :] <= t[:, None], 0.0, NEG)
        elif r > j:
            am[:, r, :] = NEG
    ohb = np.zeros((32, 5 * 256), np.float32)
    for kbrel in range(5):
        delta = j + 1 - kbrel
        for v in range(255):
            dd = 128 * delta + (v - 127)
            n = max(dd, 0)
            if n < 16:
                bkt = n
            else:
                bkt = min(31, 16 + int(np.float32(np.log(np.float32(max(n, 1)) / np.float32(16)) / np.float32(np.log(8.0)) * np.float32(16))))
            ohb[bkt, kbrel * 256 + v] += 1.0
            ohb[31, kbrel * 256 + v] -= 1.0
    _CONST[j] = dict(c_ident=ident, c_tri=tri, c_flags=flags, c_amask=am.reshape(128, 512), c_ohb=ohb)
    return _CONST[j]


def make_in_maps(inputs):
    x = np.ascontiguousarray(inputs["x"], dtype=np.float32)
    shared = dict(
        w_in=np.ascontiguousarray(inputs["w_in"][0]),
        w_ab=np.ascontiguousarray(inputs["w_att_branch"][0]),
        w_sb=np.ascontiguousarray(inputs["w_ssm_branch"][0]),
        w_out=np.ascontiguousarray(inputs["w_out"][0]),
        w_up=np.ascontiguousarray(inputs["w_up"][0]),
        w_down=np.ascontiguousarray(inputs["w_down"][0]),
        norm1_g=np.ascontiguousarray(inputs["norm1_g"].reshape(1, D)),
        norm2_g=np.ascontiguousarray(inputs["norm2_g"].reshape(1, D)),
        ssm_norm_g=np.ascontiguousarray(inputs["ssm_norm_g"].reshape(1, 4096)),
        q_norm_g=np.ascontiguousarray(inputs["q_norm_g"].reshape(128, 1)),
        k_norm_g=np.ascontiguousarray(inputs["k_norm_g"].reshape(128, 1)),
        conv_w=np.ascontiguousarray(inputs["conv_w"][0].T.reshape(48, 128, 4).transpose(1, 0, 2)),
        conv_b=np.ascontiguousarray(inputs["conv_b"][0].reshape(48, 128).T),
        dt_bias=np.ascontiguousarray(inputs["dt_bias"].reshape(1, 64)),
        a_log=np.ascontiguousarray(inputs["a_log"].reshape(1, 64)),
        d_skip=np.ascontiguousarray(inputs["d_skip"].reshape(1, 64)),
        rel_bias=np.ascontiguousarray(inputs["rel_bias"]),
    )
    maps = []
    for c in range(8):
        b, j = c // 4, c % 4
        xb = x[b].reshape(NB, 128, D)
        own = np.ascontiguousarray(xb[j::4])
        halo = np.zeros((OWN, HALO, D), np.float32)
        for m in range(OWN):
            t0 = (4 * m + j) * 128
            if t0 >= HALO:
                halo[m] = x[b, t0 - HALO:t0]
        m_ = dict(shared)
        m_.update(x_all=x[b], x_own=own, x_halo=halo.reshape(OWN * HALO, D))
        m_.update(_consts(j))
        maps.append(m_)
    return maps


_NC = None


def kernel(**inputs):
    global _NC
    if _NC is None:
        _NC = build_program()
    maps = make_in_maps(inputs)
    res = run_bass_kernel_spmd(_NC, maps, core_ids=list(range(8)))
    out = np.zeros((2, SEQ, D), np.float32)
    for c in range(8):
        b, j = c // 4, c % 4
        o = res.results[c]["out"]
        out[b].reshape(NB, 128, D)[j::4] = o
    return out
```
